# Optimizing a Trainium2 kernel written in Bass

```python
import math
import jax
import jax.numpy as jnp
from jax import lax
import numpy as np

D_MODEL = 1024
BATCH = 16
SEQ = 256
DEPTH = 4
DEC_BATCH = 4
DEC_SEQ = 4096
PAST_LEN = 256

GRID_W = 64
N_EVEN = (DEPTH + 1) // 2
N_ODD = DEPTH // 2
BLK = 128
WINDOW = 128
HD = 64
H_A = 8
KV_A = 2
G_A = H_A // KV_A
NG_B = 4
CG_B = 128
H_C = 4
HD_C = 64
H_D = 4
DK_D = 128
DV_D = 128
CHUNK = 128
D_FF = 2816
N_MOD = 9
ROPE_BASE = 10000.0
N_FREQ = HD // 4
EPS = 1e-6

W_QA = H_A * HD
W_KA = KV_A * HD
W_B = NG_B * CG_B
W_IN_EVEN = W_QA + 2 * W_KA + W_B
W_MIX_EVEN = W_QA + W_B
W_QC = H_C * 2 * HD_C
W_VC = H_C * 2 * HD_C
W_QD = H_D * DK_D
W_VD = H_D * DV_D
N_GATE_D = 4 * H_D
W_IN_ODD = 2 * W_QC + W_VC + 2 * W_QD + 2 * W_VD + N_GATE_D
W_MIX_ODD = W_VC + W_VD

kernel_name = 'hybrid_diffusion_prefix_trunk_step'


def rms_norm(x, g):
    xf = x.astype(jnp.float32)
    y = xf * lax.rsqrt(jnp.mean(xf * xf, axis=-1, keepdims=True) + EPS)
    return (y * g.astype(jnp.float32)).astype(x.dtype)


def swiglu(h, w_i, w_o):
    gate, up = jnp.split(h @ w_i, 2, axis=-1)
    return (jax.nn.silu(gate) * up) @ w_o


def ffn_half(x, mod, j, g, w_i, w_o):
    h = rms_norm(x, g) * (1 + mod[..., 3 * j + 1, :]) + mod[..., 3 * j, :]
    return x + 0.5 * mod[..., 3 * j + 2, :] * swiglu(h, w_i, w_o)


def grid_rope_tables(n_tokens):
    rows = n_tokens // GRID_W
    row = jnp.repeat(jnp.arange(rows), GRID_W).astype(jnp.float32)
    col = jnp.tile(jnp.arange(GRID_W), rows).astype(jnp.float32)
    inv = ROPE_BASE ** (-jnp.arange(N_FREQ, dtype=jnp.float32) / N_FREQ)
    ang = jnp.stack([row[:, None] * inv, col[:, None] * inv], axis=1)
    return jnp.cos(ang), jnp.sin(ang)


def rope2d(x, rope):
    cos, sin = rope
    xs = x.astype(jnp.float32).reshape(x.shape[:-1] + (2, 2, N_FREQ))
    x1, x2 = xs[..., 0, :], xs[..., 1, :]
    shp = (cos.shape[0],) + (1,) * (x1.ndim - 4) + (2, N_FREQ)
    c, s = cos.reshape(shp), sin.reshape(shp)
    out = jnp.stack([x1 * c - x2 * s, x1 * s + x2 * c], axis=-2)
    return out.reshape(x.shape).astype(x.dtype)


def to_blocks(a, size):
    b, s = a.shape[:2]
    return jnp.moveaxis(a.reshape((b, s // size, size) + a.shape[2:]), 1, 0)


def from_blocks(a):
    a = jnp.moveaxis(a, 0, 1)
    return a.reshape((a.shape[0], a.shape[1] * a.shape[2]) + a.shape[3:])


def sink_softmax(s, sink):
    sk = jnp.broadcast_to(sink.astype(jnp.float32), s.shape[:-1] + (1,))
    return jax.nn.softmax(jnp.concatenate([s, sk], axis=-1), axis=-1)[..., :-1]


def context_attn_sink(q, k, v, sink):
    b, s = q.shape[:2]
    qg = q.reshape(b, s, KV_A, G_A, HD)
    sink_b = sink.reshape(KV_A, G_A, 1, 1)

    def block(qb):
        sc = jnp.einsum('bqkgd,bckd->bkgqc', qb, k).astype(jnp.float32) * HD ** -0.5
        p = sink_softmax(sc, sink_b)
        return jnp.einsum('bkgqc,bckd->bqkgd', p.astype(v.dtype), v)

    out = from_blocks(lax.map(block, to_blocks(qg, BLK)))
    return out.reshape(b, s, H_A, HD)


def window_attn_sink(q, k, v, kc, vc, sink):
    b, s = q.shape[:2]
    nb = s // BLK
    qb = q.reshape(b, nb, BLK, KV_A, G_A, HD)
    pad = ((0, 0), (BLK, BLK), (0, 0), (0, 0))
    kp = jnp.pad(k, pad).reshape(b, nb + 2, BLK, KV_A, HD)
    vp = jnp.pad(v, pad).reshape(b, nb + 2, BLK, KV_A, HD)
    kw = jnp.concatenate([kp[:, :-2], kp[:, 1:-1], kp[:, 2:]], axis=2)
    vw = jnp.concatenate([vp[:, :-2], vp[:, 1:-1], vp[:, 2:]], axis=2)
    scale = HD ** -0.5
    s_loc = jnp.einsum('bnqkgd,bnjkd->bnkgqj', qb, kw).astype(jnp.float32) * scale
    s_ctx = jnp.einsum('bnqkgd,bckd->bnkgqc', qb, kc).astype(jnp.float32) * scale
    blk_i = jnp.arange(nb)[:, None, None]
    qpos = blk_i * BLK + jnp.arange(BLK)[None, :, None]
    kpos = (blk_i - 1) * BLK + jnp.arange(3 * BLK)[None, None, :]
    valid = (jnp.abs(qpos - kpos) <= WINDOW) & (kpos >= 0) & (kpos < s)
    s_loc = jnp.where(valid[None, :, None, None], s_loc, -jnp.inf)
    p = sink_softmax(jnp.concatenate([s_loc, s_ctx], axis=-1), sink.reshape(KV_A, G_A, 1, 1)).astype(v.dtype)
    out = (jnp.einsum('bnkgqj,bnjkd->bnqkgd', p[..., :3 * BLK], vw)
           + jnp.einsum('bnkgqc,bckd->bnqkgd', p[..., 3 * BLK:], vc))
    return out.reshape(b, s, H_A, HD)


def fourier_mix(u):
    f = jnp.fft.fft2(u.astype(jnp.float32), axes=(1, 3), norm='ortho')
    return jnp.real(f).astype(u.dtype)


def mixer_even(h, w_in, w_out, qn, kn, sink, rope, ctx_k, ctx_v):
    b, s, _ = h.shape
    p = h @ w_in
    q = rms_norm(p[..., :W_QA].reshape(b, s, H_A, HD), qn)
    k = rms_norm(p[..., W_QA:W_QA + W_KA].reshape(b, s, KV_A, HD), kn)
    v = p[..., W_QA + W_KA:W_QA + 2 * W_KA].reshape(b, s, KV_A, HD)
    u = p[..., W_QA + 2 * W_KA:].reshape(b, s, NG_B, CG_B)
    if ctx_k is None:
        a = context_attn_sink(q, k, v, sink)
        new = (k, v)
    else:
        a = window_attn_sink(rope2d(q, rope), rope2d(k, rope), v, ctx_k, ctx_v, sink)
        new = None
    f = fourier_mix(u)
    out = jnp.concatenate([a.reshape(b, s, W_QA), f.reshape(b, s, W_B)], axis=-1) @ w_out
    return out, new


def diff_attention(q, k, v, lam_vecs, lam_init, subln_g):
    lv = lam_vecs.astype(jnp.float32)
    lam = jnp.exp(jnp.sum(lv[0] * lv[1])) - jnp.exp(jnp.sum(lv[2] * lv[3])) + lam_init

    def block(qb):
        sc = jnp.einsum('bqhid,bkhid->bihqk', qb, k).astype(jnp.float32) * HD_C ** -0.5
        pr = jax.nn.softmax(sc, axis=-1)
        a = pr[:, 0] - lam * pr[:, 1]
        return jnp.einsum('bhqk,bkhe->bqhe', a.astype(v.dtype), v)

    o = from_blocks(lax.map(block, to_blocks(q, BLK)))
    return rms_norm(o, subln_g) * (1.0 - lam_init)


def mlstm_chunked(q, k, v, ig, fg, c0, n0, m0):
    f32 = jnp.float32
    logf = jax.nn.log_sigmoid(fg.astype(f32))
    tril = jnp.tril(jnp.ones((CHUNK, CHUNK), dtype=bool))

    def step(carry, xs):
        c, n, m = carry
        qc, kc, vc, ic, fc = xs
        bcum = jnp.cumsum(fc, axis=1)
        dlog = bcum[:, :, None, :] - bcum[:, None, :, :] + ic[:, None, :, :]
        dlog = jnp.where(tril[None, :, :, None], dlog, -jnp.inf)
        g = bcum + m[:, None, :]
        mt = jnp.maximum(g, jnp.max(dlog, axis=2))
        w = jnp.exp(dlog - mt[:, :, None, :])
        winter = jnp.exp(g - mt)
        sc = jnp.einsum('bthd,bshd->btsh', qc, kc) * w
        num = (jnp.einsum('btsh,bshv->bthv', sc, vc)
               + winter[..., None] * jnp.einsum('bthd,bhdv->bthv', qc, c))
        den = jnp.sum(sc, axis=2) + winter * jnp.einsum('bthd,bhd->bth', qc, n)
        hc = num / jnp.maximum(jnp.abs(den), jnp.exp(-mt))[..., None]
        b_end = bcum[:, -1]
        wlog = b_end[:, None, :] - bcum + ic
        m_new = jnp.maximum(b_end + m, jnp.max(wlog, axis=1))
        ws = jnp.exp(wlog - m_new[:, None, :])
        decay = jnp.exp(b_end + m - m_new)
        c_new = decay[..., None, None] * c + jnp.einsum('bsh,bshd,bshv->bhdv', ws, kc, vc)
        n_new = decay[..., None] * n + jnp.einsum('bsh,bshd->bhd', ws, kc)
        return (c_new, n_new, m_new), hc

    xs = tuple(to_blocks(a.astype(f32), CHUNK) for a in (q, k, v, ig, logf))
    carry, hs = lax.scan(step, (c0.astype(f32), n0.astype(f32), m0.astype(f32)), xs)
    return from_blocks(hs), carry


def mixer_odd(h, w_in, b_gate, w_out, qn, kn, lam_vecs, subln_g, outnorm_g, lam_init, rope, ctx):
    b, s, _ = h.shape
    f32 = jnp.float32
    p = h @ w_in
    sizes = (W_QC, W_QC, W_VC, W_QD, W_QD, W_VD, W_VD)
    bounds = []
    acc = 0
    for sz in sizes:
        acc += sz
        bounds.append(acc)
    qc, kc, vc, qd, kd, vd, od, gates = jnp.split(p, bounds, axis=-1)
    qc = rms_norm(qc.reshape(b, s, H_C, 2, HD_C), qn)
    kc = rms_norm(kc.reshape(b, s, H_C, 2, HD_C), kn)
    vc = vc.reshape(b, s, H_C, 2 * HD_C)
    qd = qd.reshape(b, s, H_D, DK_D)
    kd = kd.reshape(b, s, H_D, DK_D) * DK_D ** -0.5
    vd = vd.reshape(b, s, H_D, DV_D)
    gates = (gates + b_gate).astype(f32).reshape(b, s, 2, 2, H_D)
    if ctx is None:
        keys, vals = kc, vc
        c0 = jnp.zeros((b, 2, H_D, DK_D, DV_D), f32)
        n0 = jnp.zeros((b, 2, H_D, DK_D), f32)
        m0 = jnp.zeros((b, 2, H_D), f32)
    else:
        ck, cv, c0, n0, m0 = ctx
        qc = rope2d(qc, rope)
        keys = jnp.concatenate([rope2d(kc, rope), ck], axis=1)
        vals = jnp.concatenate([vc, cv], axis=1)
    a = diff_attention(qc, keys, vals, lam_vecs, lam_init, subln_g)
    h_f, (cf, nf, mf) = mlstm_chunked(qd, kd, vd, gates[:, :, 0, 0], gates[:, :, 0, 1],
                                      c0[:, 0], n0[:, 0], m0[:, 0])
    h_b, (cb, nbk, mb) = mlstm_chunked(jnp.flip(qd, 1), jnp.flip(kd, 1), jnp.flip(vd, 1),
                                       jnp.flip(gates[:, :, 1, 0], 1), jnp.flip(gates[:, :, 1, 1], 1),
                                       c0[:, 1], n0[:, 1], m0[:, 1])
    hm = rms_norm(h_f + jnp.flip(h_b, 1), outnorm_g).astype(h.dtype)
    hm = hm * jax.nn.sigmoid(od.reshape(b, s, H_D, DV_D))
    out = jnp.concatenate([a.reshape(b, s, W_VC), hm.reshape(b, s, W_VD)], axis=-1) @ w_out
    if ctx is None:
        new = (kc, vc, jnp.stack([cf, cb], axis=1), jnp.stack([nf, nbk], axis=1), jnp.stack([mf, mb], axis=1))
    else:
        new = None
    return out, new


def setup_inputs(seed: int = 0) -> dict:
    key = jax.random.key(seed)
    ks = jax.random.split(key, 32)

    def nrm(k, shape, s):
        return jax.random.normal(k, shape, jnp.float32) * s

    gate_offset = jnp.tile(jnp.repeat(jnp.array([0.0, 3.0], jnp.float32), H_D), 2)
    return {
        'x_prompt': nrm(ks[0], (BATCH, SEQ, D_MODEL), 1.0),
        'x_sample': nrm(ks[1], (DEC_BATCH, DEC_SEQ, D_MODEL), 1.0),
        'c': nrm(ks[2], (DEC_BATCH, D_MODEL), 1.0),
        'cache_k_a': nrm(ks[3], (DEC_BATCH, N_EVEN, PAST_LEN, KV_A, HD), 1.0),
        'cache_v_a': nrm(ks[4], (DEC_BATCH, N_EVEN, PAST_LEN, KV_A, HD), 1.0),
        'cache_k_c': nrm(ks[5], (DEC_BATCH, N_ODD, PAST_LEN, H_C, 2, HD_C), 1.0),
        'cache_v_c': nrm(ks[6], (DEC_BATCH, N_ODD, PAST_LEN, H_C, 2 * HD_C), 1.0),
        'state_C_d': nrm(ks[7], (DEC_BATCH, N_ODD, 2, H_D, DK_D, DV_D), 0.05),
        'state_n_d': nrm(ks[8], (DEC_BATCH, N_ODD, 2, H_D, DK_D), 0.5),
        'state_m_d': nrm(ks[9], (DEC_BATCH, N_ODD, 2, H_D), 1.0),
        'c_ctx': nrm(ks[10], (D_MODEL,), 1.0),
        'w_ada': nrm(ks[11], (DEPTH, D_MODEL, N_MOD * D_MODEL), 0.5 * D_MODEL ** -0.5),
        'b_ada': nrm(ks[12], (DEPTH, N_MOD * D_MODEL), 0.02),
        'g_norm': 1.0 + nrm(ks[13], (DEPTH, 3, D_MODEL), 0.02),
        'w_ffn_in': nrm(ks[14], (DEPTH, 2, D_MODEL, 2 * D_FF), D_MODEL ** -0.5),
        'w_ffn_out': nrm(ks[15], (DEPTH, 2, D_FF, D_MODEL), D_FF ** -0.5),
        'w_in_even': nrm(ks[16], (N_EVEN, D_MODEL, W_IN_EVEN), D_MODEL ** -0.5),
        'w_out_even': nrm(ks[17], (N_EVEN, W_MIX_EVEN, D_MODEL), W_MIX_EVEN ** -0.5),
        'qn_a': 1.0 + nrm(ks[18], (N_EVEN, HD), 0.02),
        'kn_a': 1.0 + nrm(ks[19], (N_EVEN, HD), 0.02),
        'sink_a': nrm(ks[20], (N_EVEN, H_A), 0.5),
        'w_in_odd': nrm(ks[21], (N_ODD, D_MODEL, W_IN_ODD), D_MODEL ** -0.5),
        'b_gate_odd': nrm(ks[22], (N_ODD, N_GATE_D), 0.1) + gate_offset,
        'w_out_odd': nrm(ks[23], (N_ODD, W_MIX_ODD, D_MODEL), W_MIX_ODD ** -0.5),
        'qn_c': 1.0 + nrm(ks[24], (N_ODD, HD_C), 0.02),
        'kn_c': 1.0 + nrm(ks[25], (N_ODD, HD_C), 0.02),
        'lam_c': nrm(ks[26], (N_ODD, 4, HD_C), 0.1),
        'subln_c': 1.0 + nrm(ks[27], (N_ODD, 2 * HD_C), 0.02),
        'outnorm_d': 1.0 + nrm(ks[28], (N_ODD, DV_D), 0.02),
    }


def reference(x_prompt, x_sample, c, cache_k_a, cache_v_a, cache_k_c, cache_v_c, state_C_d, state_n_d,
              state_m_d, c_ctx, w_ada, b_ada, g_norm, w_ffn_in, w_ffn_out, w_in_even, w_out_even, qn_a, kn_a,
              sink_a, w_in_odd, b_gate_odd, w_out_odd, qn_c, kn_c, lam_c, subln_c, outnorm_d):
    rope = grid_rope_tables(x_sample.shape[1])
    yp, ys = x_prompt, x_sample
    ka_l, va_l, kc_l, vc_l, cd_l, nd_l, md_l = [], [], [], [], [], [], []
    for l in range(DEPTH):
        i = l // 2
        mc = (jax.nn.silu(c_ctx) @ w_ada[l] + b_ada[l]).reshape(1, 1, N_MOD, D_MODEL)
        ms = (jax.nn.silu(c) @ w_ada[l] + b_ada[l]).reshape(c.shape[0], 1, N_MOD, D_MODEL)
        yp = ffn_half(yp, mc, 0, g_norm[l, 0], w_ffn_in[l, 0], w_ffn_out[l, 0])
        ys = ffn_half(ys, ms, 0, g_norm[l, 0], w_ffn_in[l, 0], w_ffn_out[l, 0])
        hp = rms_norm(yp, g_norm[l, 1]) * (1 + mc[..., 4, :]) + mc[..., 3, :]
        hs = rms_norm(ys, g_norm[l, 1]) * (1 + ms[..., 4, :]) + ms[..., 3, :]
        if l % 2 == 0:
            op, (ka, va) = mixer_even(hp, w_in_even[i], w_out_even[i], qn_a[i], kn_a[i], sink_a[i],
                                      None, None, None)
            os_, _ = mixer_even(hs, w_in_even[i], w_out_even[i], qn_a[i], kn_a[i], sink_a[i],
                                rope, cache_k_a[:, i], cache_v_a[:, i])
            ka_l.append(ka)
            va_l.append(va)
        else:
            lam_init = 0.8 - 0.6 * math.exp(-0.3 * l)
            op, (kc, vc, cd, nd, md) = mixer_odd(hp, w_in_odd[i], b_gate_odd[i], w_out_odd[i], qn_c[i], kn_c[i],
                                                 lam_c[i], subln_c[i], outnorm_d[i], lam_init, None, None)
            os_, _ = mixer_odd(hs, w_in_odd[i], b_gate_odd[i], w_out_odd[i], qn_c[i], kn_c[i],
                               lam_c[i], subln_c[i], outnorm_d[i], lam_init, rope,
                               (cache_k_c[:, i], cache_v_c[:, i], state_C_d[:, i], state_n_d[:, i],
                                state_m_d[:, i]))
            kc_l.append(kc)
            vc_l.append(vc)
            cd_l.append(cd)
            nd_l.append(nd)
            md_l.append(md)
        yp = yp + mc[..., 5, :] * op
        ys = ys + ms[..., 5, :] * os_
        yp = ffn_half(yp, mc, 2, g_norm[l, 2], w_ffn_in[l, 1], w_ffn_out[l, 1])
        ys = ffn_half(ys, ms, 2, g_norm[l, 2], w_ffn_in[l, 1], w_ffn_out[l, 1])
    new_k_a = jnp.stack(ka_l, axis=1)
    new_v_a = jnp.stack(va_l, axis=1)
    new_k_c = jnp.stack(kc_l, axis=1)
    new_v_c = jnp.stack(vc_l, axis=1)
    new_C_d = jnp.stack(cd_l, axis=1)
    new_n_d = jnp.stack(nd_l, axis=1)
    new_m_d = jnp.stack(md_l, axis=1)
    return (yp, ys, new_k_a, new_v_a, new_k_c, new_v_c, new_C_d, new_n_d, new_m_d)
```

```python
import math
import re
from contextlib import ExitStack
import numpy as np
import ml_dtypes
import concourse.bass as bass
import concourse.mybir as mybir
from concourse.bass_utils import run_bass_kernel_spmd

F32 = mybir.dt.float32
BF16 = mybir.dt.bfloat16
AF = mybir.ActivationFunctionType
ALU = mybir.AluOpType
AX = mybir.AxisListType

D = 1024
NCH = 8
DEPTH = 4
DFF = 2816
NFC = 22
EPS = 1e-6
HD = 64
PAST = 256
NEG = -30000.0


_PS_RE = re.compile(r"^(pm|msb|pg\d|pu\d|py\d|pq\d|pv|pvv|pg4|pms|prot|pf\d|pS\d|pO\d|pB\d|pD\d|pb\d|pmisc|PA\d_\d|PC\d)$")


class Res:
    __slots__ = ("name", "lw", "rd", "sem", "cnt", "ldma", "ps", "lock")

    def __init__(self, name, lock=False):
        self.name = name
        self.ps = bool(_PS_RE.match(name))
        self.lock = lock
        self.lw = None
        self.rd = []
        self.sem = None
        self.cnt = 0
        self.ldma = None


class Ev:
    __slots__ = ("key", "val", "snap", "eng")

    def __init__(self, key, val, snap, eng):
        self.key, self.val, self.snap, self.eng = key, val, snap, eng


class K:
    EPOCH = 20000

    def __init__(self, nc, stack):
        self.nc = nc
        self.stack = stack
        self.eng = {"pe": nc.tensor, "act": nc.scalar, "dve": nc.vector, "pool": nc.gpsimd, "sp": nc.sync}
        self.seq = {e: 0 for e in self.eng}
        self.know = {e: {} for e in self.eng}
        self.sems = {}
        self.dma_free = []
        self.n_dma_sems = 0
        self.dma_cnt = {}
        self.nwait = 0
        self.ninst = 0

    def sem(self, key):
        s = self.sems.get(key)
        if s is None:
            s = self.stack.enter_context(self.nc.semaphore("s_%s" % (str(key).replace(" ", ""))))
            self.sems[key] = s
        return s

    def _need(self, e, ev):
        if ev is None:
            return
        kn = self.know[e]
        if kn.get(ev.key, 0) >= ev.val:
            return
        self.eng[e].wait_ge(self.sem(ev.key), ev.val)
        self.nwait += 1
        kn[ev.key] = ev.val
        for k2, v2 in ev.snap.items():
            if kn.get(k2, 0) < v2:
                kn[k2] = v2

    def _deps(self, e, reads, writes):
        for r in reads:
            self._need(e, r.lw)
            if r.ps:
                for ev in r.rd:
                    if ev.eng != e:
                        self._need(e, ev)
        for w in writes:
            lw = w.lw
            if lw is not None and not (lw.eng == e and (e == "pe" or w.lock)):
                self._need(e, lw)
            for ev in w.rd:
                if ev.eng != e or e in ("sp", "pool"):
                    self._need(e, ev)

    def _commit(self, ev, reads, writes):
        for r in reads:
            r.rd.append(ev)
        for w in writes:
            w.lw = ev
            w.rd = []

    def op(self, e, fn, reads=(), writes=()):
        self._deps(e, reads, writes)
        n = self.seq[e]
        key = (e, n // self.EPOCH)
        val = n % self.EPOCH + 1
        ins = fn(self.eng[e])
        ins.then_inc(self.sem(key), 1)
        self.seq[e] = n + 1
        self.ninst += 1
        ev = Ev(key, val, dict(self.know[e]), e)
        self._commit(ev, reads, writes)
        return ev

    def dma(self, q, pairs, own, reads=(), writes=()):
        if own.sem is None:
            if self.dma_free:
                own.sem = self.dma_free.pop()
            else:
                own.sem = ("dma", self.n_dma_sems)
                self.n_dma_sems += 1
        self._need(q, own.ldma)
        self._deps(q, reads, writes)
        s = self.sem(own.sem)
        c = self.dma_cnt.get(own.sem, 0)
        for (o, i) in pairs:
            self.eng[q].dma_start(out=o, in_=i).then_inc(s, 16)
            c += 1
            self.ninst += 1
        self.dma_cnt[own.sem] = c
        ev = Ev(own.sem, 16 * c, dict(self.know[q]), "dma")
        own.ldma = ev
        self._commit(ev, reads, writes)
        return ev

    def release(self, ress):
        for r in ress:
            if r.sem is not None:
                self.dma_free.append(r.sem)
                r.sem = None

    def barrier(self):
        evs = []
        for e in self.eng:
            n = self.seq[e]
            if n > 0:
                evs.append(Ev((e, (n - 1) // self.EPOCH), (n - 1) % self.EPOCH + 1, {}, e))
        for key, c in self.dma_cnt.items():
            if c > 0:
                evs.append(Ev(key, 16 * c, {}, "dma"))
        for e in self.eng:
            for ev in evs:
                if ev.eng == e and e != "dma":
                    pass
                self._need(e, ev)


class Cfg:
    def __init__(self, ns=4096, npr=256, depth=DEPTH, do_mix=True):
        self.NS = ns
        self.NP = npr
        self.T = ns + 2 * npr
        self.depth = depth
        self.do_mix = do_mix
        self.TT = 512
        assert self.T % self.TT == 0 and ns % self.TT == 0
        self.NT = self.T // self.TT
        self.seqs = [(0, ns, 1, True), (ns, npr, 0, False), (ns + npr, npr, 0, False)]


def build(cfg):
    nc = bass.Bass("TRN2", target_bir_lowering=False)
    T, TT, NT = cfg.T, cfg.TT, cfg.NT
    dt = nc.dram_tensor

    xT_in = dt("xT", [NCH, 128, T], F32, kind="ExternalInput").ap()
    cT_in = dt("cT", [128, NCH, 2], F32, kind="ExternalInput").ap()
    w_ada = dt("w_ada", [DEPTH, D, 9 * D], F32, kind="ExternalInput").ap()
    b_adaT = dt("b_adaT", [128, DEPTH, 72], F32, kind="ExternalInput").ap()
    g_normT = dt("g_normT", [128, DEPTH, 3, NCH], F32, kind="ExternalInput").ap()
    w_ffn_in = dt("w_ffn_in", [DEPTH, 2, D, 2 * DFF], F32, kind="ExternalInput").ap()
    w_ffn_out = dt("w_ffn_out", [DEPTH, 2, DFF, D], F32, kind="ExternalInput").ap()
    yT = dt("yT", [NCH, 128, T], F32, kind="ExternalOutput").ap()
    NS, NP = cfg.NS, cfg.NP
    NTK = T // 128
    NPT = 2 * NP
    cst_f = dt("cst_f", [128, 5, 128], F32, kind="ExternalInput").ap()
    cst_b = dt("cst_b", [128, 13, 128], BF16, kind="ExternalInput").ap()
    cst4 = dt("cst4", [4, 4, 128], F32, kind="ExternalInput").ap()
    cst4b = dt("cst4b", [4, 5, 128], BF16, kind="ExternalInput").ap()
    ropeT = dt("ropeT", [128, 2, NS], F32, kind="ExternalInput").ap()
    tabS = dt("tabS", [NS // 256, 128, NS // 128, 2, 256], BF16, kind="ExternalInput").ap()
    tabP = dt("tabP", [NP // 256, 128, NP // 128, 2, 256], BF16, kind="ExternalInput").ap()
    w_in_e = dt("w_in_e", [2, D, 1280], F32, kind="ExternalInput").ap()
    w_out_e = dt("w_out_e", [2, D, D], F32, kind="ExternalInput").ap()
    par_e = dt("par_e", [128, 2, 2], F32, kind="ExternalInput").ap()
    sink_e = dt("sink_e", [2, 8], F32, kind="ExternalInput").ap()
    kctx_e = dt("kctx_e", [2, 128, PAST], F32, kind="ExternalInput").ap()
    vctx_e = dt("vctx_e", [2, 128, 2, 2, 128], F32, kind="ExternalInput").ap()
    w_in_o = dt("w_in_o", [2, D, 3600], F32, kind="ExternalInput").ap()
    w_out_o = dt("w_out_o", [2, D, D], F32, kind="ExternalInput").ap()
    par_o = dt("par_o", [128, 2, 4], F32, kind="ExternalInput").ap()
    bg_o = dt("bg_o", [4, 2, 4], F32, kind="ExternalInput").ap()
    lam_o = dt("lam_o", [2, 256], F32, kind="ExternalInput").ap()
    kctx_o = dt("kctx_o", [2, 128, 4, PAST], F32, kind="ExternalInput").ap()
    vctx_o = dt("vctx_o", [2, 128, 2, 512], F32, kind="ExternalInput").ap()
    c0_o = dt("c0_o", [128, 2, 2, 4, 128], F32, kind="ExternalInput").ap()
    n0_o = dt("n0_o", [2, 2, 4, 128, 1], F32, kind="ExternalInput").ap()
    m0_o = dt("m0_o", [4, 2, 2], F32, kind="ExternalInput").ap()
    knew_a = dt("knew_a", [2, 128, NPT], F32, kind="ExternalOutput").ap()
    vnew_a = dt("vnew_a", [2, 128, NPT // 128, 128], F32, kind="ExternalOutput").ap()
    knew_c = dt("knew_c", [2, 128, 4, NPT], F32, kind="ExternalOutput").ap()
    vnew_c = dt("vnew_c", [2, 128, NPT // 128, 512], F32, kind="ExternalOutput").ap()
    Cnew = dt("Cnew", [2, 2, 2, 4, 128, 128], F32, kind="ExternalOutput").ap()
    nnew = dt("nnew", [2, 2, 2, 4, 128, 1], F32, kind="ExternalOutput").ap()
    mnew = dt("mnew", [4, 2, 2, 2], F32, kind="ExternalOutput").ap()
    QT_d = dt("QT_d", [128, 4, T], BF16, kind="Internal").ap()
    KT_d = dt("KT_d", [128, 4, T], BF16, kind="Internal").ap()
    VA_d = dt("VA_d", [128, NTK, 2, 128], BF16, kind="Internal").ap()
    PQ_d = dt("PQ_d", [128, NTK, 4, 256], BF16, kind="Internal").ap()
    MIX_d = dt("MIX_d", [128, 8, T], BF16, kind="Internal").ap()
    VCt_d = dt("VCt_d", [128, NTK, 512], BF16, kind="Internal").ap()
    QDT_d = dt("QDT_d", [4, 128, T], BF16, kind="Internal").ap()
    KDT_d = dt("KDT_d", [4, 128, T], BF16, kind="Internal").ap()
    KDt_d = dt("KDt_d", [4, 128, NTK, 128], BF16, kind="Internal").ap()
    VD1t_d = dt("VD1t_d", [4, 128, NTK, 160], BF16, kind="Internal").ap()
    SODT_d = dt("SODT_d", [4, 128, T], BF16, kind="Internal").ap()
    G_d = dt("G_d", [4, 4, T], F32, kind="Internal").ap()

    with ExitStack() as gs:
        k = K(nc, gs)

        uid = [0]

        def sb(name, shape, dtype, st=gs):
            uid[0] += 1
            return st.enter_context(nc.sbuf_tensor("%s_%d" % (name, uid[0]), shape, dtype))

        def ps(name, shape, dtype, st=gs):
            uid[0] += 1
            return st.enter_context(nc.psum_tensor("%s_%d" % (name, uid[0]), shape, dtype))

        ones_bf = sb("ones_bf", [128, 128], BF16)
        r_ones = Res("ones_bf")
        k.op("pool", lambda e: e.memset(ones_bf[:], 1.0 / D), writes=[r_ones])
        eps_t = sb("eps_t", [128, 1], F32)
        k.op("pool", lambda e: e.memset(eps_t[:], EPS), writes=[r_ones])
        one_t = sb("one_t", [128, 1], F32)
        k.op("pool", lambda e: e.memset(one_t[:], 1.0), writes=[r_ones])
        cTs = sb("cTs", [128, NCH, 2], F32)
        scT = sb("scT", [128, NCH, 2], BF16)
        bada = sb("bada", [128, DEPTH, 72], F32)
        gnrm = sb("gnrm", [128, DEPTH, 3, NCH], F32)
        r_par = Res("par")
        k.dma("sp", [(cTs[:], cT_in), (bada[:], b_adaT), (gnrm[:], g_normT)], r_par, writes=[r_par])
        r_scT = Res("scT")
        k.op("act", lambda e: e.activation(out=scT[:], in_=cTs[:], func=AF.Silu), reads=[r_par], writes=[r_scT])
        MOD = sb("MOD", [128, DEPTH, 9, NCH, 2], F32)
        r_mod = Res("MOD")
        AM = sb("AM", [128, DEPTH, 3, NCH, 2], F32)
        GM = sb("GM", [128, DEPTH, 3, NCH, 2], F32)
        r_am = Res("AM")

        with ExitStack() as st:
            wa = [sb("wa%d" % i, [128, NCH, D], BF16, st) for i in range(2)]
            r_wa = [Res("wa%d" % i) for i in range(2)]
            pm = ps("pm", [128, 8, 2], F32, st)
            r_pm = Res("pm")
            it = 0
            for l in range(cfg.depth):
                for j in range(9):
                    b = it % 2
                    it += 1
                    k.dma("pool", [(wa[b][:, kc, :], w_ada[l, kc * 128:(kc + 1) * 128, j * D:(j + 1) * D])
                                   for kc in range(NCH)], r_wa[b], writes=[r_wa[b]])
                    for cc in range(NCH):
                        for kc in range(NCH):
                            k.op("pe", lambda e, cc=cc, kc=kc, b=b: e.matmul(
                                pm[:, cc, :], lhsT=wa[b][:, kc, cc * 128:(cc + 1) * 128], rhs=scT[:, kc, :],
                                start=(kc == 0), stop=(kc == NCH - 1)),
                                reads=[r_wa[b], r_scT], writes=[r_pm])
                    k.op("dve", lambda e, l=l, j=j: e.tensor_tensor(
                        out=MOD[:, l, j, :, :], in0=pm[:],
                        in1=bada[:, l, j * 8:(j + 1) * 8].unsqueeze(2).to_broadcast([128, 8, 2]),
                        op=ALU.add), reads=[r_pm, r_par], writes=[r_mod])
            for l in range(cfg.depth):
                for w in range(3):
                    k.op("dve", lambda e, l=l, w=w: e.scalar_tensor_tensor(
                        out=AM[:, l, w, :, :], in0=MOD[:, l, 3 * w + 1, :, :], scalar=1.0,
                        in1=gnrm[:, l, w, :].unsqueeze(2).to_broadcast([128, 8, 2]),
                        op0=ALU.add, op1=ALU.mult), reads=[r_mod, r_par], writes=[r_am])
                    k.op("dve", lambda e, l=l, w=w: e.tensor_scalar(
                        out=GM[:, l, w, :, :], in0=MOD[:, l, 3 * w + 2, :, :],
                        scalar1=(1.0 if w == 1 else 0.5), scalar2=None, op0=ALU.mult),
                        reads=[r_mod], writes=[r_am])
            k.barrier()
            k.release(r_wa)

        def norm_mod(st_x, r_x, xt, ht, r_h, l, w, cond, tmp):
            sq, r_sq, msb, r_ms, rstd, r_rstd, u, r_u = tmp
            for c in range(NCH):
                b = c % 2
                k.op("act", lambda e, c=c, b=b: e.activation(out=sq[b][:], in_=xt[:, c, :], func=AF.Square),
                     reads=[r_x], writes=[r_sq[b]])
                k.op("pe", lambda e, c=c, b=b: e.matmul(msb[:], lhsT=ones_bf[:], rhs=sq[b][:],
                                                       start=(c == 0), stop=(c == NCH - 1)),
                     reads=[r_sq[b], r_ones], writes=[r_ms])
            k.op("act", lambda e: e.activation(out=rstd[:], in_=msb[:], func=AF.Sqrt, bias=eps_t[:], scale=1.0),
                 reads=[r_ms, r_ones], writes=[r_rstd])
            k.op("dve", lambda e: e.reciprocal(out=rstd[:], in_=rstd[:]), reads=[r_rstd], writes=[r_rstd])
            for c in range(NCH):
                b = c % 2
                k.op("dve", lambda e, c=c, b=b: e.scalar_tensor_tensor(
                    out=u[b][:], in0=xt[:, c, :], scalar=AM[:, l, w, c, cond:cond + 1], in1=rstd[:],
                    op0=ALU.mult, op1=ALU.mult), reads=[r_x, r_rstd, r_am], writes=[r_u[b]])
                k.op("act", lambda e, c=c, b=b: e.activation(
                    out=ht[:, c, :], in_=u[b][:], func=AF.Identity,
                    bias=MOD[:, l, 3 * w, c, cond:cond + 1], scale=1.0),
                    reads=[r_u[b], r_mod], writes=[r_h])

        def alloc_norm_tmp(st):
            sq = [sb("sq%d" % i, [128, TT], BF16, st) for i in range(2)]
            r_sq = [Res("sq%d" % i) for i in range(2)]
            msb = ps("msb", [128, TT], F32, st)
            rstd = sb("rstd", [128, TT], F32, st)
            u = [sb("u%d" % i, [128, TT], F32, st) for i in range(2)]
            r_u = [Res("u%d" % i) for i in range(2)]
            return (sq, r_sq, msb, Res("msb"), rstd, Res("rstd"), u, r_u)

        def tile_cond(t):
            return 1 if t * TT < cfg.NS else 0

        def ffn_phase(l, j, src):
            w = 0 if j == 0 else 2
            with ExitStack() as st:
                wi = sb("wi", [128, NCH, 2 * DFF], BF16, st)
                wo = sb("wo", [128, NFC, D], BF16, st)
                r_wi = [Res("wi%d" % i) for i in range(NCH)]
                r_wo = [Res("wo%d" % i) for i in range(2)]
                for kc in range(NCH):
                    k.dma("pool", [(wi[:, kc, :], w_ffn_in[l, j, kc * 128:(kc + 1) * 128, :])], r_wi[kc],
                          writes=[r_wi[kc]])
                for hf in range(2):
                    k.dma("pool", [(wo[:, fc, :], w_ffn_out[l, j, fc * 128:(fc + 1) * 128, :])
                                   for fc in range(hf * 11, hf * 11 + 11)], r_wo[hf], writes=[r_wo[hf]])
                xt = sb("xt", [128, NCH, TT], F32, st)
                r_x = Res("xt")
                ht = sb("ht", [128, NCH, TT], BF16, st)
                r_h = Res("ht")
                tmp = alloc_norm_tmp(st)
                sg = [sb("sg%d" % i, [128, TT], F32, st) for i in range(2)]
                r_sg = [Res("sg%d" % i) for i in range(2)]
                act = sb("act", [128, NFC, TT], BF16, st)
                r_act = [Res("act%d" % i) for i in range(NFC)]
                pg = [ps("pg%d" % i, [128, TT], F32, st) for i in range(2)]
                pu = [ps("pu%d" % i, [128, TT], F32, st) for i in range(2)]
                py = [ps("py%d" % i, [128, TT], F32, st) for i in range(2)]
                r_pg = [Res("pg%d" % i) for i in range(2)]
                r_pu = [Res("pu%d" % i) for i in range(2)]
                r_py = [Res("py%d" % i) for i in range(2)]
                r_yT = Res("yT")
                for t in range(NT):
                    cond = tile_cond(t)
                    tsl = slice(t * TT, (t + 1) * TT)
                    k.dma("sp", [(xt[:, c, :], src[c, :, tsl]) for c in range(NCH)], r_x,
                          reads=[r_yT], writes=[r_x])
                    norm_mod(st, r_x, xt, ht, r_h, l, w, cond, tmp)
                    for fc in range(NFC):
                        b = fc % 2
                        for kc in range(NCH):
                            k.op("pe", lambda e, fc=fc, kc=kc, b=b: e.matmul(
                                pg[b][:], lhsT=wi[:, kc, fc * 128:(fc + 1) * 128], rhs=ht[:, kc, :],
                                start=(kc == 0), stop=(kc == NCH - 1)),
                                reads=[r_wi[kc], r_h], writes=[r_pg[b]])
                        for kc in range(NCH):
                            k.op("pe", lambda e, fc=fc, kc=kc, b=b: e.matmul(
                                pu[b][:], lhsT=wi[:, kc, DFF + fc * 128:DFF + (fc + 1) * 128], rhs=ht[:, kc, :],
                                start=(kc == 0), stop=(kc == NCH - 1)),
                                reads=[r_wi[kc], r_h], writes=[r_pu[b]])
                        k.op("act", lambda e, b=b: e.activation(out=sg[b][:], in_=pg[b][:], func=AF.Silu),
                             reads=[r_pg[b]], writes=[r_sg[b]])
                        k.op("dve", lambda e, b=b, fc=fc: e.tensor_tensor(
                            out=act[:, fc, :], in0=pu[b][:], in1=sg[b][:], op=ALU.mult),
                            reads=[r_pu[b], r_sg[b]], writes=[r_act[fc]])
                    for dc in range(NCH):
                        b = dc % 2
                        for fc in range(NFC):
                            k.op("pe", lambda e, fc=fc, dc=dc, b=b: e.matmul(
                                py[b][:], lhsT=wo[:, fc, dc * 128:(dc + 1) * 128], rhs=act[:, fc, :],
                                start=(fc == 0), stop=(fc == NFC - 1)),
                                reads=[r_wo[fc // 11], r_act[fc]], writes=[r_py[b]])
                        k.op("dve", lambda e, dc=dc, b=b: e.scalar_tensor_tensor(
                            out=xt[:, dc, :], in0=py[b][:], scalar=GM[:, l, w, dc, cond:cond + 1],
                            in1=xt[:, dc, :], op0=ALU.mult, op1=ALU.add),
                            reads=[r_py[b], r_am, r_x], writes=[r_x])
                    k.dma("sp", [(yT[c, :, tsl], xt[:, c, :]) for c in range(NCH)], r_x,
                          reads=[r_x], writes=[r_yT])
                k.barrier()
                k.release(r_wi + r_wo + [r_x])

        cstf = sb("cstf", [128, 5, 128], F32)
        cstb = sb("cstb", [128, 13, 128], BF16)
        c4f = sb("c4f", [4, 4, 128], F32)
        c4b = sb("c4b", [4, 5, 128], BF16)
        pare = sb("pare", [128, 2, 2], F32)
        paro = sb("paro", [128, 2, 4], F32)
        bgo = sb("bgo", [4, 2, 4], F32)
        m0t = sb("m0t", [4, 2, 2], F32)
        r_cst = Res("cst")
        k.dma("sp", [(cstf[:], cst_f), (cstb[:], cst_b), (c4f[:], cst4), (c4b[:], cst4b), (pare[:], par_e),
                     (paro[:], par_o), (bgo[:], bg_o), (m0t[:], m0_o)], r_cst, writes=[r_cst])
        RM, BD, F128, IDN, ONESF = 0, 1, 2, 3, 4
        CB_CS, CB_MP, CB_MN, CB_MF, CB_MB, CB_ONE = 0, 2, 6, 10, 11, 12

        def load_w_bf(st, name, src2d, ncols):
            ncp = (ncols + 31) // 32 * 32
            wt = sb(name, [128, NCH, ncp], BF16, st)
            r = Res(name)
            k.dma("pool", [(wt[:, kc, 0:ncols], src2d[kc * 128:(kc + 1) * 128, :]) for kc in range(NCH)], r, writes=[r])
            return wt, r

        def proj_fm(wt, r_w, col0, ht, r_h, pq, r_pq, m=128):
            for kc in range(NCH):
                k.op("pe", lambda e, kc=kc: e.matmul(pq, lhsT=wt[:, kc, col0:col0 + m], rhs=ht[:, kc, :],
                                                     start=(kc == 0), stop=(kc == NCH - 1)),
                     reads=[r_w, r_h], writes=[r_pq])

        def proj_tm(wt, r_w, col0, ncols, ht, r_h, sub, pv, r_pv):
            for kc in range(NCH):
                k.op("pe", lambda e, kc=kc: e.matmul(pv, lhsT=ht[:, kc, sub * 128:(sub + 1) * 128],
                                                     rhs=wt[:, kc, col0:col0 + ncols],
                                                     start=(kc == 0), stop=(kc == NCH - 1)),
                     reads=[r_w, r_h], writes=[r_pv])

        class QKN:
            def __init__(self, st):
                self.sqq = sb("sqq", [128, TT], F32, st)
                self.rq = sb("rq", [128, TT], F32, st)
                self.qn = sb("qn", [128, TT], F32, st)
                self.t1 = sb("t1", [128, TT], F32, st)
                self.t2 = sb("t2", [128, TT], F32, st)
                self.pms = ps("pms", [128, TT], F32, st)
                self.prot = ps("prot", [128, TT], F32, st)
                self.r = {n: Res(n) for n in ("sqq", "rq", "qn", "t1", "t2", "pms", "prot")}

            def run(self, pq, r_pq, gain, rope, out_bf, r_out, r_rt=None):
                r = self.r
                k.op("act", lambda e: e.activation(out=self.sqq[:], in_=pq, func=AF.Square),
                     reads=[r_pq], writes=[r["sqq"]])
                k.op("pe", lambda e: e.matmul(self.pms[:], lhsT=cstf[:, BD, :], rhs=self.sqq[:], start=True, stop=True),
                     reads=[r["sqq"], r_cst], writes=[r["pms"]])
                k.op("act", lambda e: e.activation(out=self.rq[:], in_=self.pms[:], func=AF.Sqrt, bias=eps_t[:], scale=1.0),
                     reads=[r["pms"], r_ones], writes=[r["rq"]])
                k.op("dve", lambda e: e.reciprocal(out=self.rq[:], in_=self.rq[:]), reads=[r["rq"]], writes=[r["rq"]])
                k.op("dve", lambda e: e.scalar_tensor_tensor(out=self.qn[:], in0=pq, scalar=gain, in1=self.rq[:],
                                                             op0=ALU.mult, op1=ALU.mult),
                     reads=[r_pq, r["rq"], r_cst], writes=[r["qn"]])
                if rope is not None:
                    k.op("pe", lambda e: e.matmul(self.prot[:], lhsT=cstf[:, RM, :], rhs=self.qn[:], start=True, stop=True),
                         reads=[r["qn"], r_cst], writes=[r["prot"]])
                    k.op("pool", lambda e: e.tensor_tensor(out=self.t1[:], in0=self.qn[:], in1=rope[:, 0, :], op=ALU.mult),
                         reads=[r["qn"], r_rt], writes=[r["t1"]])
                    k.op("dve", lambda e: e.tensor_tensor(out=self.t2[:], in0=self.prot[:], in1=rope[:, 1, :], op=ALU.mult),
                         reads=[r["prot"], r_rt], writes=[r["t2"]])
                    k.op("pool", lambda e: e.tensor_tensor(out=out_bf, in0=self.t1[:], in1=self.t2[:], op=ALU.add),
                         reads=[r["t1"], r["t2"]], writes=[r_out])
                else:
                    k.op("act", lambda e: e.activation(out=out_bf, in_=self.qn[:], func=AF.Copy),
                         reads=[r["qn"]], writes=[r_out])

        def is_sample_tile(t):
            return t * TT < NS

        def even_proj(l, i):
            with ExitStack() as st:
                wie, r_wie = load_w_bf(st, "wie", w_in_e[i], 1280)
                xt = sb("xt", [128, NCH, TT], F32, st)
                r_x = Res("xt")
                ht = sb("ht", [128, NCH, TT], BF16, st)
                r_h = Res("ht")
                tmp = alloc_norm_tmp(st)
                qk = QKN(st)
                pq = [ps("pq%d" % b, [128, TT], F32, st) for b in range(2)]
                r_pq = [Res("pq%d" % b) for b in range(2)]
                pv = ps("pv", [128, 256], F32, st)
                r_pv = Res("pv")
                rt = sb("rt", [128, 2, TT], F32, st)
                r_rt = Res("rt")
                qo = sb("qo", [128, 5, TT], BF16, st)
                r_qo = Res("qo")
                va = sb("va", [128, 4, 2, 128], BF16, st)
                r_va = Res("va")
                v32 = sb("v32", [128, 4, 128], F32, st)
                r_v32 = Res("v32")
                ut = [sb("ut%d" % b, [128, TT], BF16, st) for b in range(2)]
                r_ut = [Res("ut%d" % b) for b in range(2)]
                pqt = sb("pqt", [128, 4, 4, 256], BF16, st)
                r_pqt = Res("pqt")
                k.op("pool", lambda e: e.memset(va[:], 0.0), writes=[r_va])
                k.op("pool", lambda e: e.memset(va[:, :, 0, 64:65], 1.0), writes=[r_va])
                k.op("pool", lambda e: e.memset(va[:, :, 1, 0:1], 1.0), writes=[r_va])
                r_scr = Res("scr_e")
                r_yT = Res("yT")
                for t in range(NT):
                    smp = is_sample_tile(t)
                    cond = 1 if smp else 0
                    tsl = slice(t * TT, (t + 1) * TT)
                    k.dma("sp", [(xt[:, c, :], yT[c, :, tsl]) for c in range(NCH)], r_x, reads=[r_yT], writes=[r_x])
                    if smp:
                        k.dma("sp", [(rt[:], ropeT[:, :, tsl])], r_rt, writes=[r_rt])
                    norm_mod(st, r_x, xt, ht, r_h, l, 1, cond, tmp)
                    for blk in range(5):
                        b = blk % 2
                        proj_fm(wie, r_wie, blk * 128, ht, r_h, pq[b][:], r_pq[b])
                        gain = pare[:, i, (0 if blk < 4 else 1):(1 if blk < 4 else 2)]
                        qk.run(pq[b][:], r_pq[b], gain, rt if smp else None, qo[:, blk, :], r_qo, r_rt)
                        if blk == 4 and not smp:
                            p0 = t * TT - NS
                            k.dma("sp", [(knew_a[i, :, p0:p0 + TT], qk.qn[:])], qk.r["qn"], reads=[qk.r["qn"]])
                    k.dma("sp", [(QT_d[:, :, tsl], qo[:, 0:4, :]), (KT_d[:, 0, tsl], qo[:, 4, :])], r_qo,
                          reads=[r_qo], writes=[r_scr])
                    for sub in range(4):
                        proj_tm(wie, r_wie, 640, 128, ht, r_h, sub, pv[:, 0:128], r_pv)
                        k.op("act", lambda e, sub=sub: e.activation(out=va[:, sub, 0, 0:64], in_=pv[:, 0:64], func=AF.Copy),
                             reads=[r_pv], writes=[r_va])
                        k.op("dve", lambda e, sub=sub: e.tensor_copy(out=va[:, sub, 1, 64:128], in_=pv[:, 64:128]),
                             reads=[r_pv], writes=[r_va])
                        if not smp:
                            k.op("dve", lambda e, sub=sub: e.tensor_copy(out=v32[:, sub, :], in_=pv[:, 0:128]),
                                 reads=[r_pv], writes=[r_v32])
                    k.dma("sp", [(VA_d[:, t * 4:(t + 1) * 4, :, :], va[:])], r_va, reads=[r_va], writes=[r_scr])
                    if not smp:
                        p0 = (t * TT - NS) // 128
                        k.dma("sp", [(vnew_a[i, :, p0:p0 + 4, :], v32[:])], r_v32, reads=[r_v32])
                    for g in range(4):
                        b = g % 2
                        proj_fm(wie, r_wie, 768 + g * 128, ht, r_h, pq[b][:], r_pq[b])
                        k.op("act", lambda e, b=b: e.activation(out=ut[b][:], in_=pq[b][:], func=AF.Copy),
                             reads=[r_pq[b]], writes=[r_ut[b]])
                        for sub in range(4):
                            k.op("pe", lambda e, b=b, sub=sub: e.matmul(
                                pv[:], lhsT=ut[b][:, sub * 128:(sub + 1) * 128], rhs=cstb[:, CB_CS:CB_CS + 2, :],
                                start=True, stop=True), reads=[r_ut[b], r_cst], writes=[r_pv])
                            eng = "dve" if sub % 2 == 0 else "act"
                            if eng == "dve":
                                k.op("dve", lambda e, g=g, sub=sub: e.tensor_copy(out=pqt[:, sub, g, :], in_=pv[:]),
                                     reads=[r_pv], writes=[r_pqt])
                            else:
                                k.op("act", lambda e, g=g, sub=sub: e.activation(out=pqt[:, sub, g, :], in_=pv[:], func=AF.Copy),
                                     reads=[r_pv], writes=[r_pqt])
                    k.dma("sp", [(PQ_d[:, t * 4:(t + 1) * 4, :, :], pqt[:])], r_pqt, reads=[r_pqt], writes=[r_scr])
                k.barrier()
                k.release([r_wie, r_x, r_rt, r_qo, r_va, r_v32, r_pqt, qk.r["qn"]])

        def even_fnet(i):
            with ExitStack() as st:
                pf = [ps("pf%d" % b, [128, 256], F32, st) for b in range(2)]
                r_pf = [Res("pf%d" % b) for b in range(2)]
                fo = [sb("fo%d" % b, [128, 4, 256], BF16, st) for b in range(2)]
                r_fo = [Res("fo%d" % b) for b in range(2)]
                rel = list(r_fo)
                r_scr = Res("mixd")
                for (off, S, cond, smp) in cfg.seqs:
                    nst, nkt = S // 128, S // 256
                    tabd = tabS if smp else tabP
                    with ExitStack() as s2:
                        pqs = sb("pqs", [128, nst, 4, 256], BF16, s2)
                        r_pqs = Res("pqs")
                        k.dma("sp", [(pqs[:, a:min(a + 8, nst)], PQ_d[:, off // 128 + a:off // 128 + min(a + 8, nst)])
                                     for a in range(0, nst, 8)], r_pqs, writes=[r_pqs])
                        tab = [sb("tab%d" % b, [128, nst, 2, 256], BF16, s2) for b in range(2)]
                        r_tab = [Res("tab%d" % b) for b in range(2)]
                        u = 0
                        for kt in range(nkt):
                            tb = kt % 2
                            k.dma("sp", [(tab[tb][:], tabd[kt])], r_tab[tb], writes=[r_tab[tb]])
                            fb = kt % 2
                            for g in range(4):
                                b = u % 2
                                u += 1
                                n = 0
                                for s_ in range(nst):
                                    for cs in range(2):
                                        k.op("pe", lambda e, b=b, s_=s_, cs=cs, g=g, n=n, tb=tb: e.matmul(
                                            pf[b][:], lhsT=pqs[:, s_, g, cs * 128:(cs + 1) * 128], rhs=tab[tb][:, s_, cs, :],
                                            start=(n == 0), stop=(n == 2 * nst - 1)),
                                            reads=[r_pqs, r_tab[tb]], writes=[r_pf[b]])
                                        n += 1
                                if g % 2 == 0:
                                    k.op("dve", lambda e, b=b, g=g, fb=fb: e.tensor_copy(out=fo[fb][:, g, :], in_=pf[b][:]),
                                         reads=[r_pf[b]], writes=[r_fo[fb]])
                                else:
                                    k.op("act", lambda e, b=b, g=g, fb=fb: e.activation(out=fo[fb][:, g, :], in_=pf[b][:], func=AF.Copy),
                                         reads=[r_pf[b]], writes=[r_fo[fb]])
                            k.dma("sp", [(MIX_d[:, 4:8, off + kt * 256:off + (kt + 1) * 256], fo[fb][:])], r_fo[fb],
                                  reads=[r_fo[fb]], writes=[r_scr])
                        k.barrier()
                        k.release([r_pqs] + r_tab)
                k.release(rel)

        def even_attn(i):
            with ExitStack() as st:
                esk = sb("esk", [128, 8], F32, st)
                r_esk = Res("esk")
                k.dma("sp", [(esk[0:1, :], sink_e[i:i + 1, :]), (esk[64:65, :], sink_e[i:i + 1, :])], r_esk, writes=[r_esk])
                k.op("act", lambda e: e.activation(out=esk[0:1, :], in_=esk[0:1, :], func=AF.Exp), reads=[r_esk], writes=[r_esk])
                k.op("act", lambda e: e.activation(out=esk[64:65, :], in_=esk[64:65, :], func=AF.Exp), reads=[r_esk], writes=[r_esk])
                pS = [ps("pS%d" % b, [128, 4, 128], F32, st) for b in range(2)]
                pO = [ps("pO%d" % b, [128, 4, 128], F32, st) for b in range(2)]
                pB = [ps("pB%d" % b, [128, 4, 128], F32, st) for b in range(2)]
                r_pS = [Res("pS%d" % b) for b in range(2)]
                r_pO = [Res("pO%d" % b) for b in range(2)]
                r_pB = [Res("pB%d" % b) for b in range(2)]
                pT = [sb("pT%d" % b, [128, 4, 128], BF16, st) for b in range(3)]
                r_pT = [Res("pT%d" % b) for b in range(3)]
                dd = [sb("dd%d" % b, [128, 4, 128], F32, st) for b in range(2)]
                r_dd = [Res("dd%d" % b) for b in range(2)]
                ob = [sb("ob%d" % b, [128, 4, 128], F32, st) for b in range(2)]
                r_ob = [Res("ob%d" % b) for b in range(2)]
                ao = [sb("ao%d" % b, [128, 4, 512], BF16, st) for b in range(2)]
                r_ao = [Res("ao%d" % b) for b in range(2)]
                r_scr = Res("mixd")
                rel = [r_esk] + r_ao
                for (off, S, cond, smp) in cfg.seqs:
                    nb = S // 128
                    with ExitStack() as s2:
                        qs = sb("qs", [128, 4, S], BF16, s2)
                        ks = sb("ks", [128, S], BF16, s2)
                        vs = sb("vs", [128, nb, 2, 128], BF16, s2)
                        r_q = Res("qs")
                        k.dma("sp", [(qs[:], QT_d[:, :, off:off + S]), (ks[:], KT_d[:, 0, off:off + S]),
                                     (vs[:], VA_d[:, off // 128:off // 128 + nb])], r_q, writes=[r_q])
                        rr = [r_q]
                        if smp:
                            kcx = sb("kcx", [128, PAST], BF16, s2)
                            vcx = sb("vcx", [128, 2, 2, 128], BF16, s2)
                            r_cx = Res("cx")
                            k.dma("pool", [(kcx[:], kctx_e[i]), (vcx[:], vctx_e[i])], r_cx, writes=[r_cx])
                            rr.append(r_cx)
                        u = 0
                        pti = 0
                        for qb in range(nb):
                            ab = (qb // 4) % 2
                            for g in range(2):
                                rows = slice(g * 64, (g + 1) * 64)
                                if smp:
                                    kts = [("l", j, (CB_MP if j == qb - 1 else (CB_MN if j == qb + 1 else None)))
                                           for j in (qb - 1, qb, qb + 1) if 0 <= j < nb]
                                    kts += [("c", 0, None), ("c", 1, None)]
                                else:
                                    kts = [("l", j, None) for j in range(nb)]
                                ub = u % 2
                                u += 1
                                for n, (kind, j, msk) in enumerate(kts):
                                    sbuf_i = pti % 2
                                    tbuf = pti % 3
                                    pti += 1
                                    if kind == "l":
                                        kap_, vap_, rk = ks[rows, j * 128:(j + 1) * 128], vs[:, j, g, :], r_q
                                    else:
                                        kap_, vap_, rk = kcx[rows, j * 128:(j + 1) * 128], vcx[:, j, g, :], r_cx
                                    k.op("pe", lambda e, kap_=kap_, sbuf_i=sbuf_i, rows=rows, qb=qb: e.matmul(
                                        pS[sbuf_i][:], lhsT=kap_, rhs=qs[rows, :, qb * 128:(qb + 1) * 128],
                                        start=True, stop=True), reads=[rk, r_q], writes=[r_pS[sbuf_i]])
                                    k.op("act", lambda e, sbuf_i=sbuf_i, tbuf=tbuf: e.activation(
                                        out=pT[tbuf][:], in_=pS[sbuf_i][:], func=AF.Exp, scale=0.125),
                                        reads=[r_pS[sbuf_i]], writes=[r_pT[tbuf]])
                                    if msk is not None:
                                        k.op("pool", lambda e, tbuf=tbuf, msk=msk: e.tensor_tensor(
                                            out=pT[tbuf][:], in0=pT[tbuf][:], in1=cstb[:, msk:msk + 4, :], op=ALU.mult),
                                            reads=[r_pT[tbuf], r_cst], writes=[r_pT[tbuf]])
                                    k.op("pe", lambda e, vap_=vap_, tbuf=tbuf, ub=ub, n=n, nk=len(kts): e.matmul(
                                        pO[ub][:], lhsT=vap_, rhs=pT[tbuf][:], start=(n == 0), stop=(n == nk - 1)),
                                        reads=[rk, r_pT[tbuf]], writes=[r_pO[ub]])
                                row = 64 if g == 0 else 0
                                k.op("dve", lambda e, ub=ub, row=row, g=g: e.tensor_tensor(
                                    out=dd[ub][row:row + 1], in0=pO[ub][row:row + 1],
                                    in1=esk[row:row + 1, g * 4:(g + 1) * 4].unsqueeze(2).to_broadcast([1, 4, 128]),
                                    op=ALU.add), reads=[r_pO[ub], r_esk], writes=[r_dd[ub]])
                                k.op("dve", lambda e, ub=ub, row=row: e.reciprocal(out=dd[ub][row:row + 1], in_=dd[ub][row:row + 1]),
                                     reads=[r_dd[ub]], writes=[r_dd[ub]])
                                k.op("pe", lambda e, ub=ub, row=row: e.matmul(
                                    pB[ub][:], lhsT=cstf[row:row + 1, ONESF, :], rhs=dd[ub][row:row + 1], start=True, stop=True),
                                    reads=[r_dd[ub], r_cst], writes=[r_pB[ub]])
                                k.op("act", lambda e, ub=ub, rows=rows: e.activation(out=ob[ub][rows], in_=pO[ub][rows], func=AF.Copy),
                                     reads=[r_pO[ub]], writes=[r_ob[ub]])
                                k.op("dve", lambda e, ub=ub, rows=rows, ab=ab, qb=qb: e.tensor_tensor(
                                    out=ao[ab][rows, :, (qb % 4) * 128:(qb % 4 + 1) * 128], in0=ob[ub][rows], in1=pB[ub][rows],
                                    op=ALU.mult), reads=[r_ob[ub], r_pB[ub]], writes=[r_ao[ab]])
                            if qb % 4 == 3 or qb == nb - 1:
                                q0 = (qb // 4) * 512
                                wdt = (qb % 4 + 1) * 128
                                k.dma("sp", [(MIX_d[:, 0:4, off + q0:off + q0 + wdt], ao[ab][:, :, 0:wdt])], r_ao[ab],
                                      reads=[r_ao[ab]], writes=[r_scr])
                        k.barrier()
                        k.release(rr)
                k.release(rel)

        def out_proj(l, wsrc):
            with ExitStack() as st:
                wom, r_wom = load_w_bf(st, "wom", wsrc, D)
                xt = sb("xt", [128, NCH, TT], F32, st)
                r_x = Res("xt")
                mx = sb("mx", [128, NCH, TT], BF16, st)
                r_mx = Res("mx")
                py = [ps("py%d" % b, [128, TT], F32, st) for b in range(2)]
                r_py = [Res("py%d" % b) for b in range(2)]
                r_yT = Res("yT")
                for t in range(NT):
                    cond = 1 if is_sample_tile(t) else 0
                    tsl = slice(t * TT, (t + 1) * TT)
                    k.dma("sp", [(xt[:, c, :], yT[c, :, tsl]) for c in range(NCH)], r_x, reads=[r_yT], writes=[r_x])
                    k.dma("sp", [(mx[:], MIX_d[:, :, tsl])], r_mx, writes=[r_mx])
                    for dc in range(NCH):
                        b = dc % 2
                        for kc in range(NCH):
                            k.op("pe", lambda e, kc=kc, dc=dc, b=b: e.matmul(
                                py[b][:], lhsT=wom[:, kc, dc * 128:(dc + 1) * 128], rhs=mx[:, kc, :],
                                start=(kc == 0), stop=(kc == NCH - 1)), reads=[r_wom, r_mx], writes=[r_py[b]])
                        k.op("dve", lambda e, dc=dc, b=b: e.scalar_tensor_tensor(
                            out=xt[:, dc, :], in0=py[b][:], scalar=GM[:, l, 1, dc, cond:cond + 1], in1=xt[:, dc, :],
                            op0=ALU.mult, op1=ALU.add), reads=[r_py[b], r_am, r_x], writes=[r_x])
                    k.dma("sp", [(yT[c, :, tsl], xt[:, c, :]) for c in range(NCH)], r_x, reads=[r_x], writes=[r_yT])
                k.barrier()
                k.release([r_wom, r_x, r_mx])

        DKS = 128 ** -0.5

        def odd_proj(l, i):
            with ExitStack() as st:
                wio, r_wio = load_w_bf(st, "wio", w_in_o[i], 3600)
                xt = sb("xt", [128, NCH, TT], F32, st)
                r_x = Res("xt")
                ht = sb("ht", [128, NCH, TT], BF16, st)
                r_h = Res("ht")
                tmp = alloc_norm_tmp(st)
                qk = QKN(st)
                pq = [ps("pq%d" % b, [128, TT], F32, st) for b in range(2)]
                r_pq = [Res("pq%d" % b) for b in range(2)]
                pvv = ps("pvv", [128, 512], F32, st)
                r_pvv = Res("pvv")
                pg4 = ps("pg4", [4, TT], F32, st)
                r_pg4 = Res("pg4")
                rt = sb("rt", [128, 2, TT], F32, st)
                r_rt = Res("rt")
                qo = sb("qo", [128, 8, TT], BF16, st)
                r_qo = Res("qo")
                vt = sb("vt", [128, 4, 512], BF16, st)
                r_vt = Res("vt")
                v32 = sb("v32", [128, 4, 512], F32, st)
                r_v32 = Res("v32")
                fo = sb("fo", [128, 12, TT], BF16, st)
                r_fo = Res("fo")
                kt_ = sb("kt_", [128, 4, 4, 128], BF16, st)
                r_kt = Res("kt_")
                v1 = sb("v1", [128, 4, 4, 160], BF16, st)
                r_v1 = Res("v1")
                gt = sb("gt", [4, 4, TT], F32, st)
                r_gt = Res("gt")
                k.op("pool", lambda e: e.memset(v1[:], 1.0), writes=[r_v1])
                r_scr = Res("scr_o")
                r_yT = Res("yT")
                for t in range(NT):
                    smp = is_sample_tile(t)
                    cond = 1 if smp else 0
                    tsl = slice(t * TT, (t + 1) * TT)
                    k.dma("sp", [(xt[:, c, :], yT[c, :, tsl]) for c in range(NCH)], r_x, reads=[r_yT], writes=[r_x])
                    if smp:
                        k.dma("sp", [(rt[:], ropeT[:, :, tsl])], r_rt, writes=[r_rt])
                    norm_mod(st, r_x, xt, ht, r_h, l, 1, cond, tmp)
                    sub_ = getattr(cfg, "odd_sub", 31)
                    for blk in (range(8) if sub_ & 1 else []):
                        b = blk % 2
                        proj_fm(wio, r_wio, blk * 128, ht, r_h, pq[b][:], r_pq[b])
                        gain = paro[:, i, (0 if blk < 4 else 1):(1 if blk < 4 else 2)]
                        qk.run(pq[b][:], r_pq[b], gain, rt if smp else None, qo[:, blk, :], r_qo, r_rt)
                        if blk >= 4 and not smp:
                            p0 = t * TT - NS
                            k.dma("sp", [(knew_c[i, :, blk - 4, p0:p0 + TT], qk.qn[:])], qk.r["qn"], reads=[qk.r["qn"]])
                    k.dma("sp", [(QT_d[:, :, tsl], qo[:, 0:4, :]), (KT_d[:, :, tsl], qo[:, 4:8, :])], r_qo,
                          reads=[r_qo], writes=[r_scr])
                    dbg_ = getattr(cfg, "odd_dbg", 31)
                    for sub in (range(4) if sub_ & 2 else []):
                        if dbg_ & 1:
                            proj_tm(wio, r_wio, 1024, 512, ht, r_h, sub, pvv[:], r_pvv)
                        if dbg_ & 2:
                            k.op("act", lambda e, sub=sub: e.activation(out=vt[:, sub, :], in_=pvv[:], func=AF.Copy),
                                 reads=[r_pvv], writes=[r_vt])
                        if not smp and (dbg_ & 4):
                            k.op("dve", lambda e, sub=sub: e.tensor_copy(out=v32[:, sub, :], in_=pvv[:]),
                                 reads=[r_pvv], writes=[r_v32])
                    if dbg_ & 8:
                        k.dma("sp", [(VCt_d[:, t * 4:(t + 1) * 4, :], vt[:])], r_vt, reads=[r_vt], writes=[r_scr])
                    if not smp and (dbg_ & 16):
                        p0 = (t * TT - NS) // 128
                        k.dma("sp", [(vnew_c[i, :, p0:p0 + 4, :], v32[:])], r_v32, reads=[r_v32])
                    for blk in (range(12) if sub_ & 4 else []):
                        b = blk % 2
                        col0 = (1536 + blk * 128) if blk < 8 else (3072 + (blk - 8) * 128)
                        proj_fm(wio, r_wio, col0, ht, r_h, pq[b][:], r_pq[b])
                        if blk < 4:
                            k.op("dve", lambda e, b=b, blk=blk: e.tensor_copy(out=fo[:, blk, :], in_=pq[b][:]),
                                 reads=[r_pq[b]], writes=[r_fo])
                        elif blk < 8:
                            k.op("act", lambda e, b=b, blk=blk: e.activation(out=fo[:, blk, :], in_=pq[b][:], func=AF.Identity, scale=DKS),
                                 reads=[r_pq[b]], writes=[r_fo])
                        else:
                            k.op("act", lambda e, b=b, blk=blk: e.activation(out=fo[:, blk, :], in_=pq[b][:], func=AF.Sigmoid),
                                 reads=[r_pq[b]], writes=[r_fo])
                    k.dma("sp", [(QDT_d[h, :, tsl], fo[:, h, :]) for h in range(4)]
                          + [(KDT_d[h, :, tsl], fo[:, 4 + h, :]) for h in range(4)]
                          + [(SODT_d[h, :, tsl], fo[:, 8 + h, :]) for h in range(4)], r_fo, reads=[r_fo], writes=[r_scr])
                    for sub in (range(4) if sub_ & 8 else []):
                        proj_tm(wio, r_wio, 2048, 512, ht, r_h, sub, pvv[:], r_pvv)
                        k.op("act", lambda e, sub=sub: e.activation(out=kt_[:, sub, :, :], in_=pvv[:].rearrange("p (h d) -> p h d", h=4),
                                                                    func=AF.Identity, scale=DKS), reads=[r_pvv], writes=[r_kt])
                        proj_tm(wio, r_wio, 2560, 512, ht, r_h, sub, pvv[:], r_pvv)
                        k.op("dve", lambda e, sub=sub: e.tensor_copy(out=v1[:, sub, :, 0:128], in_=pvv[:].rearrange("p (h d) -> p h d", h=4)),
                             reads=[r_pvv], writes=[r_v1])
                    k.dma("sp", [(KDt_d[h, :, t * 4:(t + 1) * 4, :], kt_[:, :, h, :]) for h in range(4)], r_kt,
                          reads=[r_kt], writes=[r_scr])
                    k.dma("sp", [(VD1t_d[h, :, t * 4:(t + 1) * 4, :], v1[:, :, h, :]) for h in range(4)], r_v1,
                          reads=[r_v1], writes=[r_scr])
                    for grp in (range(4) if sub_ & 16 else []):
                        proj_fm(wio, r_wio, 3584 + grp * 4, ht, r_h, pg4[:], r_pg4, m=4)
                        k.op("act", lambda e, grp=grp: e.activation(out=gt[:, grp, :], in_=pg4[:], func=AF.Identity,
                                                                    bias=bgo[:, i, grp:grp + 1], scale=1.0),
                             reads=[r_pg4, r_cst], writes=[r_gt])
                    k.dma("sp", [(G_d[grp, :, tsl], gt[:, grp, :]) for grp in range(4)], r_gt, reads=[r_gt], writes=[r_scr])
                k.barrier()
                k.release([r_wio, r_x, r_rt, r_qo, r_vt, r_v32, r_fo, r_kt, r_v1, r_gt, qk.r["qn"]])

        def odd_attn(l, i):
            lam_init = 0.8 - 0.6 * math.exp(-0.3 * l)
            with ExitStack() as st:
                lv = sb("lv", [1, 2, 2, 64], F32, st)
                pr = sb("pr", [1, 2, 64], F32, st)
                s2_ = sb("s2_", [1, 16], F32, st)
                nlb = sb("nlb", [128, 1], F32, st)
                r_lv = Res("lv")
                k.dma("sp", [(lv[:], lam_o[i:i + 1, :].rearrange("o (a b d) -> o a b d", a=2, b=2))], r_lv, writes=[r_lv])
                k.op("dve", lambda e: e.tensor_tensor(out=pr[:], in0=lv[:, :, 0, :], in1=lv[:, :, 1, :], op=ALU.mult),
                     reads=[r_lv], writes=[r_lv])
                k.op("dve", lambda e: e.reduce_sum(out=s2_[:, 0:2], in_=pr[:], axis=AX.X), reads=[r_lv], writes=[r_lv])
                k.op("act", lambda e: e.activation(out=s2_[:, 0:2], in_=s2_[:, 0:2], func=AF.Exp), reads=[r_lv], writes=[r_lv])
                k.op("dve", lambda e: e.tensor_tensor(out=s2_[:, 2:3], in0=s2_[:, 1:2], in1=s2_[:, 0:1], op=ALU.subtract),
                     reads=[r_lv], writes=[r_lv])
                k.op("dve", lambda e: e.tensor_scalar(out=s2_[:, 8:9], in0=s2_[:, 2:3], scalar1=-lam_init, scalar2=None, op0=ALU.add),
                     reads=[r_lv], writes=[r_lv])
                pS = [ps("pS%d" % b, [128, 512], F32, st) for b in range(2)]
                pO = [ps("pO%d" % b, [128, 512], F32, st) for b in range(2)]
                pD = [ps("pD%d" % b, [128, 512], F32, st) for b in range(2)]
                pms = ps("pms", [128, 512], F32, st)
                r_pS = [Res("pS%d" % b) for b in range(2)]
                r_pO = [Res("pO%d" % b) for b in range(2)]
                r_pD = [Res("pD%d" % b) for b in range(2)]
                r_pms = Res("pms")
                k.op("pe", lambda e: e.matmul(pms[:, 0:1], lhsT=cstf[0:1, ONESF, :], rhs=s2_[:, 8:9], start=True, stop=True),
                     reads=[r_lv, r_cst], writes=[r_pms])
                r_nlb = Res("nlb")
                k.op("dve", lambda e: e.tensor_copy(out=nlb[:], in_=pms[:, 0:1]), reads=[r_pms], writes=[r_nlb])
                E = [sb("E%d" % b, [128, 512], BF16, st) for b in range(3)]
                r_E = [Res("E%d" % b) for b in range(3)]
                rd = [sb("rd%d" % b, [128, 512], F32, st) for b in range(2)]
                om = [sb("om%d" % b, [128, 512], F32, st) for b in range(2)]
                r_rd = [Res("rd%d" % b) for b in range(2)]
                r_om = [Res("om%d" % b) for b in range(2)]
                aa = sb("aa", [128, 512], F32, st)
                sq = sb("sqa", [128, 512], F32, st)
                rq = sb("rqa", [128, 512], F32, st)
                r_aa, r_sq, r_rq = Res("aa"), Res("sqa"), Res("rqa")
                ao = [sb("ao%d" % b, [128, 4, 512], BF16, st) for b in range(2)]
                r_ao = [Res("ao%d" % b) for b in range(2)]
                r_scr = Res("mixd")
                rel = [r_lv] + r_ao
                for (off, S, cond, smp) in cfg.seqs:
                    nb = S // 128
                    QW = min(512, S)
                    with ExitStack() as s2:
                        qs = sb("qs", [128, 4, S], BF16, s2)
                        ks = sb("ks", [128, 4, S], BF16, s2)
                        vs = sb("vs", [128, nb, 512], BF16, s2)
                        r_q = Res("qs")
                        k.dma("sp", [(qs[:], QT_d[:, :, off:off + S]), (ks[:], KT_d[:, :, off:off + S]),
                                     (vs[:], VCt_d[:, off // 128:off // 128 + nb, :])], r_q, writes=[r_q])
                        rr = [r_q]
                        if smp:
                            kcx = sb("kcx", [128, 4, PAST], BF16, s2)
                            vcx = sb("vcx", [128, 2, 512], BF16, s2)
                            r_cx = Res("cx")
                            k.dma("pool", [(kcx[:], kctx_o[i]), (vcx[:], vctx_o[i])], r_cx, writes=[r_cx])
                            rr.append(r_cx)
                        kts = [("l", j) for j in range(nb)] + ([("c", 0), ("c", 1)] if smp else [])
                        ei = 0
                        for qt in range(S // QW):
                            ab = qt % 2
                            qsl = slice(qt * QW, (qt + 1) * QW)
                            for h in range(4):
                                for m in range(2):
                                    rows = slice(m * 64, (m + 1) * 64)
                                    for n, (kind, j) in enumerate(kts):
                                        sb_i = ei % 2
                                        eb = ei % 3
                                        ei += 1
                                        if kind == "l":
                                            kap_, vap_, rk = ks[rows, h, j * 128:(j + 1) * 128], vs[:, j, h * 128:(h + 1) * 128], r_q
                                        else:
                                            kap_, vap_, rk = kcx[rows, h, j * 128:(j + 1) * 128], vcx[:, j, h * 128:(h + 1) * 128], r_cx
                                        k.op("pe", lambda e, kap_=kap_, sb_i=sb_i, rows=rows, h=h, qsl=qsl: e.matmul(
                                            pS[sb_i][:, 0:QW], lhsT=kap_, rhs=qs[rows, h, qsl], start=True, stop=True),
                                            reads=[rk, r_q], writes=[r_pS[sb_i]])
                                        k.op("act", lambda e, sb_i=sb_i, eb=eb: e.activation(
                                            out=E[eb][:, 0:QW], in_=pS[sb_i][:, 0:QW], func=AF.Exp, scale=0.125),
                                            reads=[r_pS[sb_i]], writes=[r_E[eb]])
                                        k.op("pe", lambda e, vap_=vap_, eb=eb, m=m, n=n: e.matmul(
                                            pO[m][:, 0:QW], lhsT=vap_, rhs=E[eb][:, 0:QW], start=(n == 0), stop=(n == len(kts) - 1)),
                                            reads=[rk, r_E[eb]], writes=[r_pO[m]])
                                        k.op("pe", lambda e, eb=eb, m=m, n=n: e.matmul(
                                            pD[m][:, 0:QW], lhsT=cstb[:, CB_ONE, :], rhs=E[eb][:, 0:QW], start=(n == 0), stop=(n == len(kts) - 1)),
                                            reads=[r_cst, r_E[eb]], writes=[r_pD[m]])
                                    k.op("dve", lambda e, m=m: e.reciprocal(out=rd[m][:, 0:QW], in_=pD[m][:, 0:QW]),
                                         reads=[r_pD[m]], writes=[r_rd[m]])
                                    k.op("dve", lambda e, m=m: e.tensor_tensor(out=om[m][:, 0:QW], in0=pO[m][:, 0:QW], in1=rd[m][:, 0:QW], op=ALU.mult),
                                         reads=[r_pO[m], r_rd[m]], writes=[r_om[m]])
                                k.op("dve", lambda e: e.scalar_tensor_tensor(out=aa[:, 0:QW], in0=om[1][:, 0:QW], scalar=nlb[:, 0:1],
                                                                             in1=om[0][:, 0:QW], op0=ALU.mult, op1=ALU.add),
                                     reads=[r_om[0], r_om[1], r_nlb], writes=[r_aa])
                                k.op("act", lambda e: e.activation(out=sq[:, 0:QW], in_=aa[:, 0:QW], func=AF.Square),
                                     reads=[r_aa], writes=[r_sq])
                                k.op("pe", lambda e: e.matmul(pms[:, 0:QW], lhsT=cstf[:, F128, :], rhs=sq[:, 0:QW], start=True, stop=True),
                                     reads=[r_sq, r_cst], writes=[r_pms])
                                k.op("act", lambda e: e.activation(out=rq[:, 0:QW], in_=pms[:, 0:QW], func=AF.Sqrt, bias=eps_t[:], scale=1.0),
                                     reads=[r_pms, r_ones], writes=[r_rq])
                                k.op("dve", lambda e: e.reciprocal(out=rq[:, 0:QW], in_=rq[:, 0:QW]), reads=[r_rq], writes=[r_rq])
                                k.op("dve", lambda e: e.scalar_tensor_tensor(out=aa[:, 0:QW], in0=aa[:, 0:QW], scalar=paro[:, i, 2:3],
                                                                             in1=rq[:, 0:QW], op0=ALU.mult, op1=ALU.mult),
                                     reads=[r_aa, r_rq, r_cst], writes=[r_aa])
                                k.op("act", lambda e, ab=ab, h=h: e.activation(out=ao[ab][:, h, 0:QW], in_=aa[:, 0:QW], func=AF.Identity,
                                                                               scale=(1.0 - lam_init)),
                                     reads=[r_aa], writes=[r_ao[ab]])
                            k.dma("sp", [(MIX_d[:, 0:4, off + qt * QW:off + (qt + 1) * QW], ao[ab][:, :, 0:QW])], r_ao[ab],
                                  reads=[r_ao[ab]], writes=[r_scr])
                        k.barrier()
                        k.release(rr)
                k.release(rel)

        def odd_mlstm(l, i):
            nch = T // 128
            with ExitStack() as st:
                Bt = [sb("Bt%d" % d_, [4, T], BF16, st) for d_ in range(2)]
                CLt = [sb("CLt%d" % d_, [4, T], BF16, st) for d_ in range(2)]
                WI = [sb("WI%d" % d_, [4, nch], F32, st) for d_ in range(2)]
                mo = sb("mo", [4, 2, 2], F32, st)
                r_bk = Res("bk")
                r_mo = Res("mo")
                with ExitStack() as s1:
                    A1 = sb("A1", [4, T], F32, s1)
                    A2 = sb("A2", [4, T], F32, s1)
                    A3 = sb("A3", [4, T], F32, s1)
                    seg = sb("seg", [4, T], F32, s1)
                    bendN = sb("bendN", [4, nch], F32, s1)
                    kap = sb("kap", [4, nch], F32, s1)
                    Mr = sb("Mr", [4, nch], F32, s1)
                    WIe = sb("WIe", [4, nch], F32, s1)
                    marr = sb("marr", [4, nch], F32, s1)
                    zer = sb("zer", [4, 1], F32, s1)
                    r_a = Res("A")
                    r_seg = Res("seg")
                    k.op("pool", lambda e: e.memset(seg[:], 1.0), writes=[r_seg])
                    k.op("pool", lambda e: e.memset(seg[:].rearrange("p (c t) -> p c t", t=128)[:, :, 0:1], 0.0), writes=[r_seg])
                    k.op("pool", lambda e: e.memset(zer[:], 0.0), writes=[r_seg])
                    A2v = A2[:].rearrange("p (c t) -> p c t", t=128)
                    A3v = A3[:].rearrange("p (c t) -> p c t", t=128)
                    for d_ in range(2):
                        k.dma("sp", [(A1[:], G_d[d_ * 2 + 1]), (A2[:], G_d[d_ * 2])], r_a, writes=[r_a])
                        k.op("act", lambda e: e.activation(out=A1[:], in_=A1[:], func=AF.Exp, scale=-1.0), reads=[r_a], writes=[r_a])
                        k.op("act", lambda e: e.activation(out=A1[:], in_=A1[:], func=AF.Ln, bias=one_t[0:4, :], scale=1.0), reads=[r_a], writes=[r_a])
                        k.op("dve", lambda e: e.tensor_tensor_scan(out=A3[:], data0=seg[:], data1=A1[:], initial=0.0,
                                                                  op0=ALU.mult, op1=ALU.add), reads=[r_a, r_seg], writes=[r_a])
                        k.op("dve", lambda e: e.tensor_copy(out=bendN[:], in_=A3v[:, :, 127]), reads=[r_a], writes=[r_a])
                        if d_ == 1:
                            k.op("dve", lambda e: e.tensor_tensor(out=A3v, in0=bendN[:].unsqueeze(2).to_broadcast([4, nch, 128]),
                                                                  in1=A3v, op=ALU.subtract), reads=[r_a], writes=[r_a])
                            k.op("dve", lambda e: e.tensor_tensor(out=A3[:], in0=A3[:], in1=A1[:], op=ALU.add), reads=[r_a], writes=[r_a])
                        k.op("dve", lambda e: e.tensor_tensor(out=A2[:], in0=A2[:], in1=A3[:], op=ALU.add), reads=[r_a], writes=[r_a])
                        k.op("dve", lambda e: e.tensor_reduce(out=kap[:], in_=A2v, axis=AX.X, op=ALU.max), reads=[r_a], writes=[r_a])
                        for si, (off, S, cond, smp) in enumerate(cfg.seqs):
                            c0, c1 = off // 128, (off + S) // 128
                            order = list(range(c0, c1)) if d_ == 0 else list(range(c1 - 1, c0 - 1, -1))
                            mcur = m0t[:, i, d_:d_ + 1] if smp else zer[:]
                            for j in order:
                                k.op("dve", lambda e, j=j, mcur=mcur: e.tensor_tensor(out=Mr[:, j:j + 1], in0=mcur, in1=kap[:, j:j + 1], op=ALU.max),
                                     reads=[r_a, r_cst, r_seg], writes=[r_a])
                                k.op("dve", lambda e, j=j, mcur=mcur: e.tensor_tensor(out=WIe[:, j:j + 1], in0=mcur, in1=Mr[:, j:j + 1], op=ALU.subtract),
                                     reads=[r_a, r_cst, r_seg], writes=[r_a])
                                k.op("dve", lambda e, j=j: e.tensor_tensor(out=marr[:, j:j + 1], in0=Mr[:, j:j + 1], in1=bendN[:, j:j + 1], op=ALU.subtract),
                                     reads=[r_a], writes=[r_a])
                                mcur = marr[:, j:j + 1]
                            if not smp:
                                k.op("dve", lambda e, si=si, mcur=mcur, d_=d_: e.tensor_copy(out=mo[:, si - 1, d_:d_ + 1], in_=mcur),
                                     reads=[r_a], writes=[r_mo])
                        Mb = Mr[:].unsqueeze(2).to_broadcast([4, nch, 128])
                        k.op("dve", lambda e: e.tensor_tensor(out=A2v, in0=A2v, in1=Mb, op=ALU.subtract), reads=[r_a], writes=[r_a])
                        k.op("act", lambda e, d_=d_: e.activation(out=Bt[d_][:], in_=A2[:], func=AF.Exp), reads=[r_a], writes=[r_bk])
                        k.op("dve", lambda e: e.tensor_tensor(out=A3v, in0=A3v, in1=Mb, op=ALU.subtract), reads=[r_a], writes=[r_a])
                        k.op("act", lambda e, d_=d_: e.activation(out=CLt[d_][:], in_=A3[:], func=AF.Exp), reads=[r_a], writes=[r_bk])
                        k.op("act", lambda e, d_=d_: e.activation(out=WI[d_][:], in_=WIe[:], func=AF.Exp), reads=[r_a], writes=[r_bk])
                    k.dma("sp", [(mnew[:, i, :, :], mo[:])], r_mo, reads=[r_mo])
                    k.barrier()
                    k.release([r_a])
                pb = [ps("pb%d" % b, [128, 512], F32, st) for b in range(2)]
                r_pb = [Res("pb%d" % b) for b in range(2)]
                pmisc = ps("pmisc", [128, 2, nch], F32, st)
                r_pmisc = Res("pmisc")
                PA = [ps("PA%d" % b, [128, 3, 128], F32, st) for b in range(2)]
                r_PA = [[Res("PA%d_%d" % (b, c_)) for c_ in range(3)] for b in range(2)]
                r_PAL = [Res("PAlock%d" % b, lock=True) for b in range(2)]
                PC = [ps("PC%d" % b, [128, 129], F32, st) for b in range(2)]
                r_PC = [Res("PC%d" % b) for b in range(2)]
                r_scr = Res("mixd")
                for h in range(4):
                    with ExitStack() as sh:
                        qd = sb("qd", [128, T], BF16, sh)
                        kdT = sb("kdT", [128, T], BF16, sh)
                        kdt = sb("kdt", [128, nch, 128], BF16, sh)
                        vd1 = sb("vd1", [128, nch, 160], BF16, sh)
                        r_ld = Res("ld")
                        k.dma("sp", [(qd[:], QDT_d[h]), (kdT[:], KDT_d[h]), (kdt[:], KDt_d[h]), (vd1[:], VD1t_d[h])], r_ld, writes=[r_ld])
                        Hs = sb("Hs", [128, T], F32, sh)
                        r_Hs = [Res("Hs%d" % j) for j in range(nch)]
                        KpT = sb("KpT", [128, T], BF16, sh)
                        CLB = sb("CLB", [128, T], BF16, sh)
                        Kpt = sb("Kpt", [128, nch, 128], BF16, sh)
                        bcol = sb("bcol", [128, nch], F32, sh)
                        wib = sb("wib", [128, nch], F32, sh)
                        r_pre = Res("pre")
                        cst_ = [sb("cst%d" % b, [128, 129], F32, sh) for b in range(2)]
                        cs = [sb("cs%d" % b, [128, 128], BF16, sh) for b in range(2)]
                        nsb = [sb("nsb%d" % b, [128, 128], BF16, sh) for b in range(2)]
                        r_c = [Res("c%d" % b) for b in range(2)]
                        r_cs = [Res("cs%d" % b) for b in range(2)]
                        scm = [sb("scm%d" % b, [128, 128], BF16, sh) for b in range(2)]
                        r_scm = [Res("scm%d" % b) for b in range(2)]
                        dcl = [sb("dcl%d" % b, [128, 128], F32, sh) for b in range(2)]
                        r_dcl = [Res("dcl%d" % b) for b in range(2)]
                        htmp = [sb("htmp%d" % b, [128, 128], F32, sh) for b in range(2)]
                        r_htmp = [Res("htmp%d" % b) for b in range(2)]
                        u = 0
                        for d_ in range(2):
                            for t in range(NT):
                                tsl = slice(t * TT, (t + 1) * TT)
                                k.op("pe", lambda e, tsl=tsl, d_=d_: e.matmul(pb[0][:], lhsT=c4b[:, h, :], rhs=Bt[d_][:, tsl], start=True, stop=True),
                                     reads=[r_bk, r_cst], writes=[r_pb[0]])
                                k.op("dve", lambda e, tsl=tsl: e.tensor_tensor(out=KpT[:, tsl], in0=kdT[:, tsl], in1=pb[0][:], op=ALU.mult),
                                     reads=[r_pb[0], r_ld], writes=[r_pre])
                                k.op("pe", lambda e, tsl=tsl, d_=d_: e.matmul(pb[1][:], lhsT=c4b[:, h, :], rhs=CLt[d_][:, tsl], start=True, stop=True),
                                     reads=[r_bk, r_cst], writes=[r_pb[1]])
                                k.op("act", lambda e, tsl=tsl: e.activation(out=CLB[:, tsl], in_=pb[1][:], func=AF.Copy),
                                     reads=[r_pb[1]], writes=[r_pre])
                            for j in range(nch):
                                k.op("pe", lambda e, j=j, d_=d_: e.matmul(pmisc[:, 0, j:j + 1], lhsT=Bt[d_][:, j * 128:(j + 1) * 128],
                                                                          rhs=c4b[:, 4, h * 32:h * 32 + 1], start=True, stop=True),
                                     reads=[r_bk, r_cst], writes=[r_pmisc])
                            k.op("pe", lambda e, d_=d_: e.matmul(pmisc[:, 1, :], lhsT=c4f[:, h, :], rhs=WI[d_][:], start=True, stop=True),
                                 reads=[r_bk, r_cst], writes=[r_pmisc])
                            k.op("dve", lambda e: e.tensor_copy(out=bcol[:], in_=pmisc[:, 0, :]), reads=[r_pmisc], writes=[r_pre])
                            k.op("dve", lambda e: e.tensor_copy(out=wib[:], in_=pmisc[:, 1, :]), reads=[r_pmisc], writes=[r_pre])
                            k.op("pool", lambda e: e.tensor_tensor(out=Kpt[:], in0=kdt[:], in1=bcol[:].unsqueeze(2).to_broadcast([128, nch, 128]),
                                                                   op=ALU.mult), reads=[r_pre, r_ld], writes=[r_pre])
                            mT = CB_MF if d_ == 0 else CB_MB
                            for si, (off, S, cond, smp) in enumerate(cfg.seqs):
                                c0, c1 = off // 128, (off + S) // 128
                                order = list(range(c0, c1)) if d_ == 0 else list(range(c1 - 1, c0 - 1, -1))
                                cur = 0
                                if smp:
                                    k.dma("sp", [(cst_[cur][:, 0:128], c0_o[:, i, d_, h, :]), (cst_[cur][:, 128:129], n0_o[i, d_, h])],
                                          r_c[cur], writes=[r_c[cur]])
                                else:
                                    k.op("pool", lambda e, cur=cur: e.memset(cst_[cur][:], 0.0), writes=[r_c[cur]])

                                def scaled_state(cur, j):
                                    k.op("act", lambda e: e.activation(out=cs[cur][:], in_=cst_[cur][:, 0:128], func=AF.Identity,
                                                                       scale=wib[:, j:j + 1]), reads=[r_c[cur], r_pre], writes=[r_cs[cur]])
                                    k.op("dve", lambda e: e.tensor_scalar(out=nsb[cur][:], in0=cst_[cur][:, 128:129].to_broadcast([128, 128]),
                                                                          scalar1=wib[:, j:j + 1], scalar2=None, op0=ALU.mult),
                                         reads=[r_c[cur], r_pre], writes=[r_cs[cur]])
                                scaled_state(cur, order[0])
                                for oi, j in enumerate(order):
                                    ub = u % 2
                                    u += 1
                                    P = PA[ub]
                                    rP = r_PA[ub]
                                    rL = r_PAL[ub]
                                    csl = slice(j * 128, (j + 1) * 128)
                                    k.op("pe", lambda e, P=P, csl=csl: e.matmul(P[:, 0, :], lhsT=KpT[:, csl], rhs=qd[:, csl], start=True, stop=True),
                                         reads=[r_pre, r_ld], writes=[rP[0], rL])
                                    k.op("dve", lambda e, P=P, ub=ub, mT=mT: e.tensor_tensor(out=scm[ub][:], in0=P[:, 0, :], in1=cstb[:, mT, :], op=ALU.mult),
                                         reads=[rP[0], r_cst], writes=[r_scm[ub], rL])
                                    k.op("pe", lambda e, P=P, ub=ub, j=j: e.matmul(P[:, 1, :], lhsT=vd1[:, j, 0:128], rhs=scm[ub][:], start=True, stop=False),
                                         reads=[r_ld, r_scm[ub]], writes=[rP[1], rL])
                                    k.op("pe", lambda e, P=P, cur=cur, csl=csl: e.matmul(P[:, 1, :], lhsT=cs[cur][:], rhs=qd[:, csl], start=False, stop=True),
                                         reads=[r_cs[cur], r_ld], writes=[rP[1], rL])
                                    k.op("pe", lambda e, P=P, ub=ub: e.matmul(P[:, 2, :], lhsT=cstb[:, CB_ONE, :], rhs=scm[ub][:], start=True, stop=False),
                                         reads=[r_cst, r_scm[ub]], writes=[rP[2], rL])
                                    k.op("pe", lambda e, P=P, cur=cur, csl=csl: e.matmul(P[:, 2, :], lhsT=nsb[cur][:], rhs=qd[:, csl], start=False, stop=True),
                                         reads=[r_cs[cur], r_ld], writes=[rP[2], rL])
                                    k.op("act", lambda e, P=P, ub=ub: e.activation(out=dcl[ub][:], in_=P[:, 2, :], func=AF.Abs),
                                         reads=[rP[2]], writes=[r_dcl[ub], rL])
                                    k.op("dve", lambda e, ub=ub, csl=csl: e.tensor_tensor(out=dcl[ub][:], in0=dcl[ub][:], in1=CLB[:, csl], op=ALU.max),
                                         reads=[r_dcl[ub], r_pre], writes=[r_dcl[ub]])
                                    k.op("dve", lambda e, ub=ub: e.reciprocal(out=dcl[ub][:], in_=dcl[ub][:]), reads=[r_dcl[ub]], writes=[r_dcl[ub]])
                                    if d_ == 0:
                                        k.op("dve", lambda e, P=P, ub=ub, csl=csl: e.tensor_tensor(out=Hs[:, csl], in0=P[:, 1, :], in1=dcl[ub][:], op=ALU.mult),
                                             reads=[rP[1], r_dcl[ub]], writes=[r_Hs[j], rL])
                                    else:
                                        k.op("dve", lambda e, P=P, ub=ub: e.tensor_tensor(out=htmp[ub][:], in0=P[:, 1, :], in1=dcl[ub][:], op=ALU.mult),
                                             reads=[rP[1], r_dcl[ub]], writes=[r_htmp[ub], rL])
                                        k.op("pool", lambda e, ub=ub, csl=csl: e.tensor_tensor(out=Hs[:, csl], in0=Hs[:, csl], in1=htmp[ub][:], op=ALU.add),
                                             reads=[r_htmp[ub], r_Hs[j]], writes=[r_Hs[j]])
                                    k.op("pe", lambda e, ub=ub, j=j: e.matmul(PC[ub][:], lhsT=Kpt[:, j, :], rhs=vd1[:, j, 0:129], start=True, stop=True),
                                         reads=[r_pre, r_ld], writes=[r_PC[ub]])
                                    nxt = 1 - cur
                                    k.op("dve", lambda e, ub=ub, cur=cur, nxt=nxt, j=j: e.scalar_tensor_tensor(
                                        out=cst_[nxt][:], in0=cst_[cur][:], scalar=wib[:, j:j + 1], in1=PC[ub][:], op0=ALU.mult, op1=ALU.add),
                                        reads=[r_c[cur], r_PC[ub], r_pre], writes=[r_c[nxt]])
                                    cur = nxt
                                    if oi + 1 < len(order):
                                        scaled_state(cur, order[oi + 1])
                                if not smp:
                                    k.dma("sp", [(Cnew[i, si - 1, d_, h], cst_[cur][:, 0:128]), (nnew[i, si - 1, d_, h], cst_[cur][:, 128:129])],
                                          r_c[cur], reads=[r_c[cur]])
                        sqh = sb("sqh", [128, TT], F32, sh)
                        rqh = sb("rqh", [128, TT], F32, sh)
                        hm = sb("hm", [128, TT], F32, sh)
                        sod = sb("sod", [128, TT], BF16, sh)
                        ho = [sb("ho%d" % b, [128, TT], BF16, sh) for b in range(2)]
                        r_sqh, r_rqh, r_hm, r_sod = Res("sqh"), Res("rqh"), Res("hm"), Res("sod")
                        r_ho = [Res("ho%d" % b) for b in range(2)]
                        for t in range(NT):
                            tsl = slice(t * TT, (t + 1) * TT)
                            rH = r_Hs[t * 4:(t + 1) * 4]
                            k.dma("sp", [(sod[:], SODT_d[h, :, tsl])], r_sod, writes=[r_sod])
                            k.op("act", lambda e, tsl=tsl: e.activation(out=sqh[:], in_=Hs[:, tsl], func=AF.Square), reads=rH, writes=[r_sqh])
                            k.op("pe", lambda e: e.matmul(pb[0][:], lhsT=cstf[:, F128, :], rhs=sqh[:], start=True, stop=True),
                                 reads=[r_sqh, r_cst], writes=[r_pb[0]])
                            k.op("act", lambda e: e.activation(out=rqh[:], in_=pb[0][:], func=AF.Sqrt, bias=eps_t[:], scale=1.0),
                                 reads=[r_pb[0], r_ones], writes=[r_rqh])
                            k.op("dve", lambda e: e.reciprocal(out=rqh[:], in_=rqh[:]), reads=[r_rqh], writes=[r_rqh])
                            k.op("dve", lambda e, tsl=tsl: e.scalar_tensor_tensor(out=hm[:], in0=Hs[:, tsl], scalar=paro[:, i, 3:4], in1=rqh[:],
                                                                                 op0=ALU.mult, op1=ALU.mult), reads=rH + [r_rqh, r_cst], writes=[r_hm])
                            b = t % 2
                            k.op("pool", lambda e, b=b: e.tensor_tensor(out=ho[b][:], in0=hm[:], in1=sod[:], op=ALU.mult),
                                 reads=[r_hm, r_sod], writes=[r_ho[b]])
                            k.dma("sp", [(MIX_d[:, 4 + h, tsl], ho[b][:])], r_ho[b], reads=[r_ho[b]], writes=[r_scr])
                        k.barrier()
                        k.release([r_ld, r_sod] + r_ho + r_c)
                k.release([r_mo])

        first = True
        for l in range(cfg.depth):
            i = l // 2
            ffn_phase(l, 0, xT_in if first else yT)
            first = False
            if cfg.do_mix:
                if l % 2 == 0:
                    even_proj(l, i)
                    even_fnet(i)
                    even_attn(i)
                    out_proj(l, w_out_e[i])
                else:
                    stage = getattr(cfg, "odd_stage", 9)
                    if stage >= 1:
                        odd_proj(l, i)
                    if stage >= 2:
                        odd_attn(l, i)
                    if stage >= 3:
                        odd_mlstm(l, i)
                    if stage >= 4:
                        out_proj(l, w_out_o[i])
            ffn_phase(l, 1, yT)
        k.barrier()
        print("instructions:", k.ninst, "waits:", k.nwait, "dma sems:", k.n_dma_sems, "sems:", len(k.sems))
    return nc


def _fm(x2d):
    return np.ascontiguousarray(x2d.T.reshape(NCH, 128, x2d.shape[0]))


def _tm(xT):
    return np.ascontiguousarray(xT.reshape(D, xT.shape[2]).T)


def _bf(a):
    return np.ascontiguousarray(a.astype(np.float32)).astype(ml_dtypes.bfloat16)


_CONST_CACHE = {}


def make_consts(cfg):
    key = (cfg.NS, cfg.NP)
    if key in _CONST_CACHE:
        return _CONST_CACHE[key]
    f32 = np.float32
    p = np.arange(128)
    cst_f = np.zeros((128, 5, 128), f32)
    for m in range(128):
        if m % 32 < 16:
            cst_f[m + 16, 0, m] = -1.0
        else:
            cst_f[m - 16, 0, m] = 1.0
    cst_f[:, 1, :] = (p[:, None] // 64 == p[None, :] // 64) / 64.0
    cst_f[:, 2, :] = 1.0 / 128.0
    cst_f[:, 3, :] = np.eye(128)
    cst_f[:, 4, :] = 1.0
    cst_b = np.zeros((128, 13, 128), f32)
    ang = 2.0 * np.pi * ((p[:, None] * p[None, :]) % 128) / 128.0
    cst_b[:, 0, :] = np.cos(ang)
    cst_b[:, 1, :] = -np.sin(ang)
    mprev = (p[None, :] <= p[:, None]).astype(f32)
    mnext = (p[:, None] <= p[None, :]).astype(f32)
    for hh in range(4):
        cst_b[:, 2 + hh, :] = mprev
        cst_b[:, 6 + hh, :] = mnext
    cst_b[:, 10, :] = (p[:, None] <= p[None, :])
    cst_b[:, 11, :] = (p[:, None] >= p[None, :])
    cst_b[:, 12, :] = 1.0
    cst4 = np.zeros((4, 4, 128), f32)
    for h in range(4):
        cst4[h, h, :] = 1.0
    cst4b = np.zeros((4, 5, 128), f32)
    cst4b[:, 0:4, :] = cst4
    for h in range(4):
        cst4b[h, 4, h * 32] = 1.0
    NS = cfg.NS
    tok = np.arange(NS)
    row = (tok // 64).astype(np.float64)
    col = (tok % 64).astype(np.float64)
    inv = 10000.0 ** (-np.arange(16, dtype=np.float32) / 16).astype(np.float32)
    d = p % 64
    axis = d // 32
    fr = d % 16
    pos = np.where(axis[:, None] == 0, row[None, :], col[None, :]).astype(np.float32)
    angr = (pos * inv[fr][:, None]).astype(np.float32)
    ropeT = np.stack([np.cos(angr), np.sin(angr)], axis=1).astype(f32)

    def seq_tab(S):
        s = np.arange(S, dtype=np.int64)
        prod = (s[:, None] * s[None, :]) % S
        a = 2.0 * np.pi * prod / S
        sc = 1.0 / math.sqrt(S * 128.0)
        cs = np.stack([np.cos(a) * sc, np.sin(a) * sc], axis=0).astype(f32)
        t = cs.reshape(2, S // 128, 128, S // 256, 256).transpose(3, 2, 1, 0, 4)
        return _bf(t)

    out = dict(cst_f=cst_f, cst_b=_bf(cst_b), cst4=cst4, cst4b=_bf(cst4b), ropeT=np.ascontiguousarray(ropeT),
               tabS=seq_tab(cfg.NS), tabP=seq_tab(cfg.NP))
    _CONST_CACHE[key] = out
    return out


def make_in_maps(cfg, inp, n_cores=8):
    f32 = np.float32
    g = lambda n: np.asarray(inp[n], f32)
    cs = make_consts(cfg)
    b_adaT = np.ascontiguousarray(g("b_ada").reshape(DEPTH, 72, 128).transpose(2, 0, 1))
    g_normT = np.ascontiguousarray(g("g_norm").reshape(DEPTH, 3, NCH, 128).transpose(3, 0, 1, 2))
    qperm = np.concatenate([np.r_[c * 64:(c + 1) * 64, (c + 4) * 64:(c + 5) * 64] for c in range(4)])
    wie = g("w_in_even")
    w_in_e = np.ascontiguousarray(np.concatenate([wie[:, :, qperm], wie[:, :, 512:]], axis=2))
    woe = g("w_out_even")
    w_out_e = np.ascontiguousarray(np.concatenate([woe[:, qperm, :], woe[:, 512:, :]], axis=1))
    p64 = np.arange(128) % 64
    par_e = np.ascontiguousarray(np.stack([g("qn_a")[:, p64], g("kn_a")[:, p64]], axis=-1).transpose(1, 0, 2))
    par_o = np.ascontiguousarray(np.stack([g("qn_c")[:, p64], g("kn_c")[:, p64], g("subln_c"), g("outnorm_d")],
                                          axis=-1).transpose(1, 0, 2))
    bg_o = np.ascontiguousarray(g("b_gate_odd").reshape(2, 4, 4).transpose(2, 0, 1))
    lam_o = np.ascontiguousarray(g("lam_c").reshape(2, 256))
    shared = dict(cs)
    shared.update(w_ada=g("w_ada"), b_adaT=b_adaT, g_normT=g_normT, w_ffn_in=g("w_ffn_in"), w_ffn_out=g("w_ffn_out"),
                  w_in_e=w_in_e, w_out_e=w_out_e, par_e=par_e, sink_e=g("sink_a"), w_in_o=g("w_in_odd"),
                  w_out_o=g("w_out_odd"), par_o=par_o, bg_o=bg_o, lam_o=lam_o)
    maps = []
    for core in range(n_cores):
        b = core // 2
        toks = np.concatenate([g("x_sample")[b], g("x_prompt")[2 * core], g("x_prompt")[2 * core + 1]], axis=0)
        cT = np.stack([g("c_ctx").reshape(NCH, 128).T, g("c")[b].reshape(NCH, 128).T], axis=-1)
        cka = g("cache_k_a")[b]
        kctx_e = np.ascontiguousarray(cka.transpose(0, 2, 3, 1).reshape(2, 128, PAST))
        cva = g("cache_v_a")[b]
        vctx_e = np.zeros((2, 128, 2, 2, 128), f32)
        cv = cva.reshape(2, 2, 128, 2, 64)
        vctx_e[:, :, :, 0, 0:64] = cv[:, :, :, 0, :].transpose(0, 2, 1, 3)
        vctx_e[:, :, :, 0, 64] = 1.0
        vctx_e[:, :, :, 1, 64:128] = cv[:, :, :, 1, :].transpose(0, 2, 1, 3)
        vctx_e[:, :, :, 1, 0] = 1.0
        ckc = g("cache_k_c")[b]
        kctx_o = np.ascontiguousarray(ckc.transpose(0, 3, 4, 2, 1).reshape(2, 128, 4, PAST))
        cvc = g("cache_v_c")[b]
        vctx_o = np.ascontiguousarray(cvc.reshape(2, 2, 128, 512).transpose(0, 2, 1, 3))
        sC = g("state_C_d")[b]
        c0_o = np.ascontiguousarray(sC.transpose(3, 0, 1, 2, 4))
        sn = g("state_n_d")[b]
        n0_o = np.ascontiguousarray(sn.reshape(2, 2, 4, 128, 1))
        sm = g("state_m_d")[b]
        m0_o = np.ascontiguousarray(sm.transpose(2, 0, 1))
        m = dict(shared)
        m.update(xT=_fm(toks), cT=np.ascontiguousarray(cT), kctx_e=kctx_e, vctx_e=vctx_e, kctx_o=kctx_o,
                 vctx_o=vctx_o, c0_o=c0_o, n0_o=n0_o, m0_o=m0_o)
        maps.append(m)
    return maps


def assemble(cfg, outs, B, n_cores=8):
    NS, NP = cfg.NS, cfg.NP
    f32 = np.float32
    yp = np.zeros((B, NP, D), f32)
    ys = np.zeros((max(1, n_cores // 2), NS, D), f32)
    ka = np.zeros((B, 2, NP, 2, 64), f32)
    va = np.zeros((B, 2, NP, 2, 64), f32)
    kc = np.zeros((B, 2, NP, 4, 2, 64), f32)
    vc = np.zeros((B, 2, NP, 4, 128), f32)
    Cd = np.zeros((B, 2, 2, 4, 128, 128), f32)
    nd = np.zeros((B, 2, 2, 4, 128), f32)
    md = np.zeros((B, 2, 2, 4), f32)
    for core in range(n_cores):
        o = outs[core]
        y = _tm(np.asarray(o["yT"], f32))
        if core % 2 == 0:
            ys[core // 2] = y[:NS]
        for s in range(2):
            bi = 2 * core + s
            yp[bi] = y[NS + s * NP:NS + (s + 1) * NP]
            tsl = slice(s * NP, (s + 1) * NP)
            kk = np.asarray(o["knew_a"], f32)[:, :, tsl]
            ka[bi] = kk.reshape(2, 2, 64, NP).transpose(0, 3, 1, 2)
            vv = np.asarray(o["vnew_a"], f32)
            vv = vv.transpose(0, 2, 1, 3).reshape(2, 2 * NP, 2, 64)[:, tsl]
            va[bi] = vv
            kk = np.asarray(o["knew_c"], f32)[:, :, :, tsl]
            kc[bi] = kk.reshape(2, 2, 64, 4, NP).transpose(0, 4, 3, 1, 2)
            vv = np.asarray(o["vnew_c"], f32).transpose(0, 2, 1, 3).reshape(2, 2 * NP, 4, 128)[:, tsl]
            vc[bi] = vv
            Cd[bi] = np.asarray(o["Cnew"], f32)[:, s]
            nd[bi] = np.asarray(o["nnew"], f32)[:, s, :, :, :, 0]
            md[bi] = np.asarray(o["mnew"], f32)[:, :, s, :].transpose(1, 2, 0)
    return (yp, ys, ka, va, kc, vc, Cd, nd, md)


_NC_CACHE = {}


def kernel(**inputs):
    inp = {k_: np.asarray(v) for k_, v in inputs.items()}
    cfg = Cfg()
    if "nc" not in _NC_CACHE:
        _NC_CACHE["nc"] = build(cfg)
    nc = _NC_CACHE["nc"]
    maps = make_in_maps(cfg, inp)
    res = run_bass_kernel_spmd(nc, maps, core_ids=list(range(8)))
    return assemble(cfg, res.results, inp["x_prompt"].shape[0])
```

```python
import math
import re
from contextlib import ExitStack
import numpy as np
import ml_dtypes
import concourse.bass as bass
import concourse.mybir as mybir
from concourse.bass_utils import run_bass_kernel_spmd

F32 = mybir.dt.float32
BF16 = mybir.dt.bfloat16
AF = mybir.ActivationFunctionType
ALU = mybir.AluOpType
AX = mybir.AxisListType

D = 1024
NCH = 8
DEPTH = 4
DFF = 2816
NFC = 22
EPS = 1e-6
HD = 64
PAST = 256
NEG = -30000.0


_PS_RE = re.compile(r"^(pm|msb|pg\d|pu\d|py\d|pq\d|pv|pvv|pg4|pms|prot|pf\d|pS\d|pO\d|pB\d|pD\d|pb\d|pmisc|PA\d_\d|PC\d)$")


class Res:
    __slots__ = ("name", "lw", "rd", "sem", "cnt", "ldma", "ps", "lock")

    def __init__(self, name, lock=False):
        self.name = name
        self.ps = bool(_PS_RE.match(name))
        self.lock = lock
        self.lw = None
        self.rd = []
        self.sem = None
        self.cnt = 0
        self.ldma = None


class Ev:
    __slots__ = ("key", "val", "snap", "eng")

    def __init__(self, key, val, snap, eng):
        self.key, self.val, self.snap, self.eng = key, val, snap, eng


class K:
    EPOCH = 20000

    def __init__(self, nc, stack):
        self.nc = nc
        self.stack = stack
        self.eng = {"pe": nc.tensor, "act": nc.scalar, "dve": nc.vector, "pool": nc.gpsimd, "sp": nc.sync}
        self.seq = {e: 0 for e in self.eng}
        self.know = {e: {} for e in self.eng}
        self.sems = {}
        self.dma_free = []
        self.n_dma_sems = 0
        self.dma_cnt = {}
        self.nwait = 0
        self.ninst = 0

    def sem(self, key):
        s = self.sems.get(key)
        if s is None:
            s = self.stack.enter_context(self.nc.semaphore("s_%s" % (str(key).replace(" ", ""))))
            self.sems[key] = s
        return s

    def _need(self, e, ev):
        if ev is None:
            return
        kn = self.know[e]
        if kn.get(ev.key, 0) >= ev.val:
            return
        self.eng[e].wait_ge(self.sem(ev.key), ev.val)
        self.nwait += 1
        kn[ev.key] = ev.val
        for k2, v2 in ev.snap.items():
            if kn.get(k2, 0) < v2:
                kn[k2] = v2

    def _deps(self, e, reads, writes):
        for r in reads:
            self._need(e, r.lw)
            if r.ps:
                for ev in r.rd:
                    if ev.eng != e:
                        self._need(e, ev)
        for w in writes:
            lw = w.lw
            if lw is not None and not (lw.eng == e and (e == "pe" or w.lock)):
                self._need(e, lw)
            for ev in w.rd:
                if ev.eng != e or e in ("sp", "pool"):
                    self._need(e, ev)

    def _commit(self, ev, reads, writes):
        for r in reads:
            r.rd.append(ev)
        for w in writes:
            w.lw = ev
            w.rd = []

    def op(self, e, fn, reads=(), writes=()):
        self._deps(e, reads, writes)
        n = self.seq[e]
        key = (e, n // self.EPOCH)
        val = n % self.EPOCH + 1
        ins = fn(self.eng[e])
        ins.then_inc(self.sem(key), 1)
        self.seq[e] = n + 1
        self.ninst += 1
        ev = Ev(key, val, dict(self.know[e]), e)
        self._commit(ev, reads, writes)
        return ev

    def dma(self, q, pairs, own, reads=(), writes=()):
        if own.sem is None:
            if self.dma_free:
                own.sem = self.dma_free.pop()
            else:
                own.sem = ("dma", self.n_dma_sems)
                self.n_dma_sems += 1
        self._need(q, own.ldma)
        self._deps(q, reads, writes)
        s = self.sem(own.sem)
        c = self.dma_cnt.get(own.sem, 0)
        for (o, i) in pairs:
            self.eng[q].dma_start(out=o, in_=i).then_inc(s, 16)
            c += 1
            self.ninst += 1
        self.dma_cnt[own.sem] = c
        ev = Ev(own.sem, 16 * c, dict(self.know[q]), "dma")
        own.ldma = ev
        self._commit(ev, reads, writes)
        return ev

    def release(self, ress):
        for r in ress:
            if r.sem is not None:
                self.dma_free.append(r.sem)
                r.sem = None

    def barrier(self):
        evs = []
        for e in self.eng:
            n = self.seq[e]
            if n > 0:
                evs.append(Ev((e, (n - 1) // self.EPOCH), (n - 1) % self.EPOCH + 1, {}, e))
        for key, c in self.dma_cnt.items():
            if c > 0:
                evs.append(Ev(key, 16 * c, {}, "dma"))
        for e in self.eng:
            for ev in evs:
                if ev.eng == e and e != "dma":
                    pass
                self._need(e, ev)


class Cfg:
    def __init__(self, ns=4096, npr=256, depth=DEPTH, do_mix=True):
        self.NS = ns
        self.NP = npr
        self.T = ns + 2 * npr
        self.depth = depth
        self.do_mix = do_mix
        self.TT = 512
        assert self.T % self.TT == 0 and ns % self.TT == 0
        self.NT = self.T // self.TT
        self.seqs = [(0, ns, 1, True), (ns, npr, 0, False), (ns + npr, npr, 0, False)]


def build(cfg):
    nc = bass.Bass("TRN2", target_bir_lowering=False)
    T, TT, NT = cfg.T, cfg.TT, cfg.NT
    dt = nc.dram_tensor

    xT_in = dt("xT", [NCH, 128, T], F32, kind="ExternalInput").ap()
    cT_in = dt("cT", [128, NCH, 2], F32, kind="ExternalInput").ap()
    w_ada = dt("w_ada", [DEPTH, D, 9 * D], F32, kind="ExternalInput").ap()
    b_adaT = dt("b_adaT", [128, DEPTH, 72], F32, kind="ExternalInput").ap()
    g_normT = dt("g_normT", [128, DEPTH, 3, NCH], F32, kind="ExternalInput").ap()
    w_ffn_in = dt("w_ffn_in", [DEPTH, 2, D, 2 * DFF], F32, kind="ExternalInput").ap()
    w_ffn_out = dt("w_ffn_out", [DEPTH, 2, DFF, D], F32, kind="ExternalInput").ap()
    yT = dt("yT", [NCH, 128, T], F32, kind="ExternalOutput").ap()
    NS, NP = cfg.NS, cfg.NP
    NTK = T // 128
    NPT = 2 * NP
    cst_f = dt("cst_f", [128, 5, 128], F32, kind="ExternalInput").ap()
    cst_b = dt("cst_b", [128, 13, 128], BF16, kind="ExternalInput").ap()
    cst4 = dt("cst4", [4, 4, 128], F32, kind="ExternalInput").ap()
    cst4b = dt("cst4b", [4, 5, 128], BF16, kind="ExternalInput").ap()
    ropeT = dt("ropeT", [128, 2, NS], F32, kind="ExternalInput").ap()
    tabS = dt("tabS", [NS // 256, 128, NS // 128, 2, 256], BF16, kind="ExternalInput").ap()
    tabP = dt("tabP", [NP // 256, 128, NP // 128, 2, 256], BF16, kind="ExternalInput").ap()
    w_in_e = dt("w_in_e", [2, D, 1280], F32, kind="ExternalInput").ap()
    w_out_e = dt("w_out_e", [2, D, D], F32, kind="ExternalInput").ap()
    par_e = dt("par_e", [128, 2, 2], F32, kind="ExternalInput").ap()
    sink_e = dt("sink_e", [2, 8], F32, kind="ExternalInput").ap()
    kctx_e = dt("kctx_e", [2, 128, PAST], F32, kind="ExternalInput").ap()
    vctx_e = dt("vctx_e", [2, 128, 2, 2, 128], F32, kind="ExternalInput").ap()
    w_in_o = dt("w_in_o", [2, D, 3600], F32, kind="ExternalInput").ap()
    w_out_o = dt("w_out_o", [2, D, D], F32, kind="ExternalInput").ap()
    par_o = dt("par_o", [128, 2, 4], F32, kind="ExternalInput").ap()
    bg_o = dt("bg_o", [4, 2, 4], F32, kind="ExternalInput").ap()
    lam_o = dt("lam_o", [2, 256], F32, kind="ExternalInput").ap()
    kctx_o = dt("kctx_o", [2, 128, 4, PAST], F32, kind="ExternalInput").ap()
    vctx_o = dt("vctx_o", [2, 128, 2, 512], F32, kind="ExternalInput").ap()
    c0_o = dt("c0_o", [128, 2, 2, 4, 128], F32, kind="ExternalInput").ap()
    n0_o = dt("n0_o", [2, 2, 4, 128, 1], F32, kind="ExternalInput").ap()
    m0_o = dt("m0_o", [4, 2, 2], F32, kind="ExternalInput").ap()
    knew_a = dt("knew_a", [2, 128, NPT], F32, kind="ExternalOutput").ap()
    vnew_a = dt("vnew_a", [2, 128, NPT // 128, 128], F32, kind="ExternalOutput").ap()
    knew_c = dt("knew_c", [2, 128, 4, NPT], F32, kind="ExternalOutput").ap()
    vnew_c = dt("vnew_c", [2, 128, NPT // 128, 512], F32, kind="ExternalOutput").ap()
    Cnew = dt("Cnew", [2, 2, 2, 4, 128, 128], F32, kind="ExternalOutput").ap()
    nnew = dt("nnew", [2, 2, 2, 4, 128, 1], F32, kind="ExternalOutput").ap()
    mnew = dt("mnew", [4, 2, 2, 2], F32, kind="ExternalOutput").ap()
    QT_d = dt("QT_d", [128, 4, T], BF16, kind="Internal").ap()
    KT_d = dt("KT_d", [128, 4, T], BF16, kind="Internal").ap()
    VA_d = dt("VA_d", [128, NTK, 2, 128], BF16, kind="Internal").ap()
    PQ_d = dt("PQ_d", [128, NTK, 4, 256], BF16, kind="Internal").ap()
    MIX_d = dt("MIX_d", [128, 8, T], BF16, kind="Internal").ap()
    VCt_d = dt("VCt_d", [128, NTK, 512], BF16, kind="Internal").ap()
    QDT_d = dt("QDT_d", [4, 128, T], BF16, kind="Internal").ap()
    KDT_d = dt("KDT_d", [4, 128, T], BF16, kind="Internal").ap()
    KDt_d = dt("KDt_d", [4, 128, NTK, 128], BF16, kind="Internal").ap()
    VD1t_d = dt("VD1t_d", [4, 128, NTK, 160], BF16, kind="Internal").ap()
    SODT_d = dt("SODT_d", [4, 128, T], BF16, kind="Internal").ap()
    G_d = dt("G_d", [4, 4, T], F32, kind="Internal").ap()

    with ExitStack() as gs:
        k = K(nc, gs)

        uid = [0]

        def sb(name, shape, dtype, st=gs):
            uid[0] += 1
            return st.enter_context(nc.sbuf_tensor("%s_%d" % (name, uid[0]), shape, dtype))

        def ps(name, shape, dtype, st=gs):
            uid[0] += 1
            return st.enter_context(nc.psum_tensor("%s_%d" % (name, uid[0]), shape, dtype))

        ones_bf = sb("ones_bf", [128, 128], BF16)
        r_ones = Res("ones_bf")
        k.op("pool", lambda e: e.memset(ones_bf[:], 1.0 / D), writes=[r_ones])
        eps_t = sb("eps_t", [128, 1], F32)
        k.op("pool", lambda e: e.memset(eps_t[:], EPS), writes=[r_ones])
        one_t = sb("one_t", [128, 1], F32)
        k.op("pool", lambda e: e.memset(one_t[:], 1.0), writes=[r_ones])
        cTs = sb("cTs", [128, NCH, 2], F32)
        scT = sb("scT", [128, NCH, 2], BF16)
        bada = sb("bada", [128, DEPTH, 72], F32)
        gnrm = sb("gnrm", [128, DEPTH, 3, NCH], F32)
        r_par = Res("par")
        k.dma("sp", [(cTs[:], cT_in), (bada[:], b_adaT), (gnrm[:], g_normT)], r_par, writes=[r_par])
        r_scT = Res("scT")
        k.op("act", lambda e: e.activation(out=scT[:], in_=cTs[:], func=AF.Silu), reads=[r_par], writes=[r_scT])
        MOD = sb("MOD", [128, DEPTH, 9, NCH, 2], F32)
        r_mod = Res("MOD")
        AM = sb("AM", [128, DEPTH, 3, NCH, 2], F32)
        GM = sb("GM", [128, DEPTH, 3, NCH, 2], F32)
        r_am = Res("AM")

        with ExitStack() as st:
            wa = [sb("wa%d" % i, [128, NCH, D], BF16, st) for i in range(2)]
            r_wa = [Res("wa%d" % i) for i in range(2)]
            pm = ps("pm", [128, 8, 2], F32, st)
            r_pm = Res("pm")
            it = 0
            for l in range(cfg.depth):
                for j in range(9):
                    b = it % 2
                    it += 1
                    k.dma("pool", [(wa[b][:, kc, :], w_ada[l, kc * 128:(kc + 1) * 128, j * D:(j + 1) * D])
                                   for kc in range(NCH)], r_wa[b], writes=[r_wa[b]])
                    for cc in range(NCH):
                        for kc in range(NCH):
                            k.op("pe", lambda e, cc=cc, kc=kc, b=b: e.matmul(
                                pm[:, cc, :], lhsT=wa[b][:, kc, cc * 128:(cc + 1) * 128], rhs=scT[:, kc, :],
                                start=(kc == 0), stop=(kc == NCH - 1)),
                                reads=[r_wa[b], r_scT], writes=[r_pm])
                    k.op("dve", lambda e, l=l, j=j: e.tensor_tensor(
                        out=MOD[:, l, j, :, :], in0=pm[:],
                        in1=bada[:, l, j * 8:(j + 1) * 8].unsqueeze(2).to_broadcast([128, 8, 2]),
                        op=ALU.add), reads=[r_pm, r_par], writes=[r_mod])
            for l in range(cfg.depth):
                for w in range(3):
                    k.op("dve", lambda e, l=l, w=w: e.scalar_tensor_tensor(
                        out=AM[:, l, w, :, :], in0=MOD[:, l, 3 * w + 1, :, :], scalar=1.0,
                        in1=gnrm[:, l, w, :].unsqueeze(2).to_broadcast([128, 8, 2]),
                        op0=ALU.add, op1=ALU.mult), reads=[r_mod, r_par], writes=[r_am])
                    k.op("dve", lambda e, l=l, w=w: e.tensor_scalar(
                        out=GM[:, l, w, :, :], in0=MOD[:, l, 3 * w + 2, :, :],
                        scalar1=(1.0 if w == 1 else 0.5), scalar2=None, op0=ALU.mult),
                        reads=[r_mod], writes=[r_am])
            k.barrier()
            k.release(r_wa)

        def norm_mod(st_x, r_x, xt, ht, r_h, l, w, cond, tmp):
            sq, r_sq, msb, r_ms, rstd, r_rstd, u, r_u = tmp
            for c in range(NCH):
                b = c % 2
                k.op("act", lambda e, c=c, b=b: e.activation(out=sq[b][:], in_=xt[:, c, :], func=AF.Square),
                     reads=[r_x], writes=[r_sq[b]])
                k.op("pe", lambda e, c=c, b=b: e.matmul(msb[:], lhsT=ones_bf[:], rhs=sq[b][:],
                                                       start=(c == 0), stop=(c == NCH - 1)),
                     reads=[r_sq[b], r_ones], writes=[r_ms])
            k.op("act", lambda e: e.activation(out=rstd[:], in_=msb[:], func=AF.Sqrt, bias=eps_t[:], scale=1.0),
                 reads=[r_ms, r_ones], writes=[r_rstd])
            k.op("dve", lambda e: e.reciprocal(out=rstd[:], in_=rstd[:]), reads=[r_rstd], writes=[r_rstd])
            for c in range(NCH):
                b = c % 2
                k.op("dve", lambda e, c=c, b=b: e.scalar_tensor_tensor(
                    out=u[b][:], in0=xt[:, c, :], scalar=AM[:, l, w, c, cond:cond + 1], in1=rstd[:],
                    op0=ALU.mult, op1=ALU.mult), reads=[r_x, r_rstd, r_am], writes=[r_u[b]])
                k.op("act", lambda e, c=c, b=b: e.activation(
                    out=ht[:, c, :], in_=u[b][:], func=AF.Identity,
                    bias=MOD[:, l, 3 * w, c, cond:cond + 1], scale=1.0),
                    reads=[r_u[b], r_mod], writes=[r_h])

        def alloc_norm_tmp(st):
            sq = [sb("sq%d" % i, [128, TT], BF16, st) for i in range(2)]
            r_sq = [Res("sq%d" % i) for i in range(2)]
            msb = ps("msb", [128, TT], F32, st)
            rstd = sb("rstd", [128, TT], F32, st)
            u = [sb("u%d" % i, [128, TT], F32, st) for i in range(2)]
            r_u = [Res("u%d" % i) for i in range(2)]
            return (sq, r_sq, msb, Res("msb"), rstd, Res("rstd"), u, r_u)

        def tile_cond(t):
            return 1 if t * TT < cfg.NS else 0

        def ffn_phase(l, j, src):
            w = 0 if j == 0 else 2
            with ExitStack() as st:
                wi = sb("wi", [128, NCH, 2 * DFF], BF16, st)
                wo = sb("wo", [128, NFC, D], BF16, st)
                r_wi = [Res("wi%d" % i) for i in range(NCH)]
                r_wo = [Res("wo%d" % i) for i in range(2)]
                for kc in range(NCH):
                    k.dma("pool", [(wi[:, kc, :], w_ffn_in[l, j, kc * 128:(kc + 1) * 128, :])], r_wi[kc],
                          writes=[r_wi[kc]])
                for hf in range(2):
                    k.dma("pool", [(wo[:, fc, :], w_ffn_out[l, j, fc * 128:(fc + 1) * 128, :])
                                   for fc in range(hf * 11, hf * 11 + 11)], r_wo[hf], writes=[r_wo[hf]])
                xt = sb("xt", [128, NCH, TT], F32, st)
                r_x = Res("xt")
                ht = sb("ht", [128, NCH, TT], BF16, st)
                r_h = Res("ht")
                tmp = alloc_norm_tmp(st)
                sg = [sb("sg%d" % i, [128, TT], F32, st) for i in range(2)]
                r_sg = [Res("sg%d" % i) for i in range(2)]
                act = sb("act", [128, NFC, TT], BF16, st)
                r_act = [Res("act%d" % i) for i in range(NFC)]
                pg = [ps("pg%d" % i, [128, TT], F32, st) for i in range(2)]
                pu = [ps("pu%d" % i, [128, TT], F32, st) for i in range(2)]
                py = [ps("py%d" % i, [128, TT], F32, st) for i in range(2)]
                r_pg = [Res("pg%d" % i) for i in range(2)]
                r_pu = [Res("pu%d" % i) for i in range(2)]
                r_py = [Res("py%d" % i) for i in range(2)]
                r_yT = Res("yT")
                for t in range(NT):
                    cond = tile_cond(t)
                    tsl = slice(t * TT, (t + 1) * TT)
                    k.dma("sp", [(xt[:, c, :], src[c, :, tsl]) for c in range(NCH)], r_x,
                          reads=[r_yT], writes=[r_x])
                    norm_mod(st, r_x, xt, ht, r_h, l, w, cond, tmp)
                    for fc in range(NFC):
                        b = fc % 2
                        for kc in range(NCH):
                            k.op("pe", lambda e, fc=fc, kc=kc, b=b: e.matmul(
                                pg[b][:], lhsT=wi[:, kc, fc * 128:(fc + 1) * 128], rhs=ht[:, kc, :],
                                start=(kc == 0), stop=(kc == NCH - 1)),
                                reads=[r_wi[kc], r_h], writes=[r_pg[b]])
                        for kc in range(NCH):
                            k.op("pe", lambda e, fc=fc, kc=kc, b=b: e.matmul(
                                pu[b][:], lhsT=wi[:, kc, DFF + fc * 128:DFF + (fc + 1) * 128], rhs=ht[:, kc, :],
                                start=(kc == 0), stop=(kc == NCH - 1)),
                                reads=[r_wi[kc], r_h], writes=[r_pu[b]])
                        k.op("act", lambda e, b=b: e.activation(out=sg[b][:], in_=pg[b][:], func=AF.Silu),
                             reads=[r_pg[b]], writes=[r_sg[b]])
                        k.op("dve", lambda e, b=b, fc=fc: e.tensor_tensor(
                            out=act[:, fc, :], in0=pu[b][:], in1=sg[b][:], op=ALU.mult),
                            reads=[r_pu[b], r_sg[b]], writes=[r_act[fc]])
                    for dc in range(NCH):
                        b = dc % 2
                        for fc in range(NFC):
                            k.op("pe", lambda e, fc=fc, dc=dc, b=b: e.matmul(
                                py[b][:], lhsT=wo[:, fc, dc * 128:(dc + 1) * 128], rhs=act[:, fc, :],
                                start=(fc == 0), stop=(fc == NFC - 1)),
                                reads=[r_wo[fc // 11], r_act[fc]], writes=[r_py[b]])
                        k.op("dve", lambda e, dc=dc, b=b: e.scalar_tensor_tensor(
                            out=xt[:, dc, :], in0=py[b][:], scalar=GM[:, l, w, dc, cond:cond + 1],
                            in1=xt[:, dc, :], op0=ALU.mult, op1=ALU.add),
                            reads=[r_py[b], r_am, r_x], writes=[r_x])
                    k.dma("sp", [(yT[c, :, tsl], xt[:, c, :]) for c in range(NCH)], r_x,
                          reads=[r_x], writes=[r_yT])
                k.barrier()
                k.release(r_wi + r_wo + [r_x])

        cstf = sb("cstf", [128, 5, 128], F32)
        cstb = sb("cstb", [128, 13, 128], BF16)
        c4f = sb("c4f", [4, 4, 128], F32)
        c4b = sb("c4b", [4, 5, 128], BF16)
        pare = sb("pare", [128, 2, 2], F32)
        paro = sb("paro", [128, 2, 4], F32)
        bgo = sb("bgo", [4, 2, 4], F32)
        m0t = sb("m0t", [4, 2, 2], F32)
        r_cst = Res("cst")
        k.dma("sp", [(cstf[:], cst_f), (cstb[:], cst_b), (c4f[:], cst4), (c4b[:], cst4b), (pare[:], par_e),
                     (paro[:], par_o), (bgo[:], bg_o), (m0t[:], m0_o)], r_cst, writes=[r_cst])
        RM, BD, F128, IDN, ONESF = 0, 1, 2, 3, 4
        CB_CS, CB_MP, CB_MN, CB_MF, CB_MB, CB_ONE = 0, 2, 6, 10, 11, 12

        def load_w_bf(st, name, src2d, ncols):
            ncp = (ncols + 31) // 32 * 32
            wt = sb(name, [128, NCH, ncp], BF16, st)
            r = Res(name)
            k.dma("pool", [(wt[:, kc, 0:ncols], src2d[kc * 128:(kc + 1) * 128, :]) for kc in range(NCH)], r, writes=[r])
            return wt, r

        def proj_fm(wt, r_w, col0, ht, r_h, pq, r_pq, m=128):
            for kc in range(NCH):
                k.op("pe", lambda e, kc=kc: e.matmul(pq, lhsT=wt[:, kc, col0:col0 + m], rhs=ht[:, kc, :],
                                                     start=(kc == 0), stop=(kc == NCH - 1)),
                     reads=[r_w, r_h], writes=[r_pq])

        def proj_tm(wt, r_w, col0, ncols, ht, r_h, sub, pv, r_pv):
            for kc in range(NCH):
                k.op("pe", lambda e, kc=kc: e.matmul(pv, lhsT=ht[:, kc, sub * 128:(sub + 1) * 128],
                                                     rhs=wt[:, kc, col0:col0 + ncols],
                                                     start=(kc == 0), stop=(kc == NCH - 1)),
                     reads=[r_w, r_h], writes=[r_pv])

        class QKN:
            def __init__(self, st):
                self.sets = []
                for z in range(2):
                    d_ = dict(
                        sqq=sb("sqq%d" % z, [128, TT], F32, st), rq=sb("rq%d" % z, [128, TT], F32, st),
                        qn=sb("qn%d" % z, [128, TT], F32, st), t1=sb("t1%d" % z, [128, TT], F32, st),
                        t2=sb("t2%d" % z, [128, TT], F32, st), pms=ps("pms", [128, TT], F32, st),
                        prot=ps("prot", [128, TT], F32, st))
                    d_["r"] = {n: Res(n) for n in ("sqq", "rq", "qn", "t1", "t2", "pms", "prot")}
                    self.sets.append(d_)
                self.n = 0
                self.last = None

            def all_qn_res(self):
                return [d_["r"]["qn"] for d_ in self.sets]

            def run(self, pq, r_pq, gain, rope, out_bf, r_out, r_rt=None):
                S_ = self.sets[self.n % 2]
                self.n += 1
                self.last = S_
                r = S_["r"]
                sqq, rq, qn, t1, t2, pms, prot = (S_[n_] for n_ in ("sqq", "rq", "qn", "t1", "t2", "pms", "prot"))
                k.op("act", lambda e: e.activation(out=sqq[:], in_=pq, func=AF.Square),
                     reads=[r_pq], writes=[r["sqq"]])
                k.op("pe", lambda e: e.matmul(pms[:], lhsT=cstf[:, BD, :], rhs=sqq[:], start=True, stop=True),
                     reads=[r["sqq"], r_cst], writes=[r["pms"]])
                k.op("act", lambda e: e.activation(out=rq[:], in_=pms[:], func=AF.Sqrt, bias=eps_t[:], scale=1.0),
                     reads=[r["pms"], r_ones], writes=[r["rq"]])
                k.op("dve", lambda e: e.reciprocal(out=rq[:], in_=rq[:]), reads=[r["rq"]], writes=[r["rq"]])
                k.op("dve", lambda e: e.scalar_tensor_tensor(out=qn[:], in0=pq, scalar=gain, in1=rq[:],
                                                             op0=ALU.mult, op1=ALU.mult),
                     reads=[r_pq, r["rq"], r_cst], writes=[r["qn"]])
                if rope is not None:
                    k.op("pe", lambda e: e.matmul(prot[:], lhsT=cstf[:, RM, :], rhs=qn[:], start=True, stop=True),
                         reads=[r["qn"], r_cst], writes=[r["prot"]])
                    k.op("pool", lambda e: e.tensor_tensor(out=t1[:], in0=qn[:], in1=rope[:, 0, :], op=ALU.mult),
                         reads=[r["qn"], r_rt], writes=[r["t1"]])
                    k.op("dve", lambda e: e.tensor_tensor(out=t2[:], in0=prot[:], in1=rope[:, 1, :], op=ALU.mult),
                         reads=[r["prot"], r_rt], writes=[r["t2"]])
                    k.op("pool", lambda e: e.tensor_tensor(out=out_bf, in0=t1[:], in1=t2[:], op=ALU.add),
                         reads=[r["t1"], r["t2"]], writes=[r_out])
                else:
                    k.op("act", lambda e: e.activation(out=out_bf, in_=qn[:], func=AF.Copy),
                         reads=[r["qn"]], writes=[r_out])

        def is_sample_tile(t):
            return t * TT < NS

        def even_proj(l, i):
            with ExitStack() as st:
                wie, r_wie = load_w_bf(st, "wie", w_in_e[i], 1280)
                xt = sb("xt", [128, NCH, TT], F32, st)
                r_x = Res("xt")
                ht = sb("ht", [128, NCH, TT], BF16, st)
                r_h = Res("ht")
                tmp = alloc_norm_tmp(st)
                qk = QKN(st)
                pq = [ps("pq%d" % b, [128, TT], F32, st) for b in range(2)]
                r_pq = [Res("pq%d" % b) for b in range(2)]
                pv = ps("pv", [128, 256], F32, st)
                r_pv = Res("pv")
                rt = sb("rt", [128, 2, TT], F32, st)
                r_rt = Res("rt")
                qo = sb("qo", [128, 5, TT], BF16, st)
                r_qo = Res("qo")
                va = sb("va", [128, 4, 2, 128], BF16, st)
                r_va = Res("va")
                v32 = sb("v32", [128, 4, 128], F32, st)
                r_v32 = Res("v32")
                ut = [sb("ut%d" % b, [128, TT], BF16, st) for b in range(2)]
                r_ut = [Res("ut%d" % b) for b in range(2)]
                pqt = sb("pqt", [128, 4, 4, 256], BF16, st)
                r_pqt = Res("pqt")
                k.op("pool", lambda e: e.memset(va[:], 0.0), writes=[r_va])
                k.op("pool", lambda e: e.memset(va[:, :, 0, 64:65], 1.0), writes=[r_va])
                k.op("pool", lambda e: e.memset(va[:, :, 1, 0:1], 1.0), writes=[r_va])
                r_scr = Res("scr_e")
                r_yT = Res("yT")
                for t in range(NT):
                    smp = is_sample_tile(t)
                    cond = 1 if smp else 0
                    tsl = slice(t * TT, (t + 1) * TT)
                    k.dma("sp", [(xt[:, c, :], yT[c, :, tsl]) for c in range(NCH)], r_x, reads=[r_yT], writes=[r_x])
                    if smp:
                        k.dma("sp", [(rt[:], ropeT[:, :, tsl])], r_rt, writes=[r_rt])
                    norm_mod(st, r_x, xt, ht, r_h, l, 1, cond, tmp)
                    for blk in range(5):
                        b = blk % 2
                        proj_fm(wie, r_wie, blk * 128, ht, r_h, pq[b][:], r_pq[b])
                        gain = pare[:, i, (0 if blk < 4 else 1):(1 if blk < 4 else 2)]
                        qk.run(pq[b][:], r_pq[b], gain, rt if smp else None, qo[:, blk, :], r_qo, r_rt)
                        if blk == 4 and not smp:
                            p0 = t * TT - NS
                            k.dma("sp", [(knew_a[i, :, p0:p0 + TT], qk.last["qn"][:])], qk.last["r"]["qn"], reads=[qk.last["r"]["qn"]])
                    k.dma("sp", [(QT_d[:, :, tsl], qo[:, 0:4, :]), (KT_d[:, 0, tsl], qo[:, 4, :])], r_qo,
                          reads=[r_qo], writes=[r_scr])
                    for sub in range(4):
                        proj_tm(wie, r_wie, 640, 128, ht, r_h, sub, pv[:, 0:128], r_pv)
                        k.op("act", lambda e, sub=sub: e.activation(out=va[:, sub, 0, 0:64], in_=pv[:, 0:64], func=AF.Copy),
                             reads=[r_pv], writes=[r_va])
                        k.op("dve", lambda e, sub=sub: e.tensor_copy(out=va[:, sub, 1, 64:128], in_=pv[:, 64:128]),
                             reads=[r_pv], writes=[r_va])
                        if not smp:
                            k.op("dve", lambda e, sub=sub: e.tensor_copy(out=v32[:, sub, :], in_=pv[:, 0:128]),
                                 reads=[r_pv], writes=[r_v32])
                    k.dma("sp", [(VA_d[:, t * 4:(t + 1) * 4, :, :], va[:])], r_va, reads=[r_va], writes=[r_scr])
                    if not smp:
                        p0 = (t * TT - NS) // 128
                        k.dma("sp", [(vnew_a[i, :, p0:p0 + 4, :], v32[:])], r_v32, reads=[r_v32])
                    for g in range(4):
                        b = g % 2
                        proj_fm(wie, r_wie, 768 + g * 128, ht, r_h, pq[b][:], r_pq[b])
                        k.op("act", lambda e, b=b: e.activation(out=ut[b][:], in_=pq[b][:], func=AF.Copy),
                             reads=[r_pq[b]], writes=[r_ut[b]])
                        for sub in range(4):
                            k.op("pe", lambda e, b=b, sub=sub: e.matmul(
                                pv[:], lhsT=ut[b][:, sub * 128:(sub + 1) * 128], rhs=cstb[:, CB_CS:CB_CS + 2, :],
                                start=True, stop=True), reads=[r_ut[b], r_cst], writes=[r_pv])
                            eng = "dve" if sub % 2 == 0 else "act"
                            if eng == "dve":
                                k.op("dve", lambda e, g=g, sub=sub: e.tensor_copy(out=pqt[:, sub, g, :], in_=pv[:]),
                                     reads=[r_pv], writes=[r_pqt])
                            else:
                                k.op("act", lambda e, g=g, sub=sub: e.activation(out=pqt[:, sub, g, :], in_=pv[:], func=AF.Copy),
                                     reads=[r_pv], writes=[r_pqt])
                    k.dma("sp", [(PQ_d[:, t * 4:(t + 1) * 4, :, :], pqt[:])], r_pqt, reads=[r_pqt], writes=[r_scr])
                k.barrier()
                k.release([r_wie, r_x, r_rt, r_qo, r_va, r_v32, r_pqt] + qk.all_qn_res())

        def even_fnet(i):
            with ExitStack() as st:
                pf = [ps("pf%d" % b, [128, 256], F32, st) for b in range(2)]
                r_pf = [Res("pf%d" % b) for b in range(2)]
                fo = [sb("fo%d" % b, [128, 4, 256], BF16, st) for b in range(2)]
                r_fo = [Res("fo%d" % b) for b in range(2)]
                rel = list(r_fo)
                r_scr = Res("mixd")
                for (off, S, cond, smp) in cfg.seqs:
                    nst, nkt = S // 128, S // 256
                    tabd = tabS if smp else tabP
                    with ExitStack() as s2:
                        pqs = sb("pqs", [128, nst, 4, 256], BF16, s2)
                        r_pqs = Res("pqs")
                        k.dma("sp", [(pqs[:, a:min(a + 8, nst)], PQ_d[:, off // 128 + a:off // 128 + min(a + 8, nst)])
                                     for a in range(0, nst, 8)], r_pqs, writes=[r_pqs])
                        tab = [sb("tab%d" % b, [128, nst, 2, 256], BF16, s2) for b in range(2)]
                        r_tab = [Res("tab%d" % b) for b in range(2)]
                        u = 0
                        for kt in range(nkt):
                            tb = kt % 2
                            k.dma("sp", [(tab[tb][:], tabd[kt])], r_tab[tb], writes=[r_tab[tb]])
                            fb = kt % 2
                            for g in range(4):
                                b = u % 2
                                u += 1
                                n = 0
                                for s_ in range(nst):
                                    for cs in range(2):
                                        k.op("pe", lambda e, b=b, s_=s_, cs=cs, g=g, n=n, tb=tb: e.matmul(
                                            pf[b][:], lhsT=pqs[:, s_, g, cs * 128:(cs + 1) * 128], rhs=tab[tb][:, s_, cs, :],
                                            start=(n == 0), stop=(n == 2 * nst - 1)),
                                            reads=[r_pqs, r_tab[tb]], writes=[r_pf[b]])
                                        n += 1
                                if g % 2 == 0:
                                    k.op("dve", lambda e, b=b, g=g, fb=fb: e.tensor_copy(out=fo[fb][:, g, :], in_=pf[b][:]),
                                         reads=[r_pf[b]], writes=[r_fo[fb]])
                                else:
                                    k.op("act", lambda e, b=b, g=g, fb=fb: e.activation(out=fo[fb][:, g, :], in_=pf[b][:], func=AF.Copy),
                                         reads=[r_pf[b]], writes=[r_fo[fb]])
                            k.dma("sp", [(MIX_d[:, 4:8, off + kt * 256:off + (kt + 1) * 256], fo[fb][:])], r_fo[fb],
                                  reads=[r_fo[fb]], writes=[r_scr])
                        k.barrier()
                        k.release([r_pqs] + r_tab)
                k.release(rel)

        def even_attn(i):
            with ExitStack() as st:
                esk = sb("esk", [128, 8], F32, st)
                r_esk = Res("esk")
                k.dma("sp", [(esk[0:1, :], sink_e[i:i + 1, :]), (esk[64:65, :], sink_e[i:i + 1, :])], r_esk, writes=[r_esk])
                k.op("act", lambda e: e.activation(out=esk[0:1, :], in_=esk[0:1, :], func=AF.Exp), reads=[r_esk], writes=[r_esk])
                k.op("act", lambda e: e.activation(out=esk[64:65, :], in_=esk[64:65, :], func=AF.Exp), reads=[r_esk], writes=[r_esk])
                pS = [ps("pS%d" % b, [128, 4, 128], F32, st) for b in range(2)]
                pO = [ps("pO%d" % b, [128, 4, 128], F32, st) for b in range(2)]
                pB = [ps("pB%d" % b, [128, 4, 128], F32, st) for b in range(2)]
                r_pS = [Res("pS%d" % b) for b in range(2)]
                r_pO = [Res("pO%d" % b) for b in range(2)]
                r_pB = [Res("pB%d" % b) for b in range(2)]
                pT = [sb("pT%d" % b, [128, 4, 128], BF16, st) for b in range(3)]
                r_pT = [Res("pT%d" % b) for b in range(3)]
                dd = [sb("dd%d" % b, [128, 4, 128], F32, st) for b in range(2)]
                r_dd = [Res("dd%d" % b) for b in range(2)]
                ob = [sb("ob%d" % b, [128, 4, 128], F32, st) for b in range(2)]
                r_ob = [Res("ob%d" % b) for b in range(2)]
                ao = [sb("ao%d" % b, [128, 4, 512], BF16, st) for b in range(2)]
                r_ao = [Res("ao%d" % b) for b in range(2)]
                r_scr = Res("mixd")
                rel = [r_esk] + r_ao
                for (off, S, cond, smp) in cfg.seqs:
                    nb = S // 128
                    with ExitStack() as s2:
                        qs = sb("qs", [128, 4, S], BF16, s2)
                        ks = sb("ks", [128, S], BF16, s2)
                        vs = sb("vs", [128, nb, 2, 128], BF16, s2)
                        r_q = Res("qs")
                        k.dma("sp", [(qs[:], QT_d[:, :, off:off + S]), (ks[:], KT_d[:, 0, off:off + S]),
                                     (vs[:], VA_d[:, off // 128:off // 128 + nb])], r_q, writes=[r_q])
                        rr = [r_q]
                        if smp:
                            kcx = sb("kcx", [128, PAST], BF16, s2)
                            vcx = sb("vcx", [128, 2, 2, 128], BF16, s2)
                            r_cx = Res("cx")
                            k.dma("pool", [(kcx[:], kctx_e[i]), (vcx[:], vctx_e[i])], r_cx, writes=[r_cx])
                            rr.append(r_cx)
                        def kts_of(qb):
                            if smp:
                                kts = [("l", j, (CB_MP if j == qb - 1 else (CB_MN if j == qb + 1 else None)))
                                       for j in (qb - 1, qb, qb + 1) if 0 <= j < nb]
                                return kts + [("c", 0, None), ("c", 1, None)]
                            return [("l", j, None) for j in range(nb)]

                        steps = []
                        for qb in range(nb):
                            for g in range(2):
                                kts = kts_of(qb)
                                for n in range(len(kts)):
                                    steps.append((qb, g, n, kts[n], len(kts), len(steps) and 0))
                        unit_of = {}
                        for idx, (qb, g, n, kt, nk, _) in enumerate(steps):
                            unit_of[idx] = qb * 2 + g

                        def opnd(g, kt):
                            kind, j, msk = kt
                            rows = slice(g * 64, (g + 1) * 64)
                            if kind == "l":
                                return ks[rows, j * 128:(j + 1) * 128], vs[:, j, g, :], r_q
                            return kcx[rows, j * 128:(j + 1) * 128], vcx[:, j, g, :], r_cx

                        def emit_S(idx):
                            qb, g, n, kt, nk, _ = steps[idx]
                            kap_, vap_, rk = opnd(g, kt)
                            rows = slice(g * 64, (g + 1) * 64)
                            sbuf_i = idx % 2
                            k.op("pe", lambda e: e.matmul(pS[sbuf_i][:], lhsT=kap_, rhs=qs[rows, :, qb * 128:(qb + 1) * 128],
                                                          start=True, stop=True), reads=[rk, r_q], writes=[r_pS[sbuf_i]])

                        def emit_rest(idx):
                            qb, g, n, kt, nk, _ = steps[idx]
                            kap_, vap_, rk = opnd(g, kt)
                            msk = kt[2]
                            rows = slice(g * 64, (g + 1) * 64)
                            sbuf_i = idx % 2
                            tbuf = idx % 3
                            ub = unit_of[idx] % 2
                            ab = (qb // 4) % 2
                            k.op("act", lambda e: e.activation(out=pT[tbuf][:], in_=pS[sbuf_i][:], func=AF.Exp, scale=0.125),
                                 reads=[r_pS[sbuf_i]], writes=[r_pT[tbuf]])
                            if msk is not None:
                                k.op("pool", lambda e: e.tensor_tensor(out=pT[tbuf][:], in0=pT[tbuf][:], in1=cstb[:, msk:msk + 4, :], op=ALU.mult),
                                     reads=[r_pT[tbuf], r_cst], writes=[r_pT[tbuf]])
                            k.op("pe", lambda e: e.matmul(pO[ub][:], lhsT=vap_, rhs=pT[tbuf][:], start=(n == 0), stop=(n == nk - 1)),
                                 reads=[rk, r_pT[tbuf]], writes=[r_pO[ub]])
                            if n != nk - 1:
                                return
                            row = 64 if g == 0 else 0
                            k.op("dve", lambda e: e.tensor_tensor(
                                out=dd[ub][row:row + 1], in0=pO[ub][row:row + 1],
                                in1=esk[row:row + 1, g * 4:(g + 1) * 4].unsqueeze(2).to_broadcast([1, 4, 128]),
                                op=ALU.add), reads=[r_pO[ub], r_esk], writes=[r_dd[ub]])
                            k.op("dve", lambda e: e.reciprocal(out=dd[ub][row:row + 1], in_=dd[ub][row:row + 1]),
                                 reads=[r_dd[ub]], writes=[r_dd[ub]])
                            k.op("pe", lambda e: e.matmul(pB[ub][:], lhsT=cstf[row:row + 1, ONESF, :], rhs=dd[ub][row:row + 1], start=True, stop=True),
                                 reads=[r_dd[ub], r_cst], writes=[r_pB[ub]])
                            k.op("act", lambda e: e.activation(out=ob[ub][rows], in_=pO[ub][rows], func=AF.Copy),
                                 reads=[r_pO[ub]], writes=[r_ob[ub]])
                            k.op("dve", lambda e: e.tensor_tensor(
                                out=ao[ab][rows, :, (qb % 4) * 128:(qb % 4 + 1) * 128], in0=ob[ub][rows], in1=pB[ub][rows],
                                op=ALU.mult), reads=[r_ob[ub], r_pB[ub]], writes=[r_ao[ab]])
                            if g == 1 and (qb % 4 == 3 or qb == nb - 1):
                                q0 = (qb // 4) * 512
                                wdt = (qb % 4 + 1) * 128
                                k.dma("sp", [(MIX_d[:, 0:4, off + q0:off + q0 + wdt], ao[ab][:, :, 0:wdt])], r_ao[ab],
                                      reads=[r_ao[ab]], writes=[r_scr])

                        emit_S(0)
                        for idx in range(len(steps)):
                            if idx + 1 < len(steps):
                                emit_S(idx + 1)
                            emit_rest(idx)
                        k.barrier()
                        k.release(rr)
                k.release(rel)

        def out_proj(l, wsrc):
            with ExitStack() as st:
                wom, r_wom = load_w_bf(st, "wom", wsrc, D)
                xt = sb("xt", [128, NCH, TT], F32, st)
                r_x = Res("xt")
                mx = sb("mx", [128, NCH, TT], BF16, st)
                r_mx = Res("mx")
                py = [ps("py%d" % b, [128, TT], F32, st) for b in range(2)]
                r_py = [Res("py%d" % b) for b in range(2)]
                r_yT = Res("yT")
                for t in range(NT):
                    cond = 1 if is_sample_tile(t) else 0
                    tsl = slice(t * TT, (t + 1) * TT)
                    k.dma("sp", [(xt[:, c, :], yT[c, :, tsl]) for c in range(NCH)], r_x, reads=[r_yT], writes=[r_x])
                    k.dma("sp", [(mx[:], MIX_d[:, :, tsl])], r_mx, writes=[r_mx])
                    for dc in range(NCH):
                        b = dc % 2
                        for kc in range(NCH):
                            k.op("pe", lambda e, kc=kc, dc=dc, b=b: e.matmul(
                                py[b][:], lhsT=wom[:, kc, dc * 128:(dc + 1) * 128], rhs=mx[:, kc, :],
                                start=(kc == 0), stop=(kc == NCH - 1)), reads=[r_wom, r_mx], writes=[r_py[b]])
                        k.op("dve", lambda e, dc=dc, b=b: e.scalar_tensor_tensor(
                            out=xt[:, dc, :], in0=py[b][:], scalar=GM[:, l, 1, dc, cond:cond + 1], in1=xt[:, dc, :],
                            op0=ALU.mult, op1=ALU.add), reads=[r_py[b], r_am, r_x], writes=[r_x])
                    k.dma("sp", [(yT[c, :, tsl], xt[:, c, :]) for c in range(NCH)], r_x, reads=[r_x], writes=[r_yT])
                k.barrier()
                k.release([r_wom, r_x, r_mx])

        DKS = 128 ** -0.5

        def odd_proj(l, i):
            with ExitStack() as st:
                wio, r_wio = load_w_bf(st, "wio", w_in_o[i], 3600)
                xt = sb("xt", [128, NCH, TT], F32, st)
                r_x = Res("xt")
                ht = sb("ht", [128, NCH, TT], BF16, st)
                r_h = Res("ht")
                tmp = alloc_norm_tmp(st)
                qk = QKN(st)
                pq = [ps("pq%d" % b, [128, TT], F32, st) for b in range(2)]
                r_pq = [Res("pq%d" % b) for b in range(2)]
                pvv = ps("pvv", [128, 512], F32, st)
                r_pvv = Res("pvv")
                pg4 = tmp[2][0:4, :]
                r_pg4 = tmp[3]
                rt = sb("rt", [128, 2, TT], F32, st)
                r_rt = Res("rt")
                qo = sb("qo", [128, 8, TT], BF16, st)
                r_qo = Res("qo")
                vt = sb("vt", [128, 4, 512], BF16, st)
                r_vt = Res("vt")
                v32 = sb("v32", [128, 4, 512], F32, st)
                r_v32 = Res("v32")
                fo = sb("fo", [128, 12, TT], BF16, st)
                r_fo = Res("fo")
                kt_ = sb("kt_", [128, 4, 4, 128], BF16, st)
                r_kt = Res("kt_")
                v1 = sb("v1", [128, 4, 4, 160], BF16, st)
                r_v1 = Res("v1")
                gt = sb("gt", [4, 4, TT], F32, st)
                r_gt = Res("gt")
                k.op("pool", lambda e: e.memset(v1[:], 1.0), writes=[r_v1])
                r_scr = Res("scr_o")
                r_yT = Res("yT")
                for t in range(NT):
                    smp = is_sample_tile(t)
                    cond = 1 if smp else 0
                    tsl = slice(t * TT, (t + 1) * TT)
                    k.dma("sp", [(xt[:, c, :], yT[c, :, tsl]) for c in range(NCH)], r_x, reads=[r_yT], writes=[r_x])
                    if smp:
                        k.dma("sp", [(rt[:], ropeT[:, :, tsl])], r_rt, writes=[r_rt])
                    norm_mod(st, r_x, xt, ht, r_h, l, 1, cond, tmp)
                    sub_ = getattr(cfg, "odd_sub", 31)
                    for blk in (range(8) if sub_ & 1 else []):
                        b = blk % 2
                        proj_fm(wio, r_wio, blk * 128, ht, r_h, pq[b][:], r_pq[b])
                        gain = paro[:, i, (0 if blk < 4 else 1):(1 if blk < 4 else 2)]
                        qk.run(pq[b][:], r_pq[b], gain, rt if smp else None, qo[:, blk, :], r_qo, r_rt)
                        if blk >= 4 and not smp:
                            p0 = t * TT - NS
                            k.dma("sp", [(knew_c[i, :, blk - 4, p0:p0 + TT], qk.last["qn"][:])], qk.last["r"]["qn"], reads=[qk.last["r"]["qn"]])
                    k.dma("sp", [(QT_d[:, :, tsl], qo[:, 0:4, :]), (KT_d[:, :, tsl], qo[:, 4:8, :])], r_qo,
                          reads=[r_qo], writes=[r_scr])
                    dbg_ = getattr(cfg, "odd_dbg", 31)
                    for sub in (range(4) if sub_ & 2 else []):
                        if dbg_ & 1:
                            proj_tm(wio, r_wio, 1024, 512, ht, r_h, sub, pvv[:], r_pvv)
                        if dbg_ & 2:
                            k.op("act", lambda e, sub=sub: e.activation(out=vt[:, sub, :], in_=pvv[:], func=AF.Copy),
                                 reads=[r_pvv], writes=[r_vt])
                        if not smp and (dbg_ & 4):
                            k.op("dve", lambda e, sub=sub: e.tensor_copy(out=v32[:, sub, :], in_=pvv[:]),
                                 reads=[r_pvv], writes=[r_v32])
                    if dbg_ & 8:
                        k.dma("sp", [(VCt_d[:, t * 4:(t + 1) * 4, :], vt[:])], r_vt, reads=[r_vt], writes=[r_scr])
                    if not smp and (dbg_ & 16):
                        p0 = (t * TT - NS) // 128
                        k.dma("sp", [(vnew_c[i, :, p0:p0 + 4, :], v32[:])], r_v32, reads=[r_v32])
                    for blk in (range(12) if sub_ & 4 else []):
                        b = blk % 2
                        col0 = (1536 + blk * 128) if blk < 8 else (3072 + (blk - 8) * 128)
                        proj_fm(wio, r_wio, col0, ht, r_h, pq[b][:], r_pq[b])
                        if blk < 4:
                            k.op("dve", lambda e, b=b, blk=blk: e.tensor_copy(out=fo[:, blk, :], in_=pq[b][:]),
                                 reads=[r_pq[b]], writes=[r_fo])
                        elif blk < 8:
                            k.op("act", lambda e, b=b, blk=blk: e.activation(out=fo[:, blk, :], in_=pq[b][:], func=AF.Identity, scale=DKS),
                                 reads=[r_pq[b]], writes=[r_fo])
                        else:
                            k.op("act", lambda e, b=b, blk=blk: e.activation(out=fo[:, blk, :], in_=pq[b][:], func=AF.Sigmoid),
                                 reads=[r_pq[b]], writes=[r_fo])
                    k.dma("sp", [(QDT_d[h, :, tsl], fo[:, h, :]) for h in range(4)]
                          + [(KDT_d[h, :, tsl], fo[:, 4 + h, :]) for h in range(4)]
                          + [(SODT_d[h, :, tsl], fo[:, 8 + h, :]) for h in range(4)], r_fo, reads=[r_fo], writes=[r_scr])
                    for sub in (range(4) if sub_ & 8 else []):
                        proj_tm(wio, r_wio, 2048, 512, ht, r_h, sub, pvv[:], r_pvv)
                        k.op("act", lambda e, sub=sub: e.activation(out=kt_[:, sub, :, :], in_=pvv[:].rearrange("p (h d) -> p h d", h=4),
                                                                    func=AF.Identity, scale=DKS), reads=[r_pvv], writes=[r_kt])
                        proj_tm(wio, r_wio, 2560, 512, ht, r_h, sub, pvv[:], r_pvv)
                        k.op("dve", lambda e, sub=sub: e.tensor_copy(out=v1[:, sub, :, 0:128], in_=pvv[:].rearrange("p (h d) -> p h d", h=4)),
                             reads=[r_pvv], writes=[r_v1])
                    k.dma("sp", [(KDt_d[h, :, t * 4:(t + 1) * 4, :], kt_[:, :, h, :]) for h in range(4)], r_kt,
                          reads=[r_kt], writes=[r_scr])
                    k.dma("sp", [(VD1t_d[h, :, t * 4:(t + 1) * 4, :], v1[:, :, h, :]) for h in range(4)], r_v1,
                          reads=[r_v1], writes=[r_scr])
                    for grp in (range(4) if sub_ & 16 else []):
                        proj_fm(wio, r_wio, 3584 + grp * 4, ht, r_h, pg4, r_pg4, m=4)
                        k.op("act", lambda e, grp=grp: e.activation(out=gt[:, grp, :], in_=pg4, func=AF.Identity,
                                                                    bias=bgo[:, i, grp:grp + 1], scale=1.0),
                             reads=[r_pg4, r_cst], writes=[r_gt])
                    k.dma("sp", [(G_d[grp, :, tsl], gt[:, grp, :]) for grp in range(4)], r_gt, reads=[r_gt], writes=[r_scr])
                k.barrier()
                k.release([r_wio, r_x, r_rt, r_qo, r_vt, r_v32, r_fo, r_kt, r_v1, r_gt] + qk.all_qn_res())

        def odd_attn(l, i):
            lam_init = 0.8 - 0.6 * math.exp(-0.3 * l)
            with ExitStack() as st:
                lv = sb("lv", [1, 2, 2, 64], F32, st)
                pr = sb("pr", [1, 2, 64], F32, st)
                s2_ = sb("s2_", [1, 16], F32, st)
                nlb = sb("nlb", [128, 1], F32, st)
                r_lv = Res("lv")
                k.dma("sp", [(lv[:], lam_o[i:i + 1, :].rearrange("o (a b d) -> o a b d", a=2, b=2))], r_lv, writes=[r_lv])
                k.op("dve", lambda e: e.tensor_tensor(out=pr[:], in0=lv[:, :, 0, :], in1=lv[:, :, 1, :], op=ALU.mult),
                     reads=[r_lv], writes=[r_lv])
                k.op("dve", lambda e: e.reduce_sum(out=s2_[:, 0:2], in_=pr[:], axis=AX.X), reads=[r_lv], writes=[r_lv])
                k.op("act", lambda e: e.activation(out=s2_[:, 0:2], in_=s2_[:, 0:2], func=AF.Exp), reads=[r_lv], writes=[r_lv])
                k.op("dve", lambda e: e.tensor_tensor(out=s2_[:, 2:3], in0=s2_[:, 1:2], in1=s2_[:, 0:1], op=ALU.subtract),
                     reads=[r_lv], writes=[r_lv])
                k.op("dve", lambda e: e.tensor_scalar(out=s2_[:, 8:9], in0=s2_[:, 2:3], scalar1=-lam_init, scalar2=None, op0=ALU.add),
                     reads=[r_lv], writes=[r_lv])
                pS = [ps("pS%d" % b, [128, 512], F32, st) for b in range(2)]
                pO = [ps("pO%d" % b, [128, 512], F32, st) for b in range(2)]
                pD = [ps("pD%d" % b, [128, 512], F32, st) for b in range(2)]
                pms = ps("pms", [128, 512], F32, st)
                r_pS = [Res("pS%d" % b) for b in range(2)]
                r_pO = [Res("pO%d" % b) for b in range(2)]
                r_pD = [Res("pD%d" % b) for b in range(2)]
                r_pms = Res("pms")
                k.op("pe", lambda e: e.matmul(pms[:, 0:1], lhsT=cstf[0:1, ONESF, :], rhs=s2_[:, 8:9], start=True, stop=True),
                     reads=[r_lv, r_cst], writes=[r_pms])
                r_nlb = Res("nlb")
                k.op("dve", lambda e: e.tensor_copy(out=nlb[:], in_=pms[:, 0:1]), reads=[r_pms], writes=[r_nlb])
                E = [sb("E%d" % b, [128, 512], BF16, st) for b in range(3)]
                r_E = [Res("E%d" % b) for b in range(3)]
                rd = [sb("rd%d" % b, [128, 512], F32, st) for b in range(2)]
                om = [sb("om%d" % b, [128, 512], F32, st) for b in range(2)]
                r_rd = [Res("rd%d" % b) for b in range(2)]
                r_om = [Res("om%d" % b) for b in range(2)]
                aa = sb("aa", [128, 512], F32, st)
                sq = sb("sqa", [128, 512], F32, st)
                rq = sb("rqa", [128, 512], F32, st)
                r_aa, r_sq, r_rq = Res("aa"), Res("sqa"), Res("rqa")
                ao = [sb("ao%d" % b, [128, 4, 512], BF16, st) for b in range(2)]
                r_ao = [Res("ao%d" % b) for b in range(2)]
                r_scr = Res("mixd")
                rel = [r_lv] + r_ao
                for (off, S, cond, smp) in cfg.seqs:
                    nb = S // 128
                    QW = min(512, S)
                    with ExitStack() as s2:
                        qs = sb("qs", [128, 4, S], BF16, s2)
                        ks = sb("ks", [128, 4, S], BF16, s2)
                        vs = sb("vs", [128, nb, 512], BF16, s2)
                        r_q = Res("qs")
                        k.dma("sp", [(qs[:], QT_d[:, :, off:off + S]), (ks[:], KT_d[:, :, off:off + S]),
                                     (vs[:], VCt_d[:, off // 128:off // 128 + nb, :])], r_q, writes=[r_q])
                        rr = [r_q]
                        if smp:
                            kcx = sb("kcx", [128, 4, PAST], BF16, s2)
                            vcx = sb("vcx", [128, 2, 512], BF16, s2)
                            r_cx = Res("cx")
                            k.dma("pool", [(kcx[:], kctx_o[i]), (vcx[:], vctx_o[i])], r_cx, writes=[r_cx])
                            rr.append(r_cx)
                        kts = [("l", j) for j in range(nb)] + ([("c", 0), ("c", 1)] if smp else [])
                        NK = len(kts)
                        steps = [(qt, h, m, n) for qt in range(S // QW) for h in range(4) for m in range(2) for n in range(NK)]

                        def opnd(h, m, n):
                            kind, j = kts[n]
                            rows = slice(m * 64, (m + 1) * 64)
                            if kind == "l":
                                return ks[rows, h, j * 128:(j + 1) * 128], vs[:, j, h * 128:(h + 1) * 128], r_q
                            return kcx[rows, h, j * 128:(j + 1) * 128], vcx[:, j, h * 128:(h + 1) * 128], r_cx

                        def emit_S(idx):
                            qt, h, m, n = steps[idx]
                            kap_, vap_, rk = opnd(h, m, n)
                            rows = slice(m * 64, (m + 1) * 64)
                            sb_i = idx % 2
                            qsl = slice(qt * QW, (qt + 1) * QW)
                            k.op("pe", lambda e: e.matmul(pS[sb_i][:, 0:QW], lhsT=kap_, rhs=qs[rows, h, qsl], start=True, stop=True),
                                 reads=[rk, r_q], writes=[r_pS[sb_i]])

                        def emit_rest(idx):
                            qt, h, m, n = steps[idx]
                            kap_, vap_, rk = opnd(h, m, n)
                            sb_i = idx % 2
                            eb = idx % 3
                            ab = qt % 2
                            k.op("act", lambda e: e.activation(out=E[eb][:, 0:QW], in_=pS[sb_i][:, 0:QW], func=AF.Exp, scale=0.125),
                                 reads=[r_pS[sb_i]], writes=[r_E[eb]])
                            k.op("pe", lambda e: e.matmul(pO[m][:, 0:QW], lhsT=vap_, rhs=E[eb][:, 0:QW], start=(n == 0), stop=(n == NK - 1)),
                                 reads=[rk, r_E[eb]], writes=[r_pO[m]])
                            k.op("pe", lambda e: e.matmul(pD[m][:, 0:QW], lhsT=cstb[:, CB_ONE, :], rhs=E[eb][:, 0:QW], start=(n == 0), stop=(n == NK - 1)),
                                 reads=[r_cst, r_E[eb]], writes=[r_pD[m]])
                            if n != NK - 1:
                                return
                            k.op("dve", lambda e: e.reciprocal(out=rd[m][:, 0:QW], in_=pD[m][:, 0:QW]),
                                 reads=[r_pD[m]], writes=[r_rd[m]])
                            k.op("dve", lambda e: e.tensor_tensor(out=om[m][:, 0:QW], in0=pO[m][:, 0:QW], in1=rd[m][:, 0:QW], op=ALU.mult),
                                 reads=[r_pO[m], r_rd[m]], writes=[r_om[m]])
                            if m != 1:
                                return
                            k.op("dve", lambda e: e.scalar_tensor_tensor(out=aa[:, 0:QW], in0=om[1][:, 0:QW], scalar=nlb[:, 0:1],
                                                                         in1=om[0][:, 0:QW], op0=ALU.mult, op1=ALU.add),
                                 reads=[r_om[0], r_om[1], r_nlb], writes=[r_aa])
                            k.op("act", lambda e: e.activation(out=sq[:, 0:QW], in_=aa[:, 0:QW], func=AF.Square),
                                 reads=[r_aa], writes=[r_sq])
                            k.op("pe", lambda e: e.matmul(pms[:, 0:QW], lhsT=cstf[:, F128, :], rhs=sq[:, 0:QW], start=True, stop=True),
                                 reads=[r_sq, r_cst], writes=[r_pms])
                            k.op("act", lambda e: e.activation(out=rq[:, 0:QW], in_=pms[:, 0:QW], func=AF.Sqrt, bias=eps_t[:], scale=1.0),
                                 reads=[r_pms, r_ones], writes=[r_rq])
                            k.op("dve", lambda e: e.reciprocal(out=rq[:, 0:QW], in_=rq[:, 0:QW]), reads=[r_rq], writes=[r_rq])
                            k.op("dve", lambda e: e.scalar_tensor_tensor(out=aa[:, 0:QW], in0=aa[:, 0:QW], scalar=paro[:, i, 2:3],
                                                                         in1=rq[:, 0:QW], op0=ALU.mult, op1=ALU.mult),
                                 reads=[r_aa, r_rq, r_cst], writes=[r_aa])
                            k.op("act", lambda e: e.activation(out=ao[ab][:, h, 0:QW], in_=aa[:, 0:QW], func=AF.Identity,
                                                               scale=(1.0 - lam_init)),
                                 reads=[r_aa], writes=[r_ao[ab]])
                            if h == 3:
                                k.dma("sp", [(MIX_d[:, 0:4, off + qt * QW:off + (qt + 1) * QW], ao[ab][:, :, 0:QW])], r_ao[ab],
                                      reads=[r_ao[ab]], writes=[r_scr])

                        emit_S(0)
                        for idx in range(len(steps)):
                            if idx + 1 < len(steps):
                                emit_S(idx + 1)
                            emit_rest(idx)
                        k.barrier()
                        k.release(rr)
                k.release(rel)

        def odd_mlstm(l, i):
            nch = T // 128
            with ExitStack() as st:
                Bt = [sb("Bt%d" % d_, [4, T], BF16, st) for d_ in range(2)]
                CLt = [sb("CLt%d" % d_, [4, T], BF16, st) for d_ in range(2)]
                WI = [sb("WI%d" % d_, [4, nch], F32, st) for d_ in range(2)]
                mo = sb("mo", [4, 2, 2], F32, st)
                r_bk = Res("bk")
                r_mo = Res("mo")
                with ExitStack() as s1:
                    A1 = sb("A1", [4, T], F32, s1)
                    A2 = sb("A2", [4, T], F32, s1)
                    A3 = sb("A3", [4, T], F32, s1)
                    seg = sb("seg", [4, T], F32, s1)
                    bendN = sb("bendN", [4, nch], F32, s1)
                    kap = sb("kap", [4, nch], F32, s1)
                    Mr = sb("Mr", [4, nch], F32, s1)
                    WIe = sb("WIe", [4, nch], F32, s1)
                    marr = sb("marr", [4, nch], F32, s1)
                    zer = sb("zer", [4, 1], F32, s1)
                    r_a = Res("A")
                    r_seg = Res("seg")
                    k.op("pool", lambda e: e.memset(seg[:], 1.0), writes=[r_seg])
                    k.op("pool", lambda e: e.memset(seg[:].rearrange("p (c t) -> p c t", t=128)[:, :, 0:1], 0.0), writes=[r_seg])
                    k.op("pool", lambda e: e.memset(zer[:], 0.0), writes=[r_seg])
                    A2v = A2[:].rearrange("p (c t) -> p c t", t=128)
                    A3v = A3[:].rearrange("p (c t) -> p c t", t=128)
                    for d_ in range(2):
                        k.dma("sp", [(A1[:], G_d[d_ * 2 + 1]), (A2[:], G_d[d_ * 2])], r_a, writes=[r_a])
                        k.op("act", lambda e: e.activation(out=A1[:], in_=A1[:], func=AF.Exp, scale=-1.0), reads=[r_a], writes=[r_a])
                        k.op("act", lambda e: e.activation(out=A1[:], in_=A1[:], func=AF.Ln, bias=one_t[0:4, :], scale=1.0), reads=[r_a], writes=[r_a])
                        k.op("dve", lambda e: e.tensor_tensor_scan(out=A3[:], data0=seg[:], data1=A1[:], initial=0.0,
                                                                  op0=ALU.mult, op1=ALU.add), reads=[r_a, r_seg], writes=[r_a])
                        k.op("dve", lambda e: e.tensor_copy(out=bendN[:], in_=A3v[:, :, 127]), reads=[r_a], writes=[r_a])
                        if d_ == 1:
                            k.op("dve", lambda e: e.tensor_tensor(out=A3v, in0=bendN[:].unsqueeze(2).to_broadcast([4, nch, 128]),
                                                                  in1=A3v, op=ALU.subtract), reads=[r_a], writes=[r_a])
                            k.op("dve", lambda e: e.tensor_tensor(out=A3[:], in0=A3[:], in1=A1[:], op=ALU.add), reads=[r_a], writes=[r_a])
                        k.op("dve", lambda e: e.tensor_tensor(out=A2[:], in0=A2[:], in1=A3[:], op=ALU.add), reads=[r_a], writes=[r_a])
                        k.op("dve", lambda e: e.tensor_reduce(out=kap[:], in_=A2v, axis=AX.X, op=ALU.max), reads=[r_a], writes=[r_a])
                        for si, (off, S, cond, smp) in enumerate(cfg.seqs):
                            c0, c1 = off // 128, (off + S) // 128
                            order = list(range(c0, c1)) if d_ == 0 else list(range(c1 - 1, c0 - 1, -1))
                            mcur = m0t[:, i, d_:d_ + 1] if smp else zer[:]
                            for j in order:
                                k.op("dve", lambda e, j=j, mcur=mcur: e.tensor_tensor(out=Mr[:, j:j + 1], in0=mcur, in1=kap[:, j:j + 1], op=ALU.max),
                                     reads=[r_a, r_cst, r_seg], writes=[r_a])
                                k.op("dve", lambda e, j=j, mcur=mcur: e.tensor_tensor(out=WIe[:, j:j + 1], in0=mcur, in1=Mr[:, j:j + 1], op=ALU.subtract),
                                     reads=[r_a, r_cst, r_seg], writes=[r_a])
                                k.op("dve", lambda e, j=j: e.tensor_tensor(out=marr[:, j:j + 1], in0=Mr[:, j:j + 1], in1=bendN[:, j:j + 1], op=ALU.subtract),
                                     reads=[r_a], writes=[r_a])
                                mcur = marr[:, j:j + 1]
                            if not smp:
                                k.op("dve", lambda e, si=si, mcur=mcur, d_=d_: e.tensor_copy(out=mo[:, si - 1, d_:d_ + 1], in_=mcur),
                                     reads=[r_a], writes=[r_mo])
                        Mb = Mr[:].unsqueeze(2).to_broadcast([4, nch, 128])
                        k.op("dve", lambda e: e.tensor_tensor(out=A2v, in0=A2v, in1=Mb, op=ALU.subtract), reads=[r_a], writes=[r_a])
                        k.op("act", lambda e, d_=d_: e.activation(out=Bt[d_][:], in_=A2[:], func=AF.Exp), reads=[r_a], writes=[r_bk])
                        k.op("dve", lambda e: e.tensor_tensor(out=A3v, in0=A3v, in1=Mb, op=ALU.subtract), reads=[r_a], writes=[r_a])
                        k.op("act", lambda e, d_=d_: e.activation(out=CLt[d_][:], in_=A3[:], func=AF.Exp), reads=[r_a], writes=[r_bk])
                        k.op("act", lambda e, d_=d_: e.activation(out=WI[d_][:], in_=WIe[:], func=AF.Exp), reads=[r_a], writes=[r_bk])
                    k.dma("sp", [(mnew[:, i, :, :], mo[:])], r_mo, reads=[r_mo])
                    k.barrier()
                    k.release([r_a])
                pb = [ps("pb%d" % b, [128, 512], F32, st) for b in range(2)]
                r_pb = [Res("pb%d" % b) for b in range(2)]
                pmisc = ps("pmisc", [128, 2, nch], F32, st)
                r_pmisc = Res("pmisc")
                PA = [ps("PA%d" % b, [128, 3, 128], F32, st) for b in range(2)]
                r_PA = [[Res("PA%d_%d" % (b, c_)) for c_ in range(3)] for b in range(2)]
                r_PAL = [Res("PAlock%d" % b, lock=True) for b in range(2)]
                PC = [ps("PC%d" % b, [128, 129], F32, st) for b in range(2)]
                r_PC = [Res("PC%d" % b) for b in range(2)]
                r_scr = Res("mixd")
                for h in range(4):
                    with ExitStack() as sh:
                        qd = sb("qd", [128, T], BF16, sh)
                        kdT = sb("kdT", [128, T], BF16, sh)
                        kdt = sb("kdt", [128, nch, 128], BF16, sh)
                        vd1 = sb("vd1", [128, nch, 160], BF16, sh)
                        r_ld = Res("ld")
                        k.dma("sp", [(qd[:], QDT_d[h]), (kdT[:], KDT_d[h]), (kdt[:], KDt_d[h]), (vd1[:], VD1t_d[h])], r_ld, writes=[r_ld])
                        Hs = sb("Hs", [128, T], F32, sh)
                        r_Hs = [Res("Hs%d" % j) for j in range(nch)]
                        KpT = sb("KpT", [128, T], BF16, sh)
                        CLB = sb("CLB", [128, T], BF16, sh)
                        Kpt = sb("Kpt", [128, nch, 128], BF16, sh)
                        bcol = sb("bcol", [128, nch], F32, sh)
                        wib = sb("wib", [128, nch], F32, sh)
                        r_pre = Res("pre")
                        cst_ = [sb("cst%d" % b, [128, 129], F32, sh) for b in range(2)]
                        cs = [sb("cs%d" % b, [128, 128], BF16, sh) for b in range(2)]
                        nsb = [sb("nsb%d" % b, [128, 128], BF16, sh) for b in range(2)]
                        r_c = [Res("c%d" % b) for b in range(2)]
                        r_cs = [Res("cs%d" % b) for b in range(2)]
                        scm = [sb("scm%d" % b, [128, 128], BF16, sh) for b in range(2)]
                        r_scm = [Res("scm%d" % b) for b in range(2)]
                        dcl = [sb("dcl%d" % b, [128, 128], F32, sh) for b in range(2)]
                        r_dcl = [Res("dcl%d" % b) for b in range(2)]
                        htmp = [sb("htmp%d" % b, [128, 128], F32, sh) for b in range(2)]
                        r_htmp = [Res("htmp%d" % b) for b in range(2)]
                        u = 0
                        for d_ in range(2):
                            for t in range(NT):
                                tsl = slice(t * TT, (t + 1) * TT)
                                k.op("pe", lambda e, tsl=tsl, d_=d_: e.matmul(pb[0][:], lhsT=c4b[:, h, :], rhs=Bt[d_][:, tsl], start=True, stop=True),
                                     reads=[r_bk, r_cst], writes=[r_pb[0]])
                                k.op("dve", lambda e, tsl=tsl: e.tensor_tensor(out=KpT[:, tsl], in0=kdT[:, tsl], in1=pb[0][:], op=ALU.mult),
                                     reads=[r_pb[0], r_ld], writes=[r_pre])
                                k.op("pe", lambda e, tsl=tsl, d_=d_: e.matmul(pb[1][:], lhsT=c4b[:, h, :], rhs=CLt[d_][:, tsl], start=True, stop=True),
                                     reads=[r_bk, r_cst], writes=[r_pb[1]])
                                k.op("act", lambda e, tsl=tsl: e.activation(out=CLB[:, tsl], in_=pb[1][:], func=AF.Copy),
                                     reads=[r_pb[1]], writes=[r_pre])
                            for j in range(nch):
                                k.op("pe", lambda e, j=j, d_=d_: e.matmul(pmisc[:, 0, j:j + 1], lhsT=Bt[d_][:, j * 128:(j + 1) * 128],
                                                                          rhs=c4b[:, 4, h * 32:h * 32 + 1], start=True, stop=True),
                                     reads=[r_bk, r_cst], writes=[r_pmisc])
                            k.op("pe", lambda e, d_=d_: e.matmul(pmisc[:, 1, :], lhsT=c4f[:, h, :], rhs=WI[d_][:], start=True, stop=True),
                                 reads=[r_bk, r_cst], writes=[r_pmisc])
                            k.op("dve", lambda e: e.tensor_copy(out=bcol[:], in_=pmisc[:, 0, :]), reads=[r_pmisc], writes=[r_pre])
                            k.op("dve", lambda e: e.tensor_copy(out=wib[:], in_=pmisc[:, 1, :]), reads=[r_pmisc], writes=[r_pre])
                            k.op("pool", lambda e: e.tensor_tensor(out=Kpt[:], in0=kdt[:], in1=bcol[:].unsqueeze(2).to_broadcast([128, nch, 128]),
                                                                   op=ALU.mult), reads=[r_pre, r_ld], writes=[r_pre])
                            mT = CB_MF if d_ == 0 else CB_MB
                            for si, (off, S, cond, smp) in enumerate(cfg.seqs):
                                c0, c1 = off // 128, (off + S) // 128
                                order = list(range(c0, c1)) if d_ == 0 else list(range(c1 - 1, c0 - 1, -1))
                                cur = 0
                                if smp:
                                    k.dma("sp", [(cst_[cur][:, 0:128], c0_o[:, i, d_, h, :]), (cst_[cur][:, 128:129], n0_o[i, d_, h])],
                                          r_c[cur], writes=[r_c[cur]])
                                else:
                                    k.op("pool", lambda e, cur=cur: e.memset(cst_[cur][:], 0.0), writes=[r_c[cur]])

                                def scaled_state(cur, j):
                                    k.op("act", lambda e: e.activation(out=cs[cur][:], in_=cst_[cur][:, 0:128], func=AF.Identity,
                                                                       scale=wib[:, j:j + 1]), reads=[r_c[cur], r_pre], writes=[r_cs[cur]])
                                    k.op("dve", lambda e: e.tensor_scalar(out=nsb[cur][:], in0=cst_[cur][:, 128:129].to_broadcast([128, 128]),
                                                                          scalar1=wib[:, j:j + 1], scalar2=None, op0=ALU.mult),
                                         reads=[r_c[cur], r_pre], writes=[r_cs[cur]])
                                scaled_state(cur, order[0])
                                for oi, j in enumerate(order):
                                    ub = u % 2
                                    u += 1
                                    P = PA[ub]
                                    rP = r_PA[ub]
                                    rL = r_PAL[ub]
                                    csl = slice(j * 128, (j + 1) * 128)
                                    k.op("pe", lambda e, P=P, csl=csl: e.matmul(P[:, 0, :], lhsT=KpT[:, csl], rhs=qd[:, csl], start=True, stop=True),
                                         reads=[r_pre, r_ld], writes=[rP[0], rL])
                                    k.op("dve", lambda e, P=P, ub=ub, mT=mT: e.tensor_tensor(out=scm[ub][:], in0=P[:, 0, :], in1=cstb[:, mT, :], op=ALU.mult),
                                         reads=[rP[0], r_cst], writes=[r_scm[ub], rL])
                                    k.op("pe", lambda e, P=P, ub=ub, j=j: e.matmul(P[:, 1, :], lhsT=vd1[:, j, 0:128], rhs=scm[ub][:], start=True, stop=False),
                                         reads=[r_ld, r_scm[ub]], writes=[rP[1], rL])
                                    k.op("pe", lambda e, P=P, cur=cur, csl=csl: e.matmul(P[:, 1, :], lhsT=cs[cur][:], rhs=qd[:, csl], start=False, stop=True),
                                         reads=[r_cs[cur], r_ld], writes=[rP[1], rL])
                                    k.op("pe", lambda e, P=P, ub=ub: e.matmul(P[:, 2, :], lhsT=cstb[:, CB_ONE, :], rhs=scm[ub][:], start=True, stop=False),
                                         reads=[r_cst, r_scm[ub]], writes=[rP[2], rL])
                                    k.op("pe", lambda e, P=P, cur=cur, csl=csl: e.matmul(P[:, 2, :], lhsT=nsb[cur][:], rhs=qd[:, csl], start=False, stop=True),
                                         reads=[r_cs[cur], r_ld], writes=[rP[2], rL])
                                    k.op("act", lambda e, P=P, ub=ub: e.activation(out=dcl[ub][:], in_=P[:, 2, :], func=AF.Abs),
                                         reads=[rP[2]], writes=[r_dcl[ub], rL])
                                    k.op("dve", lambda e, ub=ub, csl=csl: e.tensor_tensor(out=dcl[ub][:], in0=dcl[ub][:], in1=CLB[:, csl], op=ALU.max),
                                         reads=[r_dcl[ub], r_pre], writes=[r_dcl[ub]])
                                    k.op("dve", lambda e, ub=ub: e.reciprocal(out=dcl[ub][:], in_=dcl[ub][:]), reads=[r_dcl[ub]], writes=[r_dcl[ub]])
                                    if d_ == 0:
                                        k.op("dve", lambda e, P=P, ub=ub, csl=csl: e.tensor_tensor(out=Hs[:, csl], in0=P[:, 1, :], in1=dcl[ub][:], op=ALU.mult),
                                             reads=[rP[1], r_dcl[ub]], writes=[r_Hs[j], rL])
                                    else:
                                        k.op("dve", lambda e, P=P, ub=ub: e.tensor_tensor(out=htmp[ub][:], in0=P[:, 1, :], in1=dcl[ub][:], op=ALU.mult),
                                             reads=[rP[1], r_dcl[ub]], writes=[r_htmp[ub], rL])
                                        k.op("pool", lambda e, ub=ub, csl=csl: e.tensor_tensor(out=Hs[:, csl], in0=Hs[:, csl], in1=htmp[ub][:], op=ALU.add),
                                             reads=[r_htmp[ub], r_Hs[j]], writes=[r_Hs[j]])
                                    k.op("pe", lambda e, ub=ub, j=j: e.matmul(PC[ub][:], lhsT=Kpt[:, j, :], rhs=vd1[:, j, 0:129], start=True, stop=True),
                                         reads=[r_pre, r_ld], writes=[r_PC[ub]])
                                    nxt = 1 - cur
                                    k.op("dve", lambda e, ub=ub, cur=cur, nxt=nxt, j=j: e.scalar_tensor_tensor(
                                        out=cst_[nxt][:], in0=cst_[cur][:], scalar=wib[:, j:j + 1], in1=PC[ub][:], op0=ALU.mult, op1=ALU.add),
                                        reads=[r_c[cur], r_PC[ub], r_pre], writes=[r_c[nxt]])
                                    cur = nxt
                                    if oi + 1 < len(order):
                                        scaled_state(cur, order[oi + 1])
                                if not smp:
                                    k.dma("sp", [(Cnew[i, si - 1, d_, h], cst_[cur][:, 0:128]), (nnew[i, si - 1, d_, h], cst_[cur][:, 128:129])],
                                          r_c[cur], reads=[r_c[cur]])
                        sqh = sb("sqh", [128, TT], F32, sh)
                        rqh = sb("rqh", [128, TT], F32, sh)
                        hm = sb("hm", [128, TT], F32, sh)
                        sod = sb("sod", [128, TT], BF16, sh)
                        ho = [sb("ho%d" % b, [128, TT], BF16, sh) for b in range(2)]
                        r_sqh, r_rqh, r_hm, r_sod = Res("sqh"), Res("rqh"), Res("hm"), Res("sod")
                        r_ho = [Res("ho%d" % b) for b in range(2)]
                        for t in range(NT):
                            tsl = slice(t * TT, (t + 1) * TT)
                            rH = r_Hs[t * 4:(t + 1) * 4]
                            k.dma("sp", [(sod[:], SODT_d[h, :, tsl])], r_sod, writes=[r_sod])
                            k.op("act", lambda e, tsl=tsl: e.activation(out=sqh[:], in_=Hs[:, tsl], func=AF.Square), reads=rH, writes=[r_sqh])
                            k.op("pe", lambda e: e.matmul(pb[0][:], lhsT=cstf[:, F128, :], rhs=sqh[:], start=True, stop=True),
                                 reads=[r_sqh, r_cst], writes=[r_pb[0]])
                            k.op("act", lambda e: e.activation(out=rqh[:], in_=pb[0][:], func=AF.Sqrt, bias=eps_t[:], scale=1.0),
                                 reads=[r_pb[0], r_ones], writes=[r_rqh])
                            k.op("dve", lambda e: e.reciprocal(out=rqh[:], in_=rqh[:]), reads=[r_rqh], writes=[r_rqh])
                            k.op("dve", lambda e, tsl=tsl: e.scalar_tensor_tensor(out=hm[:], in0=Hs[:, tsl], scalar=paro[:, i, 3:4], in1=rqh[:],
                                                                                 op0=ALU.mult, op1=ALU.mult), reads=rH + [r_rqh, r_cst], writes=[r_hm])
                            b = t % 2
                            k.op("pool", lambda e, b=b: e.tensor_tensor(out=ho[b][:], in0=hm[:], in1=sod[:], op=ALU.mult),
                                 reads=[r_hm, r_sod], writes=[r_ho[b]])
                            k.dma("sp", [(MIX_d[:, 4 + h, tsl], ho[b][:])], r_ho[b], reads=[r_ho[b]], writes=[r_scr])
                        k.barrier()
                        k.release([r_ld, r_sod] + r_ho + r_c)
                k.release([r_mo])

        first = True
        for l in range(cfg.depth):
            i = l // 2
            ffn_phase(l, 0, xT_in if first else yT)
            first = False
            if cfg.do_mix:
                if l % 2 == 0:
                    even_proj(l, i)
                    even_fnet(i)
                    even_attn(i)
                    out_proj(l, w_out_e[i])
                else:
                    stage = getattr(cfg, "odd_stage", 9)
                    if stage >= 1:
                        odd_proj(l, i)
                    if stage >= 2:
                        odd_attn(l, i)
                    if stage >= 3:
                        odd_mlstm(l, i)
                    if stage >= 4:
                        out_proj(l, w_out_o[i])
            ffn_phase(l, 1, yT)
        k.barrier()
        print("instructions:", k.ninst, "waits:", k.nwait, "dma sems:", k.n_dma_sems, "sems:", len(k.sems))
    return nc


def _fm(x2d):
    return np.ascontiguousarray(x2d.T.reshape(NCH, 128, x2d.shape[0]))


def _tm(xT):
    return np.ascontiguousarray(xT.reshape(D, xT.shape[2]).T)


def _bf(a):
    return np.ascontiguousarray(a.astype(np.float32)).astype(ml_dtypes.bfloat16)


_CONST_CACHE = {}


def make_consts(cfg):
    key = (cfg.NS, cfg.NP)
    if key in _CONST_CACHE:
        return _CONST_CACHE[key]
    f32 = np.float32
    p = np.arange(128)
    cst_f = np.zeros((128, 5, 128), f32)
    for m in range(128):
        if m % 32 < 16:
            cst_f[m + 16, 0, m] = -1.0
        else:
            cst_f[m - 16, 0, m] = 1.0
    cst_f[:, 1, :] = (p[:, None] // 64 == p[None, :] // 64) / 64.0
    cst_f[:, 2, :] = 1.0 / 128.0
    cst_f[:, 3, :] = np.eye(128)
    cst_f[:, 4, :] = 1.0
    cst_b = np.zeros((128, 13, 128), f32)
    ang = 2.0 * np.pi * ((p[:, None] * p[None, :]) % 128) / 128.0
    cst_b[:, 0, :] = np.cos(ang)
    cst_b[:, 1, :] = -np.sin(ang)
    mprev = (p[None, :] <= p[:, None]).astype(f32)
    mnext = (p[:, None] <= p[None, :]).astype(f32)
    for hh in range(4):
        cst_b[:, 2 + hh, :] = mprev
        cst_b[:, 6 + hh, :] = mnext
    cst_b[:, 10, :] = (p[:, None] <= p[None, :])
    cst_b[:, 11, :] = (p[:, None] >= p[None, :])
    cst_b[:, 12, :] = 1.0
    cst4 = np.zeros((4, 4, 128), f32)
    for h in range(4):
        cst4[h, h, :] = 1.0
    cst4b = np.zeros((4, 5, 128), f32)
    cst4b[:, 0:4, :] = cst4
    for h in range(4):
        cst4b[h, 4, h * 32] = 1.0
    NS = cfg.NS
    tok = np.arange(NS)
    row = (tok // 64).astype(np.float64)
    col = (tok % 64).astype(np.float64)
    inv = 10000.0 ** (-np.arange(16, dtype=np.float32) / 16).astype(np.float32)
    d = p % 64
    axis = d // 32
    fr = d % 16
    pos = np.where(axis[:, None] == 0, row[None, :], col[None, :]).astype(np.float32)
    angr = (pos * inv[fr][:, None]).astype(np.float32)
    ropeT = np.stack([np.cos(angr), np.sin(angr)], axis=1).astype(f32)

    def seq_tab(S):
        s = np.arange(S, dtype=np.int64)
        prod = (s[:, None] * s[None, :]) % S
        a = 2.0 * np.pi * prod / S
        sc = 1.0 / math.sqrt(S * 128.0)
        cs = np.stack([np.cos(a) * sc, np.sin(a) * sc], axis=0).astype(f32)
        t = cs.reshape(2, S // 128, 128, S // 256, 256).transpose(3, 2, 1, 0, 4)
        return _bf(t)

    out = dict(cst_f=cst_f, cst_b=_bf(cst_b), cst4=cst4, cst4b=_bf(cst4b), ropeT=np.ascontiguousarray(ropeT),
               tabS=seq_tab(cfg.NS), tabP=seq_tab(cfg.NP))
    _CONST_CACHE[key] = out
    return out


def make_in_maps(cfg, inp, n_cores=8):
    f32 = np.float32
    g = lambda n: np.asarray(inp[n], f32)
    cs = make_consts(cfg)
    b_adaT = np.ascontiguousarray(g("b_ada").reshape(DEPTH, 72, 128).transpose(2, 0, 1))
    g_normT = np.ascontiguousarray(g("g_norm").reshape(DEPTH, 3, NCH, 128).transpose(3, 0, 1, 2))
    qperm = np.concatenate([np.r_[c * 64:(c + 1) * 64, (c + 4) * 64:(c + 5) * 64] for c in range(4)])
    wie = g("w_in_even")
    w_in_e = np.ascontiguousarray(np.concatenate([wie[:, :, qperm], wie[:, :, 512:]], axis=2))
    woe = g("w_out_even")
    w_out_e = np.ascontiguousarray(np.concatenate([woe[:, qperm, :], woe[:, 512:, :]], axis=1))
    p64 = np.arange(128) % 64
    par_e = np.ascontiguousarray(np.stack([g("qn_a")[:, p64], g("kn_a")[:, p64]], axis=-1).transpose(1, 0, 2))
    par_o = np.ascontiguousarray(np.stack([g("qn_c")[:, p64], g("kn_c")[:, p64], g("subln_c"), g("outnorm_d")],
                                          axis=-1).transpose(1, 0, 2))
    bg_o = np.ascontiguousarray(g("b_gate_odd").reshape(2, 4, 4).transpose(2, 0, 1))
    lam_o = np.ascontiguousarray(g("lam_c").reshape(2, 256))
    shared = dict(cs)
    shared.update(w_ada=g("w_ada"), b_adaT=b_adaT, g_normT=g_normT, w_ffn_in=g("w_ffn_in"), w_ffn_out=g("w_ffn_out"),
                  w_in_e=w_in_e, w_out_e=w_out_e, par_e=par_e, sink_e=g("sink_a"), w_in_o=g("w_in_odd"),
                  w_out_o=g("w_out_odd"), par_o=par_o, bg_o=bg_o, lam_o=lam_o)
    maps = []
    for core in range(n_cores):
        b = core // 2
        toks = np.concatenate([g("x_sample")[b], g("x_prompt")[2 * core], g("x_prompt")[2 * core + 1]], axis=0)
        cT = np.stack([g("c_ctx").reshape(NCH, 128).T, g("c")[b].reshape(NCH, 128).T], axis=-1)
        cka = g("cache_k_a")[b]
        kctx_e = np.ascontiguousarray(cka.transpose(0, 2, 3, 1).reshape(2, 128, PAST))
        cva = g("cache_v_a")[b]
        vctx_e = np.zeros((2, 128, 2, 2, 128), f32)
        cv = cva.reshape(2, 2, 128, 2, 64)
        vctx_e[:, :, :, 0, 0:64] = cv[:, :, :, 0, :].transpose(0, 2, 1, 3)
        vctx_e[:, :, :, 0, 64] = 1.0
        vctx_e[:, :, :, 1, 64:128] = cv[:, :, :, 1, :].transpose(0, 2, 1, 3)
        vctx_e[:, :, :, 1, 0] = 1.0
        ckc = g("cache_k_c")[b]
        kctx_o = np.ascontiguousarray(ckc.transpose(0, 3, 4, 2, 1).reshape(2, 128, 4, PAST))
        cvc = g("cache_v_c")[b]
        vctx_o = np.ascontiguousarray(cvc.reshape(2, 2, 128, 512).transpose(0, 2, 1, 3))
        sC = g("state_C_d")[b]
        c0_o = np.ascontiguousarray(sC.transpose(3, 0, 1, 2, 4))
        sn = g("state_n_d")[b]
        n0_o = np.ascontiguousarray(sn.reshape(2, 2, 4, 128, 1))
        sm = g("state_m_d")[b]
        m0_o = np.ascontiguousarray(sm.transpose(2, 0, 1))
        m = dict(shared)
        m.update(xT=_fm(toks), cT=np.ascontiguousarray(cT), kctx_e=kctx_e, vctx_e=vctx_e, kctx_o=kctx_o,
                 vctx_o=vctx_o, c0_o=c0_o, n0_o=n0_o, m0_o=m0_o)
        maps.append(m)
    return maps


def assemble(cfg, outs, B, n_cores=8):
    NS, NP = cfg.NS, cfg.NP
    f32 = np.float32
    yp = np.zeros((B, NP, D), f32)
    ys = np.zeros((max(1, n_cores // 2), NS, D), f32)
    ka = np.zeros((B, 2, NP, 2, 64), f32)
    va = np.zeros((B, 2, NP, 2, 64), f32)
    kc = np.zeros((B, 2, NP, 4, 2, 64), f32)
    vc = np.zeros((B, 2, NP, 4, 128), f32)
    Cd = np.zeros((B, 2, 2, 4, 128, 128), f32)
    nd = np.zeros((B, 2, 2, 4, 128), f32)
    md = np.zeros((B, 2, 2, 4), f32)
    for core in range(n_cores):
        o = outs[core]
        y = _tm(np.asarray(o["yT"], f32))
        if core % 2 == 0:
            ys[core // 2] = y[:NS]
        for s in range(2):
            bi = 2 * core + s
            yp[bi] = y[NS + s * NP:NS + (s + 1) * NP]
            tsl = slice(s * NP, (s + 1) * NP)
            kk = np.asarray(o["knew_a"], f32)[:, :, tsl]
            ka[bi] = kk.reshape(2, 2, 64, NP).transpose(0, 3, 1, 2)
            vv = np.asarray(o["vnew_a"], f32)
            vv = vv.transpose(0, 2, 1, 3).reshape(2, 2 * NP, 2, 64)[:, tsl]
            va[bi] = vv
            kk = np.asarray(o["knew_c"], f32)[:, :, :, tsl]
            kc[bi] = kk.reshape(2, 2, 64, 4, NP).transpose(0, 4, 3, 1, 2)
            vv = np.asarray(o["vnew_c"], f32).transpose(0, 2, 1, 3).reshape(2, 2 * NP, 4, 128)[:, tsl]
            vc[bi] = vv
            Cd[bi] = np.asarray(o["Cnew"], f32)[:, s]
            nd[bi] = np.asarray(o["nnew"], f32)[:, s, :, :, :, 0]
            md[bi] = np.asarray(o["mnew"], f32)[:, :, s, :].transpose(1, 2, 0)
    return (yp, ys, ka, va, kc, vc, Cd, nd, md)


_NC_CACHE = {}


def kernel(**inputs):
    inp = {k_: np.asarray(v) for k_, v in inputs.items()}
    cfg = Cfg()
    if "nc" not in _NC_CACHE:
        _NC_CACHE["nc"] = build(cfg)
    nc = _NC_CACHE["nc"]
    maps = make_in_maps(cfg, inp)
    res = run_bass_kernel_spmd(nc, maps, core_ids=list(range(8)))
    return assemble(cfg, res.results, inp["x_prompt"].shape[0])
```

```python
import math
import re
from contextlib import ExitStack
import numpy as np
import ml_dtypes
import concourse.bass as bass
import concourse.mybir as mybir
from concourse.bass_utils import run_bass_kernel_spmd

F32 = mybir.dt.float32
BF16 = mybir.dt.bfloat16
AF = mybir.ActivationFunctionType
ALU = mybir.AluOpType
AX = mybir.AxisListType

D = 1024
NCH = 8
DEPTH = 4
DFF = 2816
NFC = 22
EPS = 1e-6
HD = 64
PAST = 256
NEG = -30000.0


_PS_RE = re.compile(r"^(pm|msb|pg\d|pu\d|py\d|pq\d|pv|pvv|pg4|pms|prot|pf\d|pS\d|pO\d|pB\d|pD\d|pb\d|pmisc|PA\d_\d|PC\d)$")


class Res:
    __slots__ = ("name", "lw", "rd", "sem", "cnt", "ldma", "ps", "lock")

    def __init__(self, name, lock=False):
        self.name = name
        self.ps = bool(_PS_RE.match(name))
        self.lock = lock
        self.lw = None
        self.rd = []
        self.sem = None
        self.cnt = 0
        self.ldma = None


class Ev:
    __slots__ = ("key", "val", "snap", "eng")

    def __init__(self, key, val, snap, eng):
        self.key, self.val, self.snap, self.eng = key, val, snap, eng


class K:
    EPOCH = 20000

    def __init__(self, nc, stack):
        self.nc = nc
        self.stack = stack
        self.eng = {"pe": nc.tensor, "act": nc.scalar, "dve": nc.vector, "pool": nc.gpsimd, "sp": nc.sync}
        self.seq = {e: 0 for e in self.eng}
        self.know = {e: {} for e in self.eng}
        self.sems = {}
        self.dma_free = []
        self.n_dma_sems = 0
        self.dma_cnt = {}
        self.nwait = 0
        self.ninst = 0

    def sem(self, key):
        s = self.sems.get(key)
        if s is None:
            s = self.stack.enter_context(self.nc.semaphore("s_%s" % (str(key).replace(" ", ""))))
            self.sems[key] = s
        return s

    def _need(self, e, ev):
        if ev is None:
            return
        kn = self.know[e]
        if kn.get(ev.key, 0) >= ev.val:
            return
        self.eng[e].wait_ge(self.sem(ev.key), ev.val)
        self.nwait += 1
        kn[ev.key] = ev.val
        for k2, v2 in ev.snap.items():
            if kn.get(k2, 0) < v2:
                kn[k2] = v2

    def _deps(self, e, reads, writes):
        for r in reads:
            self._need(e, r.lw)
            if r.ps:
                for ev in r.rd:
                    if ev.eng != e:
                        self._need(e, ev)
        for w in writes:
            lw = w.lw
            if lw is not None and not (lw.eng == e and (e == "pe" or w.lock)):
                self._need(e, lw)
            for ev in w.rd:
                if ev.eng != e or e in ("sp", "pool"):
                    self._need(e, ev)

    def _commit(self, ev, reads, writes):
        for r in reads:
            r.rd.append(ev)
        for w in writes:
            w.lw = ev
            w.rd = []

    def op(self, e, fn, reads=(), writes=()):
        self._deps(e, reads, writes)
        n = self.seq[e]
        key = (e, n // self.EPOCH)
        val = n % self.EPOCH + 1
        ins = fn(self.eng[e])
        ins.then_inc(self.sem(key), 1)
        self.seq[e] = n + 1
        self.ninst += 1
        ev = Ev(key, val, dict(self.know[e]), e)
        self._commit(ev, reads, writes)
        return ev

    def dma(self, q, pairs, own, reads=(), writes=()):
        if own.sem is None:
            if self.dma_free:
                own.sem = self.dma_free.pop()
            else:
                own.sem = ("dma", self.n_dma_sems)
                self.n_dma_sems += 1
        self._need(q, own.ldma)
        self._deps(q, reads, writes)
        s = self.sem(own.sem)
        c = self.dma_cnt.get(own.sem, 0)
        for (o, i) in pairs:
            self.eng[q].dma_start(out=o, in_=i).then_inc(s, 16)
            c += 1
            self.ninst += 1
        self.dma_cnt[own.sem] = c
        ev = Ev(own.sem, 16 * c, dict(self.know[q]), "dma")
        own.ldma = ev
        self._commit(ev, reads, writes)
        return ev

    def release(self, ress):
        for r in ress:
            if r.sem is not None:
                self.dma_free.append(r.sem)
                r.sem = None

    def barrier(self):
        evs = []
        for e in self.eng:
            n = self.seq[e]
            if n > 0:
                evs.append(Ev((e, (n - 1) // self.EPOCH), (n - 1) % self.EPOCH + 1, {}, e))
        for key, c in self.dma_cnt.items():
            if c > 0:
                evs.append(Ev(key, 16 * c, {}, "dma"))
        for e in self.eng:
            for ev in evs:
                if ev.eng == e and e != "dma":
                    pass
                self._need(e, ev)


class Cfg:
    def __init__(self, ns=4096, npr=256, depth=DEPTH, do_mix=True):
        self.NS = ns
        self.NP = npr
        self.T = ns + 2 * npr
        self.depth = depth
        self.do_mix = do_mix
        self.TT = 512
        assert self.T % self.TT == 0 and ns % self.TT == 0
        self.NT = self.T // self.TT
        self.seqs = [(0, ns, 1, True), (ns, npr, 0, False), (ns + npr, npr, 0, False)]


def build(cfg):
    nc = bass.Bass("TRN2", target_bir_lowering=False)
    T, TT, NT = cfg.T, cfg.TT, cfg.NT
    dt = nc.dram_tensor

    xT_in = dt("xT", [NCH, 128, T], F32, kind="ExternalInput").ap()
    cT_in = dt("cT", [128, NCH, 2], F32, kind="ExternalInput").ap()
    w_ada = dt("w_ada", [DEPTH, D, 9 * D], F32, kind="ExternalInput").ap()
    b_adaT = dt("b_adaT", [128, DEPTH, 72], F32, kind="ExternalInput").ap()
    g_normT = dt("g_normT", [128, DEPTH, 3, NCH], F32, kind="ExternalInput").ap()
    w_ffn_in = dt("w_ffn_in", [DEPTH, 2, D, 2 * DFF], F32, kind="ExternalInput").ap()
    w_ffn_out = dt("w_ffn_out", [DEPTH, 2, DFF, D], F32, kind="ExternalInput").ap()
    yT = dt("yT", [NCH, 128, T], F32, kind="ExternalOutput").ap()
    NS, NP = cfg.NS, cfg.NP
    NTK = T // 128
    NPT = 2 * NP
    cst_f = dt("cst_f", [128, 5, 128], F32, kind="ExternalInput").ap()
    cst_b = dt("cst_b", [128, 13, 128], BF16, kind="ExternalInput").ap()
    cst4 = dt("cst4", [4, 4, 128], F32, kind="ExternalInput").ap()
    cst4b = dt("cst4b", [4, 5, 128], BF16, kind="ExternalInput").ap()
    ropeT = dt("ropeT", [128, 2, NS], F32, kind="ExternalInput").ap()
    tabS = dt("tabS", [NS // 256, 128, NS // 128, 2, 256], BF16, kind="ExternalInput").ap()
    tabP = dt("tabP", [NP // 256, 128, NP // 128, 2, 256], BF16, kind="ExternalInput").ap()
    w_in_e = dt("w_in_e", [2, D, 1280], F32, kind="ExternalInput").ap()
    w_out_e = dt("w_out_e", [2, D, D], F32, kind="ExternalInput").ap()
    par_e = dt("par_e", [128, 2, 2], F32, kind="ExternalInput").ap()
    sink_e = dt("sink_e", [2, 8], F32, kind="ExternalInput").ap()
    kctx_e = dt("kctx_e", [2, 128, PAST], F32, kind="ExternalInput").ap()
    vctx_e = dt("vctx_e", [2, 128, 2, 2, 128], F32, kind="ExternalInput").ap()
    w_in_o = dt("w_in_o", [2, D, 3600], F32, kind="ExternalInput").ap()
    w_out_o = dt("w_out_o", [2, D, D], F32, kind="ExternalInput").ap()
    par_o = dt("par_o", [128, 2, 4], F32, kind="ExternalInput").ap()
    bg_o = dt("bg_o", [4, 2, 4], F32, kind="ExternalInput").ap()
    lam_o = dt("lam_o", [2, 256], F32, kind="ExternalInput").ap()
    kctx_o = dt("kctx_o", [2, 128, 4, PAST], F32, kind="ExternalInput").ap()
    vctx_o = dt("vctx_o", [2, 128, 2, 512], F32, kind="ExternalInput").ap()
    c0_o = dt("c0_o", [128, 2, 2, 4, 128], F32, kind="ExternalInput").ap()
    n0_o = dt("n0_o", [2, 2, 4, 128, 1], F32, kind="ExternalInput").ap()
    m0_o = dt("m0_o", [4, 2, 2], F32, kind="ExternalInput").ap()
    knew_a = dt("knew_a", [2, 128, NPT], F32, kind="ExternalOutput").ap()
    vnew_a = dt("vnew_a", [2, 128, NPT // 128, 128], F32, kind="ExternalOutput").ap()
    knew_c = dt("knew_c", [2, 128, 4, NPT], F32, kind="ExternalOutput").ap()
    vnew_c = dt("vnew_c", [2, 128, NPT // 128, 512], F32, kind="ExternalOutput").ap()
    Cnew = dt("Cnew", [2, 2, 2, 4, 128, 128], F32, kind="ExternalOutput").ap()
    nnew = dt("nnew", [2, 2, 2, 4, 128, 1], F32, kind="ExternalOutput").ap()
    mnew = dt("mnew", [4, 2, 2, 2], F32, kind="ExternalOutput").ap()
    QT_d = dt("QT_d", [128, 4, T], BF16, kind="Internal").ap()
    KT_d = dt("KT_d", [128, 4, T], BF16, kind="Internal").ap()
    VA_d = dt("VA_d", [128, NTK, 2, 128], BF16, kind="Internal").ap()
    PQ_d = dt("PQ_d", [128, NTK, 4, 256], BF16, kind="Internal").ap()
    MIX_d = dt("MIX_d", [128, 8, T], BF16, kind="Internal").ap()
    VCt_d = dt("VCt_d", [128, NTK, 512], BF16, kind="Internal").ap()
    QDT_d = dt("QDT_d", [4, 128, T], BF16, kind="Internal").ap()
    KDT_d = dt("KDT_d", [4, 128, T], BF16, kind="Internal").ap()
    KDt_d = dt("KDt_d", [4, 128, NTK, 128], BF16, kind="Internal").ap()
    VD1t_d = dt("VD1t_d", [4, 128, NTK, 160], BF16, kind="Internal").ap()
    SODT_d = dt("SODT_d", [4, 128, T], BF16, kind="Internal").ap()
    G_d = dt("G_d", [4, 4, T], F32, kind="Internal").ap()

    with ExitStack() as gs:
        k = K(nc, gs)

        uid = [0]

        def sb(name, shape, dtype, st=gs):
            uid[0] += 1
            return st.enter_context(nc.sbuf_tensor("%s_%d" % (name, uid[0]), shape, dtype))

        def ps(name, shape, dtype, st=gs):
            uid[0] += 1
            return st.enter_context(nc.psum_tensor("%s_%d" % (name, uid[0]), shape, dtype))

        ones_bf = sb("ones_bf", [128, 128], BF16)
        r_ones = Res("ones_bf")
        k.op("pool", lambda e: e.memset(ones_bf[:], 1.0 / D), writes=[r_ones])
        eps_t = sb("eps_t", [128, 1], F32)
        k.op("pool", lambda e: e.memset(eps_t[:], EPS), writes=[r_ones])
        one_t = sb("one_t", [128, 1], F32)
        k.op("pool", lambda e: e.memset(one_t[:], 1.0), writes=[r_ones])
        cTs = sb("cTs", [128, NCH, 2], F32)
        scT = sb("scT", [128, NCH, 2], BF16)
        bada = sb("bada", [128, DEPTH, 72], F32)
        gnrm = sb("gnrm", [128, DEPTH, 3, NCH], F32)
        r_par = Res("par")
        k.dma("sp", [(cTs[:], cT_in), (bada[:], b_adaT), (gnrm[:], g_normT)], r_par, writes=[r_par])
        r_scT = Res("scT")
        k.op("act", lambda e: e.activation(out=scT[:], in_=cTs[:], func=AF.Silu), reads=[r_par], writes=[r_scT])
        MOD = sb("MOD", [128, DEPTH, 9, NCH, 2], F32)
        r_mod = Res("MOD")
        AM = sb("AM", [128, DEPTH, 3, NCH, 2], F32)
        GM = sb("GM", [128, DEPTH, 3, NCH, 2], F32)
        r_am = Res("AM")

        with ExitStack() as st:
            wa = [sb("wa%d" % i, [128, NCH, D], BF16, st) for i in range(2)]
            r_wa = [Res("wa%d" % i) for i in range(2)]
            pm = ps("pm", [128, 8, 2], F32, st)
            r_pm = Res("pm")
            it = 0
            for l in range(cfg.depth):
                for j in range(9):
                    b = it % 2
                    it += 1
                    k.dma("pool", [(wa[b][:, kc, :], w_ada[l, kc * 128:(kc + 1) * 128, j * D:(j + 1) * D])
                                   for kc in range(NCH)], r_wa[b], writes=[r_wa[b]])
                    for cc in range(NCH):
                        for kc in range(NCH):
                            k.op("pe", lambda e, cc=cc, kc=kc, b=b: e.matmul(
                                pm[:, cc, :], lhsT=wa[b][:, kc, cc * 128:(cc + 1) * 128], rhs=scT[:, kc, :],
                                start=(kc == 0), stop=(kc == NCH - 1)),
                                reads=[r_wa[b], r_scT], writes=[r_pm])
                    k.op("dve", lambda e, l=l, j=j: e.tensor_tensor(
                        out=MOD[:, l, j, :, :], in0=pm[:],
                        in1=bada[:, l, j * 8:(j + 1) * 8].unsqueeze(2).to_broadcast([128, 8, 2]),
                        op=ALU.add), reads=[r_pm, r_par], writes=[r_mod])
            for l in range(cfg.depth):
                for w in range(3):
                    k.op("dve", lambda e, l=l, w=w: e.scalar_tensor_tensor(
                        out=AM[:, l, w, :, :], in0=MOD[:, l, 3 * w + 1, :, :], scalar=1.0,
                        in1=gnrm[:, l, w, :].unsqueeze(2).to_broadcast([128, 8, 2]),
                        op0=ALU.add, op1=ALU.mult), reads=[r_mod, r_par], writes=[r_am])
                    k.op("dve", lambda e, l=l, w=w: e.tensor_scalar(
                        out=GM[:, l, w, :, :], in0=MOD[:, l, 3 * w + 2, :, :],
                        scalar1=(1.0 if w == 1 else 0.5), scalar2=None, op0=ALU.mult),
                        reads=[r_mod], writes=[r_am])
            k.barrier()
            k.release(r_wa)

        def norm_mod(st_x, r_x, xt, ht, r_h, l, w, cond, tmp):
            sq, r_sq, msb, r_ms, rstd, r_rstd, u, r_u = tmp
            for c in range(NCH):
                b = c % 2
                k.op("act", lambda e, c=c, b=b: e.activation(out=sq[b][:], in_=xt[:, c, :], func=AF.Square),
                     reads=[r_x], writes=[r_sq[b]])
                k.op("pe", lambda e, c=c, b=b: e.matmul(msb[:], lhsT=ones_bf[:], rhs=sq[b][:],
                                                       start=(c == 0), stop=(c == NCH - 1)),
                     reads=[r_sq[b], r_ones], writes=[r_ms])
            k.op("act", lambda e: e.activation(out=rstd[:], in_=msb[:], func=AF.Sqrt, bias=eps_t[:], scale=1.0),
                 reads=[r_ms, r_ones], writes=[r_rstd])
            k.op("dve", lambda e: e.reciprocal(out=rstd[:], in_=rstd[:]), reads=[r_rstd], writes=[r_rstd])
            for c in range(NCH):
                b = c % 2
                k.op("dve", lambda e, c=c, b=b: e.scalar_tensor_tensor(
                    out=u[b][:], in0=xt[:, c, :], scalar=AM[:, l, w, c, cond:cond + 1], in1=rstd[:],
                    op0=ALU.mult, op1=ALU.mult), reads=[r_x, r_rstd, r_am], writes=[r_u[b]])
                k.op("act", lambda e, c=c, b=b: e.activation(
                    out=ht[:, c, :], in_=u[b][:], func=AF.Identity,
                    bias=MOD[:, l, 3 * w, c, cond:cond + 1], scale=1.0),
                    reads=[r_u[b], r_mod], writes=[r_h])

        def alloc_norm_tmp(st):
            sq = [sb("sq%d" % i, [128, TT], BF16, st) for i in range(2)]
            r_sq = [Res("sq%d" % i) for i in range(2)]
            msb = ps("msb", [128, TT], F32, st)
            rstd = sb("rstd", [128, TT], F32, st)
            u = [sb("u%d" % i, [128, TT], F32, st) for i in range(2)]
            r_u = [Res("u%d" % i) for i in range(2)]
            return (sq, r_sq, msb, Res("msb"), rstd, Res("rstd"), u, r_u)

        def tile_cond(t):
            return 1 if t * TT < cfg.NS else 0

        def ffn_phase(l, j, src):
            w = 0 if j == 0 else 2
            with ExitStack() as st:
                wi = sb("wi", [128, NCH, 2 * DFF], BF16, st)
                wo = sb("wo", [128, NFC, D], BF16, st)
                r_wi = [Res("wi%d" % i) for i in range(NCH)]
                r_wo = [Res("wo%d" % i) for i in range(2)]
                for kc in range(NCH):
                    k.dma("pool", [(wi[:, kc, :], w_ffn_in[l, j, kc * 128:(kc + 1) * 128, :])], r_wi[kc],
                          writes=[r_wi[kc]])
                for hf in range(2):
                    k.dma("pool", [(wo[:, fc, :], w_ffn_out[l, j, fc * 128:(fc + 1) * 128, :])
                                   for fc in range(hf * 11, hf * 11 + 11)], r_wo[hf], writes=[r_wo[hf]])
                xt = sb("xt", [128, NCH, TT], F32, st)
                r_x = Res("xt")
                ht = sb("ht", [128, NCH, TT], BF16, st)
                r_h = Res("ht")
                tmp = alloc_norm_tmp(st)
                sg = [sb("sg%d" % i, [128, TT], F32, st) for i in range(2)]
                r_sg = [Res("sg%d" % i) for i in range(2)]
                act = sb("act", [128, NFC, TT], BF16, st)
                r_act = [Res("act%d" % i) for i in range(NFC)]
                pg = [ps("pg%d" % i, [128, TT], F32, st) for i in range(2)]
                pu = [ps("pu%d" % i, [128, TT], F32, st) for i in range(2)]
                py = [ps("py%d" % i, [128, TT], F32, st) for i in range(2)]
                r_pg = [Res("pg%d" % i) for i in range(2)]
                r_pu = [Res("pu%d" % i) for i in range(2)]
                r_py = [Res("py%d" % i) for i in range(2)]
                r_yT = Res("yT")
                for t in range(NT):
                    cond = tile_cond(t)
                    tsl = slice(t * TT, (t + 1) * TT)
                    k.dma("sp", [(xt[:, c, :], src[c, :, tsl]) for c in range(NCH)], r_x,
                          reads=[r_yT], writes=[r_x])
                    norm_mod(st, r_x, xt, ht, r_h, l, w, cond, tmp)
                    for fc in range(NFC):
                        b = fc % 2
                        for kc in range(NCH):
                            k.op("pe", lambda e, fc=fc, kc=kc, b=b: e.matmul(
                                pg[b][:], lhsT=wi[:, kc, fc * 128:(fc + 1) * 128], rhs=ht[:, kc, :],
                                start=(kc == 0), stop=(kc == NCH - 1)),
                                reads=[r_wi[kc], r_h], writes=[r_pg[b]])
                        for kc in range(NCH):
                            k.op("pe", lambda e, fc=fc, kc=kc, b=b: e.matmul(
                                pu[b][:], lhsT=wi[:, kc, DFF + fc * 128:DFF + (fc + 1) * 128], rhs=ht[:, kc, :],
                                start=(kc == 0), stop=(kc == NCH - 1)),
                                reads=[r_wi[kc], r_h], writes=[r_pu[b]])
                        k.op("act", lambda e, b=b: e.activation(out=sg[b][:], in_=pg[b][:], func=AF.Silu),
                             reads=[r_pg[b]], writes=[r_sg[b]])
                        k.op("dve", lambda e, b=b, fc=fc: e.tensor_tensor(
                            out=act[:, fc, :], in0=pu[b][:], in1=sg[b][:], op=ALU.mult),
                            reads=[r_pu[b], r_sg[b]], writes=[r_act[fc]])
                    for dc in range(NCH):
                        b = dc % 2
                        for fc in range(NFC):
                            k.op("pe", lambda e, fc=fc, dc=dc, b=b: e.matmul(
                                py[b][:], lhsT=wo[:, fc, dc * 128:(dc + 1) * 128], rhs=act[:, fc, :],
                                start=(fc == 0), stop=(fc == NFC - 1)),
                                reads=[r_wo[fc // 11], r_act[fc]], writes=[r_py[b]])
                        k.op("dve", lambda e, dc=dc, b=b: e.scalar_tensor_tensor(
                            out=xt[:, dc, :], in0=py[b][:], scalar=GM[:, l, w, dc, cond:cond + 1],
                            in1=xt[:, dc, :], op0=ALU.mult, op1=ALU.add),
                            reads=[r_py[b], r_am, r_x], writes=[r_x])
                    k.dma("sp", [(yT[c, :, tsl], xt[:, c, :]) for c in range(NCH)], r_x,
                          reads=[r_x], writes=[r_yT])
                k.barrier()
                k.release(r_wi + r_wo + [r_x])

        cstf = sb("cstf", [128, 5, 128], F32)
        cstb = sb("cstb", [128, 13, 128], BF16)
        c4f = sb("c4f", [4, 4, 128], F32)
        c4b = sb("c4b", [4, 5, 128], BF16)
        pare = sb("pare", [128, 2, 2], F32)
        paro = sb("paro", [128, 2, 4], F32)
        bgo = sb("bgo", [4, 2, 4], F32)
        m0t = sb("m0t", [4, 2, 2], F32)
        r_cst = Res("cst")
        k.dma("sp", [(cstf[:], cst_f), (cstb[:], cst_b), (c4f[:], cst4), (c4b[:], cst4b), (pare[:], par_e),
                     (paro[:], par_o), (bgo[:], bg_o), (m0t[:], m0_o)], r_cst, writes=[r_cst])
        RM, BD, F128, IDN, ONESF = 0, 1, 2, 3, 4
        CB_CS, CB_MP, CB_MN, CB_MF, CB_MB, CB_ONE = 0, 2, 6, 10, 11, 12

        def load_w_bf(st, name, src2d, ncols):
            ncp = (ncols + 31) // 32 * 32
            wt = sb(name, [128, NCH, ncp], BF16, st)
            r = Res(name)
            k.dma("pool", [(wt[:, kc, 0:ncols], src2d[kc * 128:(kc + 1) * 128, :]) for kc in range(NCH)], r, writes=[r])
            return wt, r

        def proj_fm(wt, r_w, col0, ht, r_h, pq, r_pq, m=128):
            for kc in range(NCH):
                k.op("pe", lambda e, kc=kc: e.matmul(pq, lhsT=wt[:, kc, col0:col0 + m], rhs=ht[:, kc, :],
                                                     start=(kc == 0), stop=(kc == NCH - 1)),
                     reads=[r_w, r_h], writes=[r_pq])

        def proj_tm(wt, r_w, col0, ncols, ht, r_h, sub, pv, r_pv):
            for kc in range(NCH):
                k.op("pe", lambda e, kc=kc: e.matmul(pv, lhsT=ht[:, kc, sub * 128:(sub + 1) * 128],
                                                     rhs=wt[:, kc, col0:col0 + ncols],
                                                     start=(kc == 0), stop=(kc == NCH - 1)),
                     reads=[r_w, r_h], writes=[r_pv])

        class QKN:
            def __init__(self, st):
                self.sets = []
                for z in range(2):
                    d_ = dict(
                        sqq=sb("sqq%d" % z, [128, TT], F32, st), rq=sb("rq%d" % z, [128, TT], F32, st),
                        qn=sb("qn%d" % z, [128, TT], F32, st), t1=sb("t1%d" % z, [128, TT], F32, st),
                        t2=sb("t2%d" % z, [128, TT], F32, st), pms=ps("pms", [128, TT], F32, st),
                        prot=ps("prot", [128, TT], F32, st))
                    d_["r"] = {n: Res(n) for n in ("sqq", "rq", "qn", "t1", "t2", "pms", "prot")}
                    self.sets.append(d_)
                self.n = 0
                self.last = None

            def all_qn_res(self):
                return [d_["r"]["qn"] for d_ in self.sets]

            def run(self, pq, r_pq, gain, rope, out_bf, r_out, r_rt=None):
                S_ = self.sets[self.n % 2]
                self.n += 1
                self.last = S_
                r = S_["r"]
                sqq, rq, qn, t1, t2, pms, prot = (S_[n_] for n_ in ("sqq", "rq", "qn", "t1", "t2", "pms", "prot"))
                k.op("act", lambda e: e.activation(out=sqq[:], in_=pq, func=AF.Square),
                     reads=[r_pq], writes=[r["sqq"]])
                k.op("pe", lambda e: e.matmul(pms[:], lhsT=cstf[:, BD, :], rhs=sqq[:], start=True, stop=True),
                     reads=[r["sqq"], r_cst], writes=[r["pms"]])
                k.op("act", lambda e: e.activation(out=rq[:], in_=pms[:], func=AF.Sqrt, bias=eps_t[:], scale=1.0),
                     reads=[r["pms"], r_ones], writes=[r["rq"]])
                k.op("dve", lambda e: e.reciprocal(out=rq[:], in_=rq[:]), reads=[r["rq"]], writes=[r["rq"]])
                k.op("dve", lambda e: e.scalar_tensor_tensor(out=qn[:], in0=pq, scalar=gain, in1=rq[:],
                                                             op0=ALU.mult, op1=ALU.mult),
                     reads=[r_pq, r["rq"], r_cst], writes=[r["qn"]])
                if rope is not None:
                    k.op("pe", lambda e: e.matmul(prot[:], lhsT=cstf[:, RM, :], rhs=qn[:], start=True, stop=True),
                         reads=[r["qn"], r_cst], writes=[r["prot"]])
                    k.op("pool", lambda e: e.tensor_tensor(out=t1[:], in0=qn[:], in1=rope[:, 0, :], op=ALU.mult),
                         reads=[r["qn"], r_rt], writes=[r["t1"]])
                    k.op("dve", lambda e: e.tensor_tensor(out=t2[:], in0=prot[:], in1=rope[:, 1, :], op=ALU.mult),
                         reads=[r["prot"], r_rt], writes=[r["t2"]])
                    k.op("pool", lambda e: e.tensor_tensor(out=out_bf, in0=t1[:], in1=t2[:], op=ALU.add),
                         reads=[r["t1"], r["t2"]], writes=[r_out])
                else:
                    k.op("act", lambda e: e.activation(out=out_bf, in_=qn[:], func=AF.Copy),
                         reads=[r["qn"]], writes=[r_out])

        def is_sample_tile(t):
            return t * TT < NS

        def even_proj(l, i):
            with ExitStack() as st:
                wie, r_wie = load_w_bf(st, "wie", w_in_e[i], 1280)
                xt = sb("xt", [128, NCH, TT], F32, st)
                r_x = Res("xt")
                ht = sb("ht", [128, NCH, TT], BF16, st)
                r_h = Res("ht")
                tmp = alloc_norm_tmp(st)
                qk = QKN(st)
                pq = [ps("pq%d" % b, [128, TT], F32, st) for b in range(2)]
                r_pq = [Res("pq%d" % b) for b in range(2)]
                pv = ps("pv", [128, 256], F32, st)
                r_pv = Res("pv")
                rt = sb("rt", [128, 2, TT], F32, st)
                r_rt = Res("rt")
                qo = sb("qo", [128, 5, TT], BF16, st)
                r_qo = Res("qo")
                va = sb("va", [128, 4, 2, 128], BF16, st)
                r_va = Res("va")
                v32 = sb("v32", [128, 4, 128], F32, st)
                r_v32 = Res("v32")
                ut = [sb("ut%d" % b, [128, TT], BF16, st) for b in range(2)]
                r_ut = [Res("ut%d" % b) for b in range(2)]
                pqt = sb("pqt", [128, 4, 4, 256], BF16, st)
                r_pqt = Res("pqt")
                k.op("pool", lambda e: e.memset(va[:], 0.0), writes=[r_va])
                k.op("pool", lambda e: e.memset(va[:, :, 0, 64:65], 1.0), writes=[r_va])
                k.op("pool", lambda e: e.memset(va[:, :, 1, 0:1], 1.0), writes=[r_va])
                r_scr = Res("scr_e")
                r_yT = Res("yT")
                for t in range(NT):
                    smp = is_sample_tile(t)
                    cond = 1 if smp else 0
                    tsl = slice(t * TT, (t + 1) * TT)
                    k.dma("sp", [(xt[:, c, :], yT[c, :, tsl]) for c in range(NCH)], r_x, reads=[r_yT], writes=[r_x])
                    if smp:
                        k.dma("sp", [(rt[:], ropeT[:, :, tsl])], r_rt, writes=[r_rt])
                    norm_mod(st, r_x, xt, ht, r_h, l, 1, cond, tmp)
                    for blk in range(5):
                        b = blk % 2
                        proj_fm(wie, r_wie, blk * 128, ht, r_h, pq[b][:], r_pq[b])
                        gain = pare[:, i, (0 if blk < 4 else 1):(1 if blk < 4 else 2)]
                        qk.run(pq[b][:], r_pq[b], gain, rt if smp else None, qo[:, blk, :], r_qo, r_rt)
                        if blk == 4 and not smp:
                            p0 = t * TT - NS
                            k.dma("sp", [(knew_a[i, :, p0:p0 + TT], qk.last["qn"][:])], qk.last["r"]["qn"], reads=[qk.last["r"]["qn"]])
                    k.dma("sp", [(QT_d[:, :, tsl], qo[:, 0:4, :]), (KT_d[:, 0, tsl], qo[:, 4, :])], r_qo,
                          reads=[r_qo], writes=[r_scr])
                    for sub in range(4):
                        proj_tm(wie, r_wie, 640, 128, ht, r_h, sub, pv[:, 0:128], r_pv)
                        k.op("act", lambda e, sub=sub: e.activation(out=va[:, sub, 0, 0:64], in_=pv[:, 0:64], func=AF.Copy),
                             reads=[r_pv], writes=[r_va])
                        k.op("dve", lambda e, sub=sub: e.tensor_copy(out=va[:, sub, 1, 64:128], in_=pv[:, 64:128]),
                             reads=[r_pv], writes=[r_va])
                        if not smp:
                            k.op("dve", lambda e, sub=sub: e.tensor_copy(out=v32[:, sub, :], in_=pv[:, 0:128]),
                                 reads=[r_pv], writes=[r_v32])
                    k.dma("sp", [(VA_d[:, t * 4:(t + 1) * 4, :, :], va[:])], r_va, reads=[r_va], writes=[r_scr])
                    if not smp:
                        p0 = (t * TT - NS) // 128
                        k.dma("sp", [(vnew_a[i, :, p0:p0 + 4, :], v32[:])], r_v32, reads=[r_v32])
                    for g in range(4):
                        b = g % 2
                        proj_fm(wie, r_wie, 768 + g * 128, ht, r_h, pq[b][:], r_pq[b])
                        k.op("act", lambda e, b=b: e.activation(out=ut[b][:], in_=pq[b][:], func=AF.Copy),
                             reads=[r_pq[b]], writes=[r_ut[b]])
                        for sub in range(4):
                            k.op("pe", lambda e, b=b, sub=sub: e.matmul(
                                pv[:], lhsT=ut[b][:, sub * 128:(sub + 1) * 128], rhs=cstb[:, CB_CS:CB_CS + 2, :],
                                start=True, stop=True), reads=[r_ut[b], r_cst], writes=[r_pv])
                            eng = "dve" if sub % 2 == 0 else "act"
                            if eng == "dve":
                                k.op("dve", lambda e, g=g, sub=sub: e.tensor_copy(out=pqt[:, sub, g, :], in_=pv[:]),
                                     reads=[r_pv], writes=[r_pqt])
                            else:
                                k.op("act", lambda e, g=g, sub=sub: e.activation(out=pqt[:, sub, g, :], in_=pv[:], func=AF.Copy),
                                     reads=[r_pv], writes=[r_pqt])
                    k.dma("sp", [(PQ_d[:, t * 4:(t + 1) * 4, :, :], pqt[:])], r_pqt, reads=[r_pqt], writes=[r_scr])
                k.barrier()
                k.release([r_wie, r_x, r_rt, r_qo, r_va, r_v32, r_pqt] + qk.all_qn_res())

        def even_fnet(i):
            with ExitStack() as st:
                pf = [ps("pf%d" % b, [128, 256], F32, st) for b in range(2)]
                r_pf = [Res("pf%d" % b) for b in range(2)]
                fo = [sb("fo%d" % b, [128, 4, 256], BF16, st) for b in range(2)]
                r_fo = [Res("fo%d" % b) for b in range(2)]
                rel = list(r_fo)
                r_scr = Res("mixd")
                for (off, S, cond, smp) in cfg.seqs:
                    nst, nkt = S // 128, S // 256
                    tabd = tabS if smp else tabP
                    with ExitStack() as s2:
                        pqs = sb("pqs", [128, nst, 4, 256], BF16, s2)
                        r_pqs = Res("pqs")
                        k.dma("sp", [(pqs[:, a:min(a + 8, nst)], PQ_d[:, off // 128 + a:off // 128 + min(a + 8, nst)])
                                     for a in range(0, nst, 8)], r_pqs, writes=[r_pqs])
                        tab = [sb("tab%d" % b, [128, nst, 2, 256], BF16, s2) for b in range(2)]
                        r_tab = [Res("tab%d" % b) for b in range(2)]
                        u = 0
                        for kt in range(nkt):
                            tb = kt % 2
                            k.dma("sp", [(tab[tb][:], tabd[kt])], r_tab[tb], writes=[r_tab[tb]])
                            fb = kt % 2
                            for g in range(4):
                                b = u % 2
                                u += 1
                                n = 0
                                for s_ in range(nst):
                                    for cs in range(2):
                                        k.op("pe", lambda e, b=b, s_=s_, cs=cs, g=g, n=n, tb=tb: e.matmul(
                                            pf[b][:], lhsT=pqs[:, s_, g, cs * 128:(cs + 1) * 128], rhs=tab[tb][:, s_, cs, :],
                                            start=(n == 0), stop=(n == 2 * nst - 1)),
                                            reads=[r_pqs, r_tab[tb]], writes=[r_pf[b]])
                                        n += 1
                                if g % 2 == 0:
                                    k.op("dve", lambda e, b=b, g=g, fb=fb: e.tensor_copy(out=fo[fb][:, g, :], in_=pf[b][:]),
                                         reads=[r_pf[b]], writes=[r_fo[fb]])
                                else:
                                    k.op("act", lambda e, b=b, g=g, fb=fb: e.activation(out=fo[fb][:, g, :], in_=pf[b][:], func=AF.Copy),
                                         reads=[r_pf[b]], writes=[r_fo[fb]])
                            k.dma("sp", [(MIX_d[:, 4:8, off + kt * 256:off + (kt + 1) * 256], fo[fb][:])], r_fo[fb],
                                  reads=[r_fo[fb]], writes=[r_scr])
                        k.barrier()
                        k.release([r_pqs] + r_tab)
                k.release(rel)

        def even_attn(i):
            with ExitStack() as st:
                esk = sb("esk", [128, 8], F32, st)
                r_esk = Res("esk")
                k.dma("sp", [(esk[0:1, :], sink_e[i:i + 1, :]), (esk[64:65, :], sink_e[i:i + 1, :])], r_esk, writes=[r_esk])
                k.op("act", lambda e: e.activation(out=esk[0:1, :], in_=esk[0:1, :], func=AF.Exp), reads=[r_esk], writes=[r_esk])
                k.op("act", lambda e: e.activation(out=esk[64:65, :], in_=esk[64:65, :], func=AF.Exp), reads=[r_esk], writes=[r_esk])
                pS = [ps("pS%d" % b, [128, 4, 128], F32, st) for b in range(3)]
                pO = [ps("pO%d" % b, [128, 4, 128], F32, st) for b in range(2)]
                pB = [ps("pB%d" % b, [128, 4, 128], F32, st) for b in range(2)]
                r_pS = [Res("pS%d" % b) for b in range(3)]
                r_pO = [Res("pO%d" % b) for b in range(2)]
                r_pB = [Res("pB%d" % b) for b in range(2)]
                pT = [sb("pT%d" % b, [128, 4, 128], BF16, st) for b in range(4)]
                r_pT = [Res("pT%d" % b) for b in range(4)]
                dd = [sb("dd%d" % b, [128, 4, 128], F32, st) for b in range(2)]
                r_dd = [Res("dd%d" % b) for b in range(2)]
                ob = [sb("ob%d" % b, [128, 4, 128], F32, st) for b in range(2)]
                r_ob = [Res("ob%d" % b) for b in range(2)]
                ao = [sb("ao%d" % b, [128, 4, 512], BF16, st) for b in range(2)]
                r_ao = [Res("ao%d" % b) for b in range(2)]
                qm = [[sb("qm%d%d" % (g, p_), [128, 4, 128], BF16, st) for p_ in range(2)] for g in range(2)]
                r_qm = [[Res("qm%d%d" % (g, p_)) for p_ in range(2)] for g in range(2)]
                for g in range(2):
                    for p_ in range(2):
                        k.op("pool", lambda e, g=g, p_=p_: e.memset(qm[g][p_][:], 0.0), writes=[r_qm[g][p_]])
                r_scr = Res("mixd")
                rel = [r_esk] + r_ao
                for (off, S, cond, smp) in cfg.seqs:
                    nb = S // 128
                    with ExitStack() as s2:
                        qs = sb("qs", [128, 4, S], BF16, s2)
                        ks = sb("ks", [128, S], BF16, s2)
                        vs = sb("vs", [128, nb, 2, 128], BF16, s2)
                        r_q = Res("qs")
                        k.dma("sp", [(qs[:], QT_d[:, :, off:off + S]), (ks[:], KT_d[:, 0, off:off + S]),
                                     (vs[:], VA_d[:, off // 128:off // 128 + nb])], r_q, writes=[r_q])
                        rr = [r_q]
                        if smp:
                            kcx = sb("kcx", [128, PAST], BF16, s2)
                            vcx = sb("vcx", [128, 2, 2, 128], BF16, s2)
                            r_cx = Res("cx")
                            k.dma("pool", [(kcx[:], kctx_e[i]), (vcx[:], vctx_e[i])], r_cx, writes=[r_cx])
                            rr.append(r_cx)
                        def kts_of(qb):
                            if smp:
                                kts = [("l", j, (CB_MP if j == qb - 1 else (CB_MN if j == qb + 1 else None)))
                                       for j in (qb - 1, qb, qb + 1) if 0 <= j < nb]
                                return kts + [("c", 0, None), ("c", 1, None)]
                            return [("l", j, None) for j in range(nb)]

                        steps = []
                        for qb in range(nb):
                            for g in range(2):
                                kts = kts_of(qb)
                                for n in range(len(kts)):
                                    steps.append((qb, g, n, kts[n], len(kts), len(steps) and 0))
                        unit_of = {}
                        for idx, (qb, g, n, kt, nk, _) in enumerate(steps):
                            unit_of[idx] = qb * 2 + g

                        def opnd(g, kt):
                            kind, j, msk = kt
                            rows = slice(g * 64, (g + 1) * 64)
                            if kind == "l":
                                return ks[rows, j * 128:(j + 1) * 128], vs[:, j, g, :], r_q
                            return kcx[rows, j * 128:(j + 1) * 128], vcx[:, j, g, :], r_cx

                        done_q = set()

                        def ensure_q(qb, g):
                            if (qb, g) in done_q or qb >= nb:
                                return
                            done_q.add((qb, g))
                            rows = slice(g * 64, (g + 1) * 64)
                            k.op("pool", lambda e: e.tensor_copy(out=qm[g][qb % 2][rows], in_=qs[rows, :, qb * 128:(qb + 1) * 128]),
                                 reads=[r_q], writes=[r_qm[g][qb % 2]])

                        def emit_S(idx):
                            qb, g, n, kt, nk, _ = steps[idx]
                            kind, j, msk = kt
                            if n == 0:
                                ensure_q(qb, g)
                                ensure_q(qb + (1 if g == 1 else 0), 1 - g)
                            if kind == "l":
                                kap_, rk = ks[:, j * 128:(j + 1) * 128], r_q
                            else:
                                kap_, rk = kcx[:, j * 128:(j + 1) * 128], r_cx
                            sbuf_i = idx % 3
                            k.op("pe", lambda e: e.matmul(pS[sbuf_i][:], lhsT=kap_, rhs=qm[g][qb % 2][:],
                                                          start=True, stop=True), reads=[rk, r_qm[g][qb % 2]], writes=[r_pS[sbuf_i]])

                        def emit_rest(idx):
                            qb, g, n, kt, nk, _ = steps[idx]
                            kap_, vap_, rk = opnd(g, kt)
                            msk = kt[2]
                            rows = slice(g * 64, (g + 1) * 64)
                            sbuf_i = idx % 3
                            tbuf = idx % 4
                            ub = unit_of[idx] % 2
                            ab = (qb // 4) % 2
                            k.op("act", lambda e: e.activation(out=pT[tbuf][:], in_=pS[sbuf_i][:], func=AF.Exp, scale=0.125),
                                 reads=[r_pS[sbuf_i]], writes=[r_pT[tbuf]])
                            if msk is not None:
                                k.op("pool", lambda e: e.tensor_tensor(out=pT[tbuf][:], in0=pT[tbuf][:], in1=cstb[:, msk:msk + 4, :], op=ALU.mult),
                                     reads=[r_pT[tbuf], r_cst], writes=[r_pT[tbuf]])
                            k.op("pe", lambda e: e.matmul(pO[ub][:], lhsT=vap_, rhs=pT[tbuf][:], start=(n == 0), stop=(n == nk - 1)),
                                 reads=[rk, r_pT[tbuf]], writes=[r_pO[ub]])
                            if n != nk - 1:
                                return
                            row = 64 if g == 0 else 0
                            k.op("dve", lambda e: e.tensor_tensor(
                                out=dd[ub][row:row + 1], in0=pO[ub][row:row + 1],
                                in1=esk[row:row + 1, g * 4:(g + 1) * 4].unsqueeze(2).to_broadcast([1, 4, 128]),
                                op=ALU.add), reads=[r_pO[ub], r_esk], writes=[r_dd[ub]])
                            k.op("dve", lambda e: e.reciprocal(out=dd[ub][row:row + 1], in_=dd[ub][row:row + 1]),
                                 reads=[r_dd[ub]], writes=[r_dd[ub]])
                            k.op("pe", lambda e: e.matmul(pB[ub][:], lhsT=cstf[row:row + 1, ONESF, :], rhs=dd[ub][row:row + 1], start=True, stop=True),
                                 reads=[r_dd[ub], r_cst], writes=[r_pB[ub]])
                            k.op("act", lambda e: e.activation(out=ob[ub][rows], in_=pO[ub][rows], func=AF.Copy),
                                 reads=[r_pO[ub]], writes=[r_ob[ub]])
                            k.op("dve", lambda e: e.tensor_tensor(
                                out=ao[ab][rows, :, (qb % 4) * 128:(qb % 4 + 1) * 128], in0=ob[ub][rows], in1=pB[ub][rows],
                                op=ALU.mult), reads=[r_ob[ub], r_pB[ub]], writes=[r_ao[ab]])
                            if g == 1 and (qb % 4 == 3 or qb == nb - 1):
                                q0 = (qb // 4) * 512
                                wdt = (qb % 4 + 1) * 128
                                k.dma("sp", [(MIX_d[:, 0:4, off + q0:off + q0 + wdt], ao[ab][:, :, 0:wdt])], r_ao[ab],
                                      reads=[r_ao[ab]], writes=[r_scr])

                        emit_S(0)
                        if len(steps) > 1:
                            emit_S(1)
                        for idx in range(len(steps)):
                            if idx + 2 < len(steps):
                                emit_S(idx + 2)
                            emit_rest(idx)
                        k.barrier()
                        k.release(rr)
                k.release(rel)

        def out_proj(l, wsrc):
            with ExitStack() as st:
                wom, r_wom = load_w_bf(st, "wom", wsrc, D)
                xt = sb("xt", [128, NCH, TT], F32, st)
                r_x = Res("xt")
                mx = sb("mx", [128, NCH, TT], BF16, st)
                r_mx = Res("mx")
                py = [ps("py%d" % b, [128, TT], F32, st) for b in range(2)]
                r_py = [Res("py%d" % b) for b in range(2)]
                r_yT = Res("yT")
                for t in range(NT):
                    cond = 1 if is_sample_tile(t) else 0
                    tsl = slice(t * TT, (t + 1) * TT)
                    k.dma("sp", [(xt[:, c, :], yT[c, :, tsl]) for c in range(NCH)], r_x, reads=[r_yT], writes=[r_x])
                    k.dma("sp", [(mx[:], MIX_d[:, :, tsl])], r_mx, writes=[r_mx])
                    for dc in range(NCH):
                        b = dc % 2
                        for kc in range(NCH):
                            k.op("pe", lambda e, kc=kc, dc=dc, b=b: e.matmul(
                                py[b][:], lhsT=wom[:, kc, dc * 128:(dc + 1) * 128], rhs=mx[:, kc, :],
                                start=(kc == 0), stop=(kc == NCH - 1)), reads=[r_wom, r_mx], writes=[r_py[b]])
                        k.op("dve", lambda e, dc=dc, b=b: e.scalar_tensor_tensor(
                            out=xt[:, dc, :], in0=py[b][:], scalar=GM[:, l, 1, dc, cond:cond + 1], in1=xt[:, dc, :],
                            op0=ALU.mult, op1=ALU.add), reads=[r_py[b], r_am, r_x], writes=[r_x])
                    k.dma("sp", [(yT[c, :, tsl], xt[:, c, :]) for c in range(NCH)], r_x, reads=[r_x], writes=[r_yT])
                k.barrier()
                k.release([r_wom, r_x, r_mx])

        DKS = 128 ** -0.5

        def odd_proj(l, i):
            with ExitStack() as st:
                wio, r_wio = load_w_bf(st, "wio", w_in_o[i], 3600)
                xt = sb("xt", [128, NCH, TT], F32, st)
                r_x = Res("xt")
                ht = sb("ht", [128, NCH, TT], BF16, st)
                r_h = Res("ht")
                tmp = alloc_norm_tmp(st)
                qk = QKN(st)
                pq = [ps("pq%d" % b, [128, TT], F32, st) for b in range(2)]
                r_pq = [Res("pq%d" % b) for b in range(2)]
                pvv = ps("pvv", [128, 512], F32, st)
                r_pvv = Res("pvv")
                pg4 = tmp[2][0:4, :]
                r_pg4 = tmp[3]
                rt = sb("rt", [128, 2, TT], F32, st)
                r_rt = Res("rt")
                qo = sb("qo", [128, 8, TT], BF16, st)
                r_qo = Res("qo")
                vt = sb("vt", [128, 4, 512], BF16, st)
                r_vt = Res("vt")
                v32 = sb("v32", [128, 4, 512], F32, st)
                r_v32 = Res("v32")
                fo = sb("fo", [128, 12, TT], BF16, st)
                r_fo = Res("fo")
                kt_ = sb("kt_", [128, 4, 4, 128], BF16, st)
                r_kt = Res("kt_")
                v1 = sb("v1", [128, 4, 4, 160], BF16, st)
                r_v1 = Res("v1")
                gt = sb("gt", [4, 4, TT], F32, st)
                r_gt = Res("gt")
                k.op("pool", lambda e: e.memset(v1[:], 1.0), writes=[r_v1])
                r_scr = Res("scr_o")
                r_yT = Res("yT")
                for t in range(NT):
                    smp = is_sample_tile(t)
                    cond = 1 if smp else 0
                    tsl = slice(t * TT, (t + 1) * TT)
                    k.dma("sp", [(xt[:, c, :], yT[c, :, tsl]) for c in range(NCH)], r_x, reads=[r_yT], writes=[r_x])
                    if smp:
                        k.dma("sp", [(rt[:], ropeT[:, :, tsl])], r_rt, writes=[r_rt])
                    norm_mod(st, r_x, xt, ht, r_h, l, 1, cond, tmp)
                    sub_ = getattr(cfg, "odd_sub", 31)
                    for blk in (range(8) if sub_ & 1 else []):
                        b = blk % 2
                        proj_fm(wio, r_wio, blk * 128, ht, r_h, pq[b][:], r_pq[b])
                        gain = paro[:, i, (0 if blk < 4 else 1):(1 if blk < 4 else 2)]
                        qk.run(pq[b][:], r_pq[b], gain, rt if smp else None, qo[:, blk, :], r_qo, r_rt)
                        if blk >= 4 and not smp:
                            p0 = t * TT - NS
                            k.dma("sp", [(knew_c[i, :, blk - 4, p0:p0 + TT], qk.last["qn"][:])], qk.last["r"]["qn"], reads=[qk.last["r"]["qn"]])
                    k.dma("sp", [(QT_d[:, :, tsl], qo[:, 0:4, :]), (KT_d[:, :, tsl], qo[:, 4:8, :])], r_qo,
                          reads=[r_qo], writes=[r_scr])
                    dbg_ = getattr(cfg, "odd_dbg", 31)
                    for sub in (range(4) if sub_ & 2 else []):
                        if dbg_ & 1:
                            proj_tm(wio, r_wio, 1024, 512, ht, r_h, sub, pvv[:], r_pvv)
                        if dbg_ & 2:
                            k.op("act", lambda e, sub=sub: e.activation(out=vt[:, sub, :], in_=pvv[:], func=AF.Copy),
                                 reads=[r_pvv], writes=[r_vt])
                        if not smp and (dbg_ & 4):
                            k.op("dve", lambda e, sub=sub: e.tensor_copy(out=v32[:, sub, :], in_=pvv[:]),
                                 reads=[r_pvv], writes=[r_v32])
                    if dbg_ & 8:
                        k.dma("sp", [(VCt_d[:, t * 4:(t + 1) * 4, :], vt[:])], r_vt, reads=[r_vt], writes=[r_scr])
                    if not smp and (dbg_ & 16):
                        p0 = (t * TT - NS) // 128
                        k.dma("sp", [(vnew_c[i, :, p0:p0 + 4, :], v32[:])], r_v32, reads=[r_v32])
                    for blk in (range(12) if sub_ & 4 else []):
                        b = blk % 2
                        col0 = (1536 + blk * 128) if blk < 8 else (3072 + (blk - 8) * 128)
                        proj_fm(wio, r_wio, col0, ht, r_h, pq[b][:], r_pq[b])
                        if blk < 4:
                            k.op("dve", lambda e, b=b, blk=blk: e.tensor_copy(out=fo[:, blk, :], in_=pq[b][:]),
                                 reads=[r_pq[b]], writes=[r_fo])
                        elif blk < 8:
                            k.op("act", lambda e, b=b, blk=blk: e.activation(out=fo[:, blk, :], in_=pq[b][:], func=AF.Identity, scale=DKS),
                                 reads=[r_pq[b]], writes=[r_fo])
                        else:
                            k.op("act", lambda e, b=b, blk=blk: e.activation(out=fo[:, blk, :], in_=pq[b][:], func=AF.Sigmoid),
                                 reads=[r_pq[b]], writes=[r_fo])
                    k.dma("sp", [(QDT_d[h, :, tsl], fo[:, h, :]) for h in range(4)]
                          + [(KDT_d[h, :, tsl], fo[:, 4 + h, :]) for h in range(4)]
                          + [(SODT_d[h, :, tsl], fo[:, 8 + h, :]) for h in range(4)], r_fo, reads=[r_fo], writes=[r_scr])
                    for sub in (range(4) if sub_ & 8 else []):
                        proj_tm(wio, r_wio, 2048, 512, ht, r_h, sub, pvv[:], r_pvv)
                        k.op("act", lambda e, sub=sub: e.activation(out=kt_[:, sub, :, :], in_=pvv[:].rearrange("p (h d) -> p h d", h=4),
                                                                    func=AF.Identity, scale=DKS), reads=[r_pvv], writes=[r_kt])
                        proj_tm(wio, r_wio, 2560, 512, ht, r_h, sub, pvv[:], r_pvv)
                        k.op("dve", lambda e, sub=sub: e.tensor_copy(out=v1[:, sub, :, 0:128], in_=pvv[:].rearrange("p (h d) -> p h d", h=4)),
                             reads=[r_pvv], writes=[r_v1])
                    k.dma("sp", [(KDt_d[h, :, t * 4:(t + 1) * 4, :], kt_[:, :, h, :]) for h in range(4)], r_kt,
                          reads=[r_kt], writes=[r_scr])
                    k.dma("sp", [(VD1t_d[h, :, t * 4:(t + 1) * 4, :], v1[:, :, h, :]) for h in range(4)], r_v1,
                          reads=[r_v1], writes=[r_scr])
                    for grp in (range(4) if sub_ & 16 else []):
                        proj_fm(wio, r_wio, 3584 + grp * 4, ht, r_h, pg4, r_pg4, m=4)
                        k.op("act", lambda e, grp=grp: e.activation(out=gt[:, grp, :], in_=pg4, func=AF.Identity,
                                                                    bias=bgo[:, i, grp:grp + 1], scale=1.0),
                             reads=[r_pg4, r_cst], writes=[r_gt])
                    k.dma("sp", [(G_d[grp, :, tsl], gt[:, grp, :]) for grp in range(4)], r_gt, reads=[r_gt], writes=[r_scr])
                k.barrier()
                k.release([r_wio, r_x, r_rt, r_qo, r_vt, r_v32, r_fo, r_kt, r_v1, r_gt] + qk.all_qn_res())

        def odd_attn(l, i):
            lam_init = 0.8 - 0.6 * math.exp(-0.3 * l)
            with ExitStack() as st:
                lv = sb("lv", [1, 2, 2, 64], F32, st)
                pr = sb("pr", [1, 2, 64], F32, st)
                s2_ = sb("s2_", [1, 16], F32, st)
                nlb = sb("nlb", [128, 1], F32, st)
                r_lv = Res("lv")
                k.dma("sp", [(lv[:], lam_o[i:i + 1, :].rearrange("o (a b d) -> o a b d", a=2, b=2))], r_lv, writes=[r_lv])
                k.op("dve", lambda e: e.tensor_tensor(out=pr[:], in0=lv[:, :, 0, :], in1=lv[:, :, 1, :], op=ALU.mult),
                     reads=[r_lv], writes=[r_lv])
                k.op("dve", lambda e: e.reduce_sum(out=s2_[:, 0:2], in_=pr[:], axis=AX.X), reads=[r_lv], writes=[r_lv])
                k.op("act", lambda e: e.activation(out=s2_[:, 0:2], in_=s2_[:, 0:2], func=AF.Exp), reads=[r_lv], writes=[r_lv])
                k.op("dve", lambda e: e.tensor_tensor(out=s2_[:, 2:3], in0=s2_[:, 1:2], in1=s2_[:, 0:1], op=ALU.subtract),
                     reads=[r_lv], writes=[r_lv])
                k.op("dve", lambda e: e.tensor_scalar(out=s2_[:, 8:9], in0=s2_[:, 2:3], scalar1=-lam_init, scalar2=None, op0=ALU.add),
                     reads=[r_lv], writes=[r_lv])
                pS = [ps("pS%d" % b, [128, 512], F32, st) for b in range(3)]
                pO = [ps("pO%d" % b, [128, 512], F32, st) for b in range(2)]
                pD = [ps("pD%d" % b, [128, 512], F32, st) for b in range(2)]
                pms = ps("pms", [128, 512], F32, st)
                r_pS = [Res("pS%d" % b) for b in range(3)]
                r_pO = [Res("pO%d" % b) for b in range(2)]
                r_pD = [Res("pD%d" % b) for b in range(2)]
                r_pms = Res("pms")
                k.op("pe", lambda e: e.matmul(pms[:, 0:1], lhsT=cstf[0:1, ONESF, :], rhs=s2_[:, 8:9], start=True, stop=True),
                     reads=[r_lv, r_cst], writes=[r_pms])
                r_nlb = Res("nlb")
                k.op("dve", lambda e: e.tensor_copy(out=nlb[:], in_=pms[:, 0:1]), reads=[r_pms], writes=[r_nlb])
                E = [sb("E%d" % b, [128, 512], BF16, st) for b in range(4)]
                r_E = [Res("E%d" % b) for b in range(4)]
                rd = [sb("rd%d" % b, [128, 512], F32, st) for b in range(2)]
                om = [sb("om%d" % b, [128, 512], F32, st) for b in range(2)]
                r_rd = [Res("rd%d" % b) for b in range(2)]
                r_om = [Res("om%d" % b) for b in range(2)]
                aa = sb("aa", [128, 512], F32, st)
                sq = sb("sqa", [128, 512], F32, st)
                rq = sb("rqa", [128, 512], F32, st)
                r_aa, r_sq, r_rq = Res("aa"), Res("sqa"), Res("rqa")
                ao = [sb("ao%d" % b, [128, 4, 512], BF16, st) for b in range(2)]
                r_ao = [Res("ao%d" % b) for b in range(2)]
                qm = [[sb("qm%d%d" % (m, p_), [128, 512], BF16, st) for p_ in range(2)] for m in range(2)]
                r_qm = [[Res("qm%d%d" % (m, p_)) for p_ in range(2)] for m in range(2)]
                for m in range(2):
                    for p_ in range(2):
                        k.op("pool", lambda e, m=m, p_=p_: e.memset(qm[m][p_][:], 0.0), writes=[r_qm[m][p_]])
                r_scr = Res("mixd")
                rel = [r_lv] + r_ao
                for (off, S, cond, smp) in cfg.seqs:
                    nb = S // 128
                    QW = min(512, S)
                    with ExitStack() as s2:
                        qs = sb("qs", [128, 4, S], BF16, s2)
                        ks = sb("ks", [128, 4, S], BF16, s2)
                        vs = sb("vs", [128, nb, 512], BF16, s2)
                        r_q = Res("qs")
                        k.dma("sp", [(qs[:], QT_d[:, :, off:off + S]), (ks[:], KT_d[:, :, off:off + S]),
                                     (vs[:], VCt_d[:, off // 128:off // 128 + nb, :])], r_q, writes=[r_q])
                        rr = [r_q]
                        if smp:
                            kcx = sb("kcx", [128, 4, PAST], BF16, s2)
                            vcx = sb("vcx", [128, 2, 512], BF16, s2)
                            r_cx = Res("cx")
                            k.dma("pool", [(kcx[:], kctx_o[i]), (vcx[:], vctx_o[i])], r_cx, writes=[r_cx])
                            rr.append(r_cx)
                        kts = [("l", j) for j in range(nb)] + ([("c", 0), ("c", 1)] if smp else [])
                        NK = len(kts)
                        steps = [(qt, h, m, n) for qt in range(S // QW) for h in range(4) for m in range(2) for n in range(NK)]

                        def opnd(h, m, n):
                            kind, j = kts[n]
                            rows = slice(m * 64, (m + 1) * 64)
                            if kind == "l":
                                return ks[rows, h, j * 128:(j + 1) * 128], vs[:, j, h * 128:(h + 1) * 128], r_q
                            return kcx[rows, h, j * 128:(j + 1) * 128], vcx[:, j, h * 128:(h + 1) * 128], r_cx

                        done_q = set()

                        def ensure_q(qt, h, m):
                            if (qt, h, m) in done_q or qt >= S // QW:
                                return
                            done_q.add((qt, h, m))
                            par = (qt * 4 + h) % 2
                            rows = slice(m * 64, (m + 1) * 64)
                            qsl = slice(qt * QW, (qt + 1) * QW)
                            k.op("pool", lambda e: e.tensor_copy(out=qm[m][par][rows, 0:QW], in_=qs[rows, h, qsl]),
                                 reads=[r_q], writes=[r_qm[m][par]])

                        def emit_S(idx):
                            qt, h, m, n = steps[idx]
                            kind, j = kts[n]
                            par = (qt * 4 + h) % 2
                            if n == 0:
                                ensure_q(qt, h, m)
                                nh = qt * 4 + h + (1 if m == 1 else 0)
                                ensure_q(nh // 4, nh % 4, 1 - m)
                            if kind == "l":
                                kap_, rk = ks[:, h, j * 128:(j + 1) * 128], r_q
                            else:
                                kap_, rk = kcx[:, h, j * 128:(j + 1) * 128], r_cx
                            sb_i = idx % 3
                            k.op("pe", lambda e: e.matmul(pS[sb_i][:, 0:QW], lhsT=kap_, rhs=qm[m][par][:, 0:QW], start=True, stop=True),
                                 reads=[rk, r_qm[m][par]], writes=[r_pS[sb_i]])

                        def emit_rest(idx):
                            qt, h, m, n = steps[idx]
                            kap_, vap_, rk = opnd(h, m, n)
                            sb_i = idx % 3
                            eb = idx % 4
                            ab = qt % 2
                            k.op("act", lambda e: e.activation(out=E[eb][:, 0:QW], in_=pS[sb_i][:, 0:QW], func=AF.Exp, scale=0.125),
                                 reads=[r_pS[sb_i]], writes=[r_E[eb]])
                            k.op("pe", lambda e: e.matmul(pO[m][:, 0:QW], lhsT=vap_, rhs=E[eb][:, 0:QW], start=(n == 0), stop=(n == NK - 1)),
                                 reads=[rk, r_E[eb]], writes=[r_pO[m]])
                            k.op("pe", lambda e: e.matmul(pD[m][:, 0:QW], lhsT=cstb[:, CB_ONE, :], rhs=E[eb][:, 0:QW], start=(n == 0), stop=(n == NK - 1)),
                                 reads=[r_cst, r_E[eb]], writes=[r_pD[m]])
                            if n != NK - 1:
                                return
                            k.op("dve", lambda e: e.reciprocal(out=rd[m][:, 0:QW], in_=pD[m][:, 0:QW]),
                                 reads=[r_pD[m]], writes=[r_rd[m]])
                            k.op("dve", lambda e: e.tensor_tensor(out=om[m][:, 0:QW], in0=pO[m][:, 0:QW], in1=rd[m][:, 0:QW], op=ALU.mult),
                                 reads=[r_pO[m], r_rd[m]], writes=[r_om[m]])
                            if m != 1:
                                return
                            k.op("dve", lambda e: e.scalar_tensor_tensor(out=aa[:, 0:QW], in0=om[1][:, 0:QW], scalar=nlb[:, 0:1],
                                                                         in1=om[0][:, 0:QW], op0=ALU.mult, op1=ALU.add),
                                 reads=[r_om[0], r_om[1], r_nlb], writes=[r_aa])
                            k.op("act", lambda e: e.activation(out=sq[:, 0:QW], in_=aa[:, 0:QW], func=AF.Square),
                                 reads=[r_aa], writes=[r_sq])
                            k.op("pe", lambda e: e.matmul(pms[:, 0:QW], lhsT=cstf[:, F128, :], rhs=sq[:, 0:QW], start=True, stop=True),
                                 reads=[r_sq, r_cst], writes=[r_pms])
                            k.op("act", lambda e: e.activation(out=rq[:, 0:QW], in_=pms[:, 0:QW], func=AF.Sqrt, bias=eps_t[:], scale=1.0),
                                 reads=[r_pms, r_ones], writes=[r_rq])
                            k.op("dve", lambda e: e.reciprocal(out=rq[:, 0:QW], in_=rq[:, 0:QW]), reads=[r_rq], writes=[r_rq])
                            k.op("dve", lambda e: e.scalar_tensor_tensor(out=aa[:, 0:QW], in0=aa[:, 0:QW], scalar=paro[:, i, 2:3],
                                                                         in1=rq[:, 0:QW], op0=ALU.mult, op1=ALU.mult),
                                 reads=[r_aa, r_rq, r_cst], writes=[r_aa])
                            k.op("act", lambda e: e.activation(out=ao[ab][:, h, 0:QW], in_=aa[:, 0:QW], func=AF.Identity,
                                                               scale=(1.0 - lam_init)),
                                 reads=[r_aa], writes=[r_ao[ab]])
                            if h == 3:
                                k.dma("sp", [(MIX_d[:, 0:4, off + qt * QW:off + (qt + 1) * QW], ao[ab][:, :, 0:QW])], r_ao[ab],
                                      reads=[r_ao[ab]], writes=[r_scr])

                        emit_S(0)
                        if len(steps) > 1:
                            emit_S(1)
                        for idx in range(len(steps)):
                            if idx + 2 < len(steps):
                                emit_S(idx + 2)
                            emit_rest(idx)
                        k.barrier()
                        k.release(rr)
                k.release(rel)

        def odd_mlstm(l, i):
            nch = T // 128
            with ExitStack() as st:
                Bt = [sb("Bt%d" % d_, [4, T], BF16, st) for d_ in range(2)]
                CLt = [sb("CLt%d" % d_, [4, T], BF16, st) for d_ in range(2)]
                WI = [sb("WI%d" % d_, [4, nch], F32, st) for d_ in range(2)]
                mo = sb("mo", [4, 2, 2], F32, st)
                r_bk = Res("bk")
                r_mo = Res("mo")
                with ExitStack() as s1:
                    A1 = sb("A1", [4, T], F32, s1)
                    A2 = sb("A2", [4, T], F32, s1)
                    A3 = sb("A3", [4, T], F32, s1)
                    seg = sb("seg", [4, T], F32, s1)
                    bendN = sb("bendN", [4, nch], F32, s1)
                    kap = sb("kap", [4, nch], F32, s1)
                    Mr = sb("Mr", [4, nch], F32, s1)
                    WIe = sb("WIe", [4, nch], F32, s1)
                    marr = sb("marr", [4, nch], F32, s1)
                    zer = sb("zer", [4, 1], F32, s1)
                    r_a = Res("A")
                    r_seg = Res("seg")
                    k.op("pool", lambda e: e.memset(seg[:], 1.0), writes=[r_seg])
                    k.op("pool", lambda e: e.memset(seg[:].rearrange("p (c t) -> p c t", t=128)[:, :, 0:1], 0.0), writes=[r_seg])
                    k.op("pool", lambda e: e.memset(zer[:], 0.0), writes=[r_seg])
                    A2v = A2[:].rearrange("p (c t) -> p c t", t=128)
                    A3v = A3[:].rearrange("p (c t) -> p c t", t=128)
                    for d_ in range(2):
                        k.dma("sp", [(A1[:], G_d[d_ * 2 + 1]), (A2[:], G_d[d_ * 2])], r_a, writes=[r_a])
                        k.op("act", lambda e: e.activation(out=A1[:], in_=A1[:], func=AF.Exp, scale=-1.0), reads=[r_a], writes=[r_a])
                        k.op("act", lambda e: e.activation(out=A1[:], in_=A1[:], func=AF.Ln, bias=one_t[0:4, :], scale=1.0), reads=[r_a], writes=[r_a])
                        k.op("dve", lambda e: e.tensor_tensor_scan(out=A3[:], data0=seg[:], data1=A1[:], initial=0.0,
                                                                  op0=ALU.mult, op1=ALU.add), reads=[r_a, r_seg], writes=[r_a])
                        k.op("dve", lambda e: e.tensor_copy(out=bendN[:], in_=A3v[:, :, 127]), reads=[r_a], writes=[r_a])
                        if d_ == 1:
                            k.op("dve", lambda e: e.tensor_tensor(out=A3v, in0=bendN[:].unsqueeze(2).to_broadcast([4, nch, 128]),
                                                                  in1=A3v, op=ALU.subtract), reads=[r_a], writes=[r_a])
                            k.op("dve", lambda e: e.tensor_tensor(out=A3[:], in0=A3[:], in1=A1[:], op=ALU.add), reads=[r_a], writes=[r_a])
                        k.op("dve", lambda e: e.tensor_tensor(out=A2[:], in0=A2[:], in1=A3[:], op=ALU.add), reads=[r_a], writes=[r_a])
                        k.op("dve", lambda e: e.tensor_reduce(out=kap[:], in_=A2v, axis=AX.X, op=ALU.max), reads=[r_a], writes=[r_a])
                        for si, (off, S, cond, smp) in enumerate(cfg.seqs):
                            c0, c1 = off // 128, (off + S) // 128
                            order = list(range(c0, c1)) if d_ == 0 else list(range(c1 - 1, c0 - 1, -1))
                            mcur = m0t[:, i, d_:d_ + 1] if smp else zer[:]
                            for j in order:
                                k.op("dve", lambda e, j=j, mcur=mcur: e.tensor_tensor(out=Mr[:, j:j + 1], in0=mcur, in1=kap[:, j:j + 1], op=ALU.max),
                                     reads=[r_a, r_cst, r_seg], writes=[r_a])
                                k.op("dve", lambda e, j=j, mcur=mcur: e.tensor_tensor(out=WIe[:, j:j + 1], in0=mcur, in1=Mr[:, j:j + 1], op=ALU.subtract),
                                     reads=[r_a, r_cst, r_seg], writes=[r_a])
                                k.op("dve", lambda e, j=j: e.tensor_tensor(out=marr[:, j:j + 1], in0=Mr[:, j:j + 1], in1=bendN[:, j:j + 1], op=ALU.subtract),
                                     reads=[r_a], writes=[r_a])
                                mcur = marr[:, j:j + 1]
                            if not smp:
                                k.op("dve", lambda e, si=si, mcur=mcur, d_=d_: e.tensor_copy(out=mo[:, si - 1, d_:d_ + 1], in_=mcur),
                                     reads=[r_a], writes=[r_mo])
                        Mb = Mr[:].unsqueeze(2).to_broadcast([4, nch, 128])
                        k.op("dve", lambda e: e.tensor_tensor(out=A2v, in0=A2v, in1=Mb, op=ALU.subtract), reads=[r_a], writes=[r_a])
                        k.op("act", lambda e, d_=d_: e.activation(out=Bt[d_][:], in_=A2[:], func=AF.Exp), reads=[r_a], writes=[r_bk])
                        k.op("dve", lambda e: e.tensor_tensor(out=A3v, in0=A3v, in1=Mb, op=ALU.subtract), reads=[r_a], writes=[r_a])
                        k.op("act", lambda e, d_=d_: e.activation(out=CLt[d_][:], in_=A3[:], func=AF.Exp), reads=[r_a], writes=[r_bk])
                        k.op("act", lambda e, d_=d_: e.activation(out=WI[d_][:], in_=WIe[:], func=AF.Exp), reads=[r_a], writes=[r_bk])
                    k.dma("sp", [(mnew[:, i, :, :], mo[:])], r_mo, reads=[r_mo])
                    k.barrier()
                    k.release([r_a])
                pb = [ps("pb%d" % b, [128, 512], F32, st) for b in range(2)]
                r_pb = [Res("pb%d" % b) for b in range(2)]
                pmisc = ps("pmisc", [128, 2, nch], F32, st)
                r_pmisc = Res("pmisc")
                PA = [ps("PA%d" % b, [128, 3, 128], F32, st) for b in range(2)]
                r_PA = [[Res("PA%d_%d" % (b, c_)) for c_ in range(3)] for b in range(2)]
                r_PAL = [Res("PAlock%d" % b, lock=True) for b in range(2)]
                PC = [ps("PC%d" % b, [128, 129], F32, st) for b in range(2)]
                r_PC = [Res("PC%d" % b) for b in range(2)]
                r_scr = Res("mixd")
                for h in range(4):
                    with ExitStack() as sh:
                        qd = sb("qd", [128, T], BF16, sh)
                        kdT = sb("kdT", [128, T], BF16, sh)
                        kdt = sb("kdt", [128, nch, 128], BF16, sh)
                        vd1 = sb("vd1", [128, nch, 160], BF16, sh)
                        r_ld = Res("ld")
                        k.dma("sp", [(qd[:], QDT_d[h]), (kdT[:], KDT_d[h]), (kdt[:], KDt_d[h]), (vd1[:], VD1t_d[h])], r_ld, writes=[r_ld])
                        Hs = [sb("Hs%d" % d_, [128, T], F32, sh) for d_ in range(2)]
                        r_Hs = [[Res("Hs%d_%d" % (d_, j)) for j in range(nch)] for d_ in range(2)]
                        KpT = [sb("KpT%d" % d_, [128, T], BF16, sh) for d_ in range(2)]
                        CLB = [sb("CLB%d" % d_, [128, T], BF16, sh) for d_ in range(2)]
                        Kpt = [sb("Kpt%d" % d_, [128, nch, 128], BF16, sh) for d_ in range(2)]
                        bcol = [sb("bcol%d" % d_, [128, nch], F32, sh) for d_ in range(2)]
                        wib = [sb("wib%d" % d_, [128, nch], F32, sh) for d_ in range(2)]
                        r_pre = [Res("pre%d" % d_) for d_ in range(2)]
                        scm = [sb("scm%d" % b, [128, 128], BF16, sh) for b in range(2)]
                        r_scm = [Res("scm%d" % b) for b in range(2)]
                        dcl = [sb("dcl%d" % b, [128, 128], F32, sh) for b in range(2)]
                        r_dcl = [Res("dcl%d" % b) for b in range(2)]
                        ucnt = [0]
                        rel_c = []
                        for d_ in range(2):
                            for t in range(NT):
                                tsl = slice(t * TT, (t + 1) * TT)
                                k.op("pe", lambda e, tsl=tsl, d_=d_: e.matmul(pb[0][:], lhsT=c4b[:, h, :], rhs=Bt[d_][:, tsl], start=True, stop=True),
                                     reads=[r_bk, r_cst], writes=[r_pb[0]])
                                k.op("dve", lambda e, tsl=tsl, d_=d_: e.tensor_tensor(out=KpT[d_][:, tsl], in0=kdT[:, tsl], in1=pb[0][:], op=ALU.mult),
                                     reads=[r_pb[0], r_ld], writes=[r_pre[d_]])
                                k.op("pe", lambda e, tsl=tsl, d_=d_: e.matmul(pb[1][:], lhsT=c4b[:, h, :], rhs=CLt[d_][:, tsl], start=True, stop=True),
                                     reads=[r_bk, r_cst], writes=[r_pb[1]])
                                k.op("act", lambda e, tsl=tsl, d_=d_: e.activation(out=CLB[d_][:, tsl], in_=pb[1][:], func=AF.Copy),
                                     reads=[r_pb[1]], writes=[r_pre[d_]])
                            for j in range(nch):
                                k.op("pe", lambda e, j=j, d_=d_: e.matmul(pmisc[:, 0, j:j + 1], lhsT=Bt[d_][:, j * 128:(j + 1) * 128],
                                                                          rhs=c4b[:, 4, h * 32:h * 32 + 1], start=True, stop=True),
                                     reads=[r_bk, r_cst], writes=[r_pmisc])
                            k.op("pe", lambda e, d_=d_: e.matmul(pmisc[:, 1, :], lhsT=c4f[:, h, :], rhs=WI[d_][:], start=True, stop=True),
                                 reads=[r_bk, r_cst], writes=[r_pmisc])
                            k.op("dve", lambda e, d_=d_: e.tensor_copy(out=bcol[d_][:], in_=pmisc[:, 0, :]), reads=[r_pmisc], writes=[r_pre[d_]])
                            k.op("dve", lambda e, d_=d_: e.tensor_copy(out=wib[d_][:], in_=pmisc[:, 1, :]), reads=[r_pmisc], writes=[r_pre[d_]])
                            k.op("pool", lambda e, d_=d_: e.tensor_tensor(out=Kpt[d_][:], in0=kdt[:], in1=bcol[d_][:].unsqueeze(2).to_broadcast([128, nch, 128]),
                                                                          op=ALU.mult), reads=[r_pre[d_], r_ld], writes=[r_pre[d_]])

                        def chain(d_, si, off, S, smp):
                            mT = CB_MF if d_ == 0 else CB_MB
                            c0, c1 = off // 128, (off + S) // 128
                            order = list(range(c0, c1)) if d_ == 0 else list(range(c1 - 1, c0 - 1, -1))
                            cst_ = [sb("cst%d" % b, [128, 129], F32, sh) for b in range(2)]
                            cs = [sb("cs%d" % b, [128, 128], BF16, sh) for b in range(2)]
                            nsb = [sb("nsb%d" % b, [128, 128], BF16, sh) for b in range(2)]
                            r_c = [Res("c%d" % b) for b in range(2)]
                            r_cs = [Res("cs%d" % b) for b in range(2)]
                            rel_c.extend(r_c)
                            rp = r_pre[d_]
                            cur = 0
                            if smp:
                                k.dma("sp", [(cst_[cur][:, 0:128], c0_o[:, i, d_, h, :]), (cst_[cur][:, 128:129], n0_o[i, d_, h])],
                                      r_c[cur], writes=[r_c[cur]])
                            else:
                                k.op("pool", lambda e, cur=cur: e.memset(cst_[cur][:], 0.0), writes=[r_c[cur]])

                            def scaled_state(cur, j):
                                k.op("act", lambda e: e.activation(out=cs[cur][:], in_=cst_[cur][:, 0:128], func=AF.Identity,
                                                                   scale=wib[d_][:, j:j + 1]), reads=[r_c[cur], rp], writes=[r_cs[cur]])
                                k.op("dve", lambda e: e.tensor_scalar(out=nsb[cur][:], in0=cst_[cur][:, 128:129].to_broadcast([128, 128]),
                                                                      scalar1=wib[d_][:, j:j + 1], scalar2=None, op0=ALU.mult),
                                     reads=[r_c[cur], rp], writes=[r_cs[cur]])
                            scaled_state(cur, order[0])
                            for oi, j in enumerate(order):
                                ub = ucnt[0] % 2
                                ucnt[0] += 1
                                P = PA[ub]
                                rP = r_PA[ub]
                                rL = r_PAL[ub]
                                csl = slice(j * 128, (j + 1) * 128)
                                k.op("pe", lambda e: e.matmul(P[:, 0, :], lhsT=KpT[d_][:, csl], rhs=qd[:, csl], start=True, stop=True),
                                     reads=[rp, r_ld], writes=[rP[0], rL])
                                k.op("dve", lambda e: e.tensor_tensor(out=scm[ub][:], in0=P[:, 0, :], in1=cstb[:, mT, :], op=ALU.mult),
                                     reads=[rP[0], r_cst], writes=[r_scm[ub], rL])
                                k.op("pe", lambda e: e.matmul(P[:, 1, :], lhsT=vd1[:, j, 0:128], rhs=scm[ub][:], start=True, stop=False),
                                     reads=[r_ld, r_scm[ub]], writes=[rP[1], rL])
                                k.op("pe", lambda e: e.matmul(P[:, 1, :], lhsT=cs[cur][:], rhs=qd[:, csl], start=False, stop=True),
                                     reads=[r_cs[cur], r_ld], writes=[rP[1], rL])
                                k.op("pe", lambda e: e.matmul(P[:, 2, :], lhsT=cstb[:, CB_ONE, :], rhs=scm[ub][:], start=True, stop=False),
                                     reads=[r_cst, r_scm[ub]], writes=[rP[2], rL])
                                k.op("pe", lambda e: e.matmul(P[:, 2, :], lhsT=nsb[cur][:], rhs=qd[:, csl], start=False, stop=True),
                                     reads=[r_cs[cur], r_ld], writes=[rP[2], rL])
                                k.op("act", lambda e: e.activation(out=dcl[ub][:], in_=P[:, 2, :], func=AF.Abs),
                                     reads=[rP[2]], writes=[r_dcl[ub], rL])
                                k.op("dve", lambda e: e.tensor_tensor(out=dcl[ub][:], in0=dcl[ub][:], in1=CLB[d_][:, csl], op=ALU.max),
                                     reads=[r_dcl[ub], rp], writes=[r_dcl[ub]])
                                k.op("dve", lambda e: e.reciprocal(out=dcl[ub][:], in_=dcl[ub][:]), reads=[r_dcl[ub]], writes=[r_dcl[ub]])
                                k.op("dve", lambda e: e.tensor_tensor(out=Hs[d_][:, csl], in0=P[:, 1, :], in1=dcl[ub][:], op=ALU.mult),
                                     reads=[rP[1], r_dcl[ub]], writes=[r_Hs[d_][j], rL])
                                k.op("pe", lambda e: e.matmul(PC[ub][:], lhsT=Kpt[d_][:, j, :], rhs=vd1[:, j, 0:129], start=True, stop=True),
                                     reads=[rp, r_ld], writes=[r_PC[ub]])
                                nxt = 1 - cur
                                k.op("dve", lambda e, cur=cur, nxt=nxt: e.scalar_tensor_tensor(
                                    out=cst_[nxt][:], in0=cst_[cur][:], scalar=wib[d_][:, j:j + 1], in1=PC[ub][:], op0=ALU.mult, op1=ALU.add),
                                    reads=[r_c[cur], r_PC[ub], rp], writes=[r_c[nxt]])
                                cur = nxt
                                if oi + 1 < len(order):
                                    scaled_state(cur, order[oi + 1])
                                else:
                                    if not smp:
                                        k.dma("sp", [(Cnew[i, si - 1, d_, h], cst_[cur][:, 0:128]), (nnew[i, si - 1, d_, h], cst_[cur][:, 128:129])],
                                              r_c[cur], reads=[r_c[cur]])
                                yield

                        chains = [chain(d_, si, off, S, smp) for d_ in range(2) for si, (off, S, cond, smp) in enumerate(cfg.seqs)]
                        while chains:
                            for c_ in list(chains):
                                try:
                                    next(c_)
                                except StopIteration:
                                    chains.remove(c_)
                        sqh = sb("sqh", [128, TT], F32, sh)
                        hsum = sb("hsum", [128, TT], F32, sh)
                        r_hsum = Res("hsum")
                        rqh = sb("rqh", [128, TT], F32, sh)
                        hm = sb("hm", [128, TT], F32, sh)
                        sod = sb("sod", [128, TT], BF16, sh)
                        ho = [sb("ho%d" % b, [128, TT], BF16, sh) for b in range(2)]
                        r_sqh, r_rqh, r_hm, r_sod = Res("sqh"), Res("rqh"), Res("hm"), Res("sod")
                        r_ho = [Res("ho%d" % b) for b in range(2)]
                        for t in range(NT):
                            tsl = slice(t * TT, (t + 1) * TT)
                            rH = r_Hs[0][t * 4:(t + 1) * 4] + r_Hs[1][t * 4:(t + 1) * 4]
                            k.dma("sp", [(sod[:], SODT_d[h, :, tsl])], r_sod, writes=[r_sod])
                            k.op("pool", lambda e, tsl=tsl: e.tensor_tensor(out=hsum[:], in0=Hs[0][:, tsl], in1=Hs[1][:, tsl], op=ALU.add),
                                 reads=rH, writes=[r_hsum])
                            rH = [r_hsum]
                            k.op("act", lambda e, tsl=tsl: e.activation(out=sqh[:], in_=hsum[:], func=AF.Square), reads=rH, writes=[r_sqh])
                            k.op("pe", lambda e: e.matmul(pb[0][:], lhsT=cstf[:, F128, :], rhs=sqh[:], start=True, stop=True),
                                 reads=[r_sqh, r_cst], writes=[r_pb[0]])
                            k.op("act", lambda e: e.activation(out=rqh[:], in_=pb[0][:], func=AF.Sqrt, bias=eps_t[:], scale=1.0),
                                 reads=[r_pb[0], r_ones], writes=[r_rqh])
                            k.op("dve", lambda e: e.reciprocal(out=rqh[:], in_=rqh[:]), reads=[r_rqh], writes=[r_rqh])
                            k.op("dve", lambda e, tsl=tsl: e.scalar_tensor_tensor(out=hm[:], in0=hsum[:], scalar=paro[:, i, 3:4], in1=rqh[:],
                                                                                 op0=ALU.mult, op1=ALU.mult), reads=rH + [r_rqh, r_cst], writes=[r_hm])
                            b = t % 2
                            k.op("pool", lambda e, b=b: e.tensor_tensor(out=ho[b][:], in0=hm[:], in1=sod[:], op=ALU.mult),
                                 reads=[r_hm, r_sod], writes=[r_ho[b]])
                            k.dma("sp", [(MIX_d[:, 4 + h, tsl], ho[b][:])], r_ho[b], reads=[r_ho[b]], writes=[r_scr])
                        k.barrier()
                        k.release([r_ld, r_sod] + r_ho + rel_c)
                k.release([r_mo])

        first = True
        for l in range(cfg.depth):
            i = l // 2
            ffn_phase(l, 0, xT_in if first else yT)
            first = False
            if cfg.do_mix:
                if l % 2 == 0:
                    even_proj(l, i)
                    even_fnet(i)
                    even_attn(i)
                    out_proj(l, w_out_e[i])
                else:
                    stage = getattr(cfg, "odd_stage", 9)
                    if stage >= 1:
                        odd_proj(l, i)
                    if stage >= 2:
                        odd_attn(l, i)
                    if stage >= 3:
                        odd_mlstm(l, i)
                    if stage >= 4:
                        out_proj(l, w_out_o[i])
            ffn_phase(l, 1, yT)
        k.barrier()
        print("instructions:", k.ninst, "waits:", k.nwait, "dma sems:", k.n_dma_sems, "sems:", len(k.sems))
    return nc


def _fm(x2d):
    return np.ascontiguousarray(x2d.T.reshape(NCH, 128, x2d.shape[0]))


def _tm(xT):
    return np.ascontiguousarray(xT.reshape(D, xT.shape[2]).T)


def _bf(a):
    return np.ascontiguousarray(a.astype(np.float32)).astype(ml_dtypes.bfloat16)


_CONST_CACHE = {}


def make_consts(cfg):
    key = (cfg.NS, cfg.NP)
    if key in _CONST_CACHE:
        return _CONST_CACHE[key]
    f32 = np.float32
    p = np.arange(128)
    cst_f = np.zeros((128, 5, 128), f32)
    for m in range(128):
        if m % 32 < 16:
            cst_f[m + 16, 0, m] = -1.0
        else:
            cst_f[m - 16, 0, m] = 1.0
    cst_f[:, 1, :] = (p[:, None] // 64 == p[None, :] // 64) / 64.0
    cst_f[:, 2, :] = 1.0 / 128.0
    cst_f[:, 3, :] = np.eye(128)
    cst_f[:, 4, :] = 1.0
    cst_b = np.zeros((128, 13, 128), f32)
    ang = 2.0 * np.pi * ((p[:, None] * p[None, :]) % 128) / 128.0
    cst_b[:, 0, :] = np.cos(ang)
    cst_b[:, 1, :] = -np.sin(ang)
    mprev = (p[None, :] <= p[:, None]).astype(f32)
    mnext = (p[:, None] <= p[None, :]).astype(f32)
    for hh in range(4):
        cst_b[:, 2 + hh, :] = mprev
        cst_b[:, 6 + hh, :] = mnext
    cst_b[:, 10, :] = (p[:, None] <= p[None, :])
    cst_b[:, 11, :] = (p[:, None] >= p[None, :])
    cst_b[:, 12, :] = 1.0
    cst4 = np.zeros((4, 4, 128), f32)
    for h in range(4):
        cst4[h, h, :] = 1.0
    cst4b = np.zeros((4, 5, 128), f32)
    cst4b[:, 0:4, :] = cst4
    for h in range(4):
        cst4b[h, 4, h * 32] = 1.0
    NS = cfg.NS
    tok = np.arange(NS)
    row = (tok // 64).astype(np.float64)
    col = (tok % 64).astype(np.float64)
    inv = 10000.0 ** (-np.arange(16, dtype=np.float32) / 16).astype(np.float32)
    d = p % 64
    axis = d // 32
    fr = d % 16
    pos = np.where(axis[:, None] == 0, row[None, :], col[None, :]).astype(np.float32)
    angr = (pos * inv[fr][:, None]).astype(np.float32)
    ropeT = np.stack([np.cos(angr), np.sin(angr)], axis=1).astype(f32)

    def seq_tab(S):
        s = np.arange(S, dtype=np.int64)
        prod = (s[:, None] * s[None, :]) % S
        a = 2.0 * np.pi * prod / S
        sc = 1.0 / math.sqrt(S * 128.0)
        cs = np.stack([np.cos(a) * sc, np.sin(a) * sc], axis=0).astype(f32)
        t = cs.reshape(2, S // 128, 128, S // 256, 256).transpose(3, 2, 1, 0, 4)
        return _bf(t)

    out = dict(cst_f=cst_f, cst_b=_bf(cst_b), cst4=cst4, cst4b=_bf(cst4b), ropeT=np.ascontiguousarray(ropeT),
               tabS=seq_tab(cfg.NS), tabP=seq_tab(cfg.NP))
    _CONST_CACHE[key] = out
    return out


def make_in_maps(cfg, inp, n_cores=8):
    f32 = np.float32
    g = lambda n: np.asarray(inp[n], f32)
    cs = make_consts(cfg)
    b_adaT = np.ascontiguousarray(g("b_ada").reshape(DEPTH, 72, 128).transpose(2, 0, 1))
    g_normT = np.ascontiguousarray(g("g_norm").reshape(DEPTH, 3, NCH, 128).transpose(3, 0, 1, 2))
    qperm = np.concatenate([np.r_[c * 64:(c + 1) * 64, (c + 4) * 64:(c + 5) * 64] for c in range(4)])
    wie = g("w_in_even")
    w_in_e = np.ascontiguousarray(np.concatenate([wie[:, :, qperm], wie[:, :, 512:]], axis=2))
    woe = g("w_out_even")
    w_out_e = np.ascontiguousarray(np.concatenate([woe[:, qperm, :], woe[:, 512:, :]], axis=1))
    p64 = np.arange(128) % 64
    par_e = np.ascontiguousarray(np.stack([g("qn_a")[:, p64], g("kn_a")[:, p64]], axis=-1).transpose(1, 0, 2))
    par_o = np.ascontiguousarray(np.stack([g("qn_c")[:, p64], g("kn_c")[:, p64], g("subln_c"), g("outnorm_d")],
                                          axis=-1).transpose(1, 0, 2))
    bg_o = np.ascontiguousarray(g("b_gate_odd").reshape(2, 4, 4).transpose(2, 0, 1))
    lam_o = np.ascontiguousarray(g("lam_c").reshape(2, 256))
    shared = dict(cs)
    shared.update(w_ada=g("w_ada"), b_adaT=b_adaT, g_normT=g_normT, w_ffn_in=g("w_ffn_in"), w_ffn_out=g("w_ffn_out"),
                  w_in_e=w_in_e, w_out_e=w_out_e, par_e=par_e, sink_e=g("sink_a"), w_in_o=g("w_in_odd"),
                  w_out_o=g("w_out_odd"), par_o=par_o, bg_o=bg_o, lam_o=lam_o)
    maps = []
    for core in range(n_cores):
        b = core // 2
        toks = np.concatenate([g("x_sample")[b], g("x_prompt")[2 * core], g("x_prompt")[2 * core + 1]], axis=0)
        cT = np.stack([g("c_ctx").reshape(NCH, 128).T, g("c")[b].reshape(NCH, 128).T], axis=-1)
        cka = g("cache_k_a")[b]
        kctx_e = np.ascontiguousarray(cka.transpose(0, 2, 3, 1).reshape(2, 128, PAST))
        cva = g("cache_v_a")[b]
        vctx_e = np.zeros((2, 128, 2, 2, 128), f32)
        cv = cva.reshape(2, 2, 128, 2, 64)
        vctx_e[:, :, :, 0, 0:64] = cv[:, :, :, 0, :].transpose(0, 2, 1, 3)
        vctx_e[:, :, :, 0, 64] = 1.0
        vctx_e[:, :, :, 1, 64:128] = cv[:, :, :, 1, :].transpose(0, 2, 1, 3)
        vctx_e[:, :, :, 1, 0] = 1.0
        ckc = g("cache_k_c")[b]
        kctx_o = np.ascontiguousarray(ckc.transpose(0, 3, 4, 2, 1).reshape(2, 128, 4, PAST))
        cvc = g("cache_v_c")[b]
        vctx_o = np.ascontiguousarray(cvc.reshape(2, 2, 128, 512).transpose(0, 2, 1, 3))
        sC = g("state_C_d")[b]
        c0_o = np.ascontiguousarray(sC.transpose(3, 0, 1, 2, 4))
        sn = g("state_n_d")[b]
        n0_o = np.ascontiguousarray(sn.reshape(2, 2, 4, 128, 1))
        sm = g("state_m_d")[b]
        m0_o = np.ascontiguousarray(sm.transpose(2, 0, 1))
        m = dict(shared)
        m.update(xT=_fm(toks), cT=np.ascontiguousarray(cT), kctx_e=kctx_e, vctx_e=vctx_e, kctx_o=kctx_o,
                 vctx_o=vctx_o, c0_o=c0_o, n0_o=n0_o, m0_o=m0_o)
        maps.append(m)
    return maps


def assemble(cfg, outs, B, n_cores=8):
    NS, NP = cfg.NS, cfg.NP
    f32 = np.float32
    yp = np.zeros((B, NP, D), f32)
    ys = np.zeros((max(1, n_cores // 2), NS, D), f32)
    ka = np.zeros((B, 2, NP, 2, 64), f32)
    va = np.zeros((B, 2, NP, 2, 64), f32)
    kc = np.zeros((B, 2, NP, 4, 2, 64), f32)
    vc = np.zeros((B, 2, NP, 4, 128), f32)
    Cd = np.zeros((B, 2, 2, 4, 128, 128), f32)
    nd = np.zeros((B, 2, 2, 4, 128), f32)
    md = np.zeros((B, 2, 2, 4), f32)
    for core in range(n_cores):
        o = outs[core]
        y = _tm(np.asarray(o["yT"], f32))
        if core % 2 == 0:
            ys[core // 2] = y[:NS]
        for s in range(2):
            bi = 2 * core + s
            yp[bi] = y[NS + s * NP:NS + (s + 1) * NP]
            tsl = slice(s * NP, (s + 1) * NP)
            kk = np.asarray(o["knew_a"], f32)[:, :, tsl]
            ka[bi] = kk.reshape(2, 2, 64, NP).transpose(0, 3, 1, 2)
            vv = np.asarray(o["vnew_a"], f32)
            vv = vv.transpose(0, 2, 1, 3).reshape(2, 2 * NP, 2, 64)[:, tsl]
            va[bi] = vv
            kk = np.asarray(o["knew_c"], f32)[:, :, :, tsl]
            kc[bi] = kk.reshape(2, 2, 64, 4, NP).transpose(0, 4, 3, 1, 2)
            vv = np.asarray(o["vnew_c"], f32).transpose(0, 2, 1, 3).reshape(2, 2 * NP, 4, 128)[:, tsl]
            vc[bi] = vv
            Cd[bi] = np.asarray(o["Cnew"], f32)[:, s]
            nd[bi] = np.asarray(o["nnew"], f32)[:, s, :, :, :, 0]
            md[bi] = np.asarray(o["mnew"], f32)[:, :, s, :].transpose(1, 2, 0)
    return (yp, ys, ka, va, kc, vc, Cd, nd, md)


_NC_CACHE = {}


def kernel(**inputs):
    inp = {k_: np.asarray(v) for k_, v in inputs.items()}
    cfg = Cfg()
    if "nc" not in _NC_CACHE:
        _NC_CACHE["nc"] = build(cfg)
    nc = _NC_CACHE["nc"]
    maps = make_in_maps(cfg, inp)
    res = run_bass_kernel_spmd(nc, maps, core_ids=list(range(8)))
    return assemble(cfg, res.results, inp["x_prompt"].shape[0])
```

```python
import math
import re
from contextlib import ExitStack
import numpy as np
import ml_dtypes
import concourse.bass as bass
import concourse.mybir as mybir
from concourse.bass_utils import run_bass_kernel_spmd

F32 = mybir.dt.float32
BF16 = mybir.dt.bfloat16
AF = mybir.ActivationFunctionType
ALU = mybir.AluOpType
AX = mybir.AxisListType

D = 1024
NCH = 8
DEPTH = 4
DFF = 2816
NFC = 22
EPS = 1e-6
HD = 64
PAST = 256
NEG = -30000.0


_PS_RE = re.compile(r"^(pm|msb|pg\d|pu\d|py\d|pq\d|pv|pvv|pg4|pms|prot|pf\d|pS\d|pO\d|pB\d|pD\d|pb\d|pmisc|PA\d_\d|PC\d)$")


class Res:
    __slots__ = ("name", "lw", "rd", "sem", "cnt", "ldma", "ps", "lock")

    def __init__(self, name, lock=False):
        self.name = name
        self.ps = bool(_PS_RE.match(name))
        self.lock = lock
        self.lw = None
        self.rd = []
        self.sem = None
        self.cnt = 0
        self.ldma = None


class Ev:
    __slots__ = ("key", "val", "snap", "eng")

    def __init__(self, key, val, snap, eng):
        self.key, self.val, self.snap, self.eng = key, val, snap, eng


class K:
    EPOCH = 20000

    def __init__(self, nc, stack):
        self.nc = nc
        self.stack = stack
        self.eng = {"pe": nc.tensor, "act": nc.scalar, "dve": nc.vector, "pool": nc.gpsimd, "sp": nc.sync}
        self.seq = {e: 0 for e in self.eng}
        self.know = {e: {} for e in self.eng}
        self.sems = {}
        self.dma_free = []
        self.n_dma_sems = 0
        self.dma_cnt = {}
        self.nwait = 0
        self.ninst = 0

    def sem(self, key):
        s = self.sems.get(key)
        if s is None:
            s = self.stack.enter_context(self.nc.semaphore("s_%s" % (str(key).replace(" ", ""))))
            self.sems[key] = s
        return s

    def _need(self, e, ev):
        if ev is None:
            return
        kn = self.know[e]
        if kn.get(ev.key, 0) >= ev.val:
            return
        self.eng[e].wait_ge(self.sem(ev.key), ev.val)
        self.nwait += 1
        kn[ev.key] = ev.val
        for k2, v2 in ev.snap.items():
            if kn.get(k2, 0) < v2:
                kn[k2] = v2

    def _deps(self, e, reads, writes):
        for r in reads:
            self._need(e, r.lw)
            if r.ps:
                for ev in r.rd:
                    if ev.eng != e:
                        self._need(e, ev)
        for w in writes:
            lw = w.lw
            if lw is not None and not (lw.eng == e and (e == "pe" or w.lock)):
                self._need(e, lw)
            for ev in w.rd:
                if ev.eng != e or e in ("sp", "pool"):
                    self._need(e, ev)

    def _commit(self, ev, reads, writes):
        for r in reads:
            r.rd.append(ev)
        for w in writes:
            w.lw = ev
            w.rd = []

    def op(self, e, fn, reads=(), writes=()):
        self._deps(e, reads, writes)
        n = self.seq[e]
        key = (e, n // self.EPOCH)
        val = n % self.EPOCH + 1
        ins = fn(self.eng[e])
        ins.then_inc(self.sem(key), 1)
        self.seq[e] = n + 1
        self.ninst += 1
        ev = Ev(key, val, dict(self.know[e]), e)
        self._commit(ev, reads, writes)
        return ev

    def dma(self, q, pairs, own, reads=(), writes=()):
        if own.sem is None:
            if self.dma_free:
                own.sem = self.dma_free.pop()
            else:
                own.sem = ("dma", self.n_dma_sems)
                self.n_dma_sems += 1
        self._need(q, own.ldma)
        self._deps(q, reads, writes)
        s = self.sem(own.sem)
        c = self.dma_cnt.get(own.sem, 0)
        for (o, i) in pairs:
            self.eng[q].dma_start(out=o, in_=i).then_inc(s, 16)
            c += 1
            self.ninst += 1
        self.dma_cnt[own.sem] = c
        ev = Ev(own.sem, 16 * c, dict(self.know[q]), "dma")
        own.ldma = ev
        self._commit(ev, reads, writes)
        return ev

    def release(self, ress):
        for r in ress:
            if r.sem is not None:
                self.dma_free.append(r.sem)
                r.sem = None

    def barrier(self):
        evs = []
        for e in self.eng:
            n = self.seq[e]
            if n > 0:
                evs.append(Ev((e, (n - 1) // self.EPOCH), (n - 1) % self.EPOCH + 1, {}, e))
        for key, c in self.dma_cnt.items():
            if c > 0:
                evs.append(Ev(key, 16 * c, {}, "dma"))
        for e in self.eng:
            for ev in evs:
                if ev.eng == e and e != "dma":
                    pass
                self._need(e, ev)


class Cfg:
    def __init__(self, ns=4096, npr=256, depth=DEPTH, do_mix=True):
        self.NS = ns
        self.NP = npr
        self.T = ns + 2 * npr
        self.depth = depth
        self.do_mix = do_mix
        self.TT = 512
        assert self.T % self.TT == 0 and ns % self.TT == 0
        self.NT = self.T // self.TT
        self.seqs = [(0, ns, 1, True), (ns, npr, 0, False), (ns + npr, npr, 0, False)]


def build(cfg):
    nc = bass.Bass("TRN2", target_bir_lowering=False)
    T, TT, NT = cfg.T, cfg.TT, cfg.NT
    dt = nc.dram_tensor

    xT_in = dt("xT", [NCH, 128, T], F32, kind="ExternalInput").ap()
    cT_in = dt("cT", [128, NCH, 2], F32, kind="ExternalInput").ap()
    w_ada = dt("w_ada", [DEPTH, D, 9 * D], F32, kind="ExternalInput").ap()
    b_adaT = dt("b_adaT", [128, DEPTH, 72], F32, kind="ExternalInput").ap()
    g_normT = dt("g_normT", [128, DEPTH, 3, NCH], F32, kind="ExternalInput").ap()
    w_ffn_in = dt("w_ffn_in", [DEPTH, 2, D, 2 * DFF], F32, kind="ExternalInput").ap()
    w_ffn_out = dt("w_ffn_out", [DEPTH, 2, DFF, D], F32, kind="ExternalInput").ap()
    yT = dt("yT", [NCH, 128, T], F32, kind="ExternalOutput").ap()
    NS, NP = cfg.NS, cfg.NP
    NTK = T // 128
    NPT = 2 * NP
    cst_f = dt("cst_f", [128, 5, 128], F32, kind="ExternalInput").ap()
    cst_b = dt("cst_b", [128, 13, 128], BF16, kind="ExternalInput").ap()
    cst4 = dt("cst4", [4, 4, 128], F32, kind="ExternalInput").ap()
    cst4b = dt("cst4b", [4, 5, 128], BF16, kind="ExternalInput").ap()
    ropeT = dt("ropeT", [128, 2, NS], F32, kind="ExternalInput").ap()
    tabS = dt("tabS", [NS // 256, 128, NS // 128, 2, 256], BF16, kind="ExternalInput").ap()
    tabP = dt("tabP", [NP // 256, 128, NP // 128, 2, 256], BF16, kind="ExternalInput").ap()
    w_in_e = dt("w_in_e", [2, D, 1280], F32, kind="ExternalInput").ap()
    w_out_e = dt("w_out_e", [2, D, D], F32, kind="ExternalInput").ap()
    par_e = dt("par_e", [128, 2, 2], F32, kind="ExternalInput").ap()
    sink_e = dt("sink_e", [2, 8], F32, kind="ExternalInput").ap()
    kctx_e = dt("kctx_e", [2, 128, PAST], F32, kind="ExternalInput").ap()
    vctx_e = dt("vctx_e", [2, 128, 2, 2, 128], F32, kind="ExternalInput").ap()
    w_in_o = dt("w_in_o", [2, D, 3600], F32, kind="ExternalInput").ap()
    w_out_o = dt("w_out_o", [2, D, D], F32, kind="ExternalInput").ap()
    par_o = dt("par_o", [128, 2, 4], F32, kind="ExternalInput").ap()
    bg_o = dt("bg_o", [4, 2, 4], F32, kind="ExternalInput").ap()
    lam_o = dt("lam_o", [2, 256], F32, kind="ExternalInput").ap()
    kctx_o = dt("kctx_o", [2, 128, 4, PAST], F32, kind="ExternalInput").ap()
    vctx_o = dt("vctx_o", [2, 128, 2, 512], F32, kind="ExternalInput").ap()
    c0_o = dt("c0_o", [128, 2, 2, 4, 128], F32, kind="ExternalInput").ap()
    n0_o = dt("n0_o", [2, 2, 4, 128, 1], F32, kind="ExternalInput").ap()
    m0_o = dt("m0_o", [4, 2, 2], F32, kind="ExternalInput").ap()
    knew_a = dt("knew_a", [2, 128, NPT], F32, kind="ExternalOutput").ap()
    vnew_a = dt("vnew_a", [2, 128, NPT // 128, 128], F32, kind="ExternalOutput").ap()
    knew_c = dt("knew_c", [2, 128, 4, NPT], F32, kind="ExternalOutput").ap()
    vnew_c = dt("vnew_c", [2, 128, NPT // 128, 512], F32, kind="ExternalOutput").ap()
    Cnew = dt("Cnew", [2, 2, 2, 4, 128, 128], F32, kind="ExternalOutput").ap()
    nnew = dt("nnew", [2, 2, 2, 4, 128, 1], F32, kind="ExternalOutput").ap()
    mnew = dt("mnew", [4, 2, 2, 2], F32, kind="ExternalOutput").ap()
    QT_d = dt("QT_d", [128, 4, T], BF16, kind="Internal").ap()
    KT_d = dt("KT_d", [128, 4, T], BF16, kind="Internal").ap()
    VA_d = dt("VA_d", [128, NTK, 2, 128], BF16, kind="Internal").ap()
    PQ_d = dt("PQ_d", [128, NTK, 4, 256], BF16, kind="Internal").ap()
    MIX_d = dt("MIX_d", [128, 8, T], BF16, kind="Internal").ap()
    VCt_d = dt("VCt_d", [128, NTK, 512], BF16, kind="Internal").ap()
    QDT_d = dt("QDT_d", [4, 128, T], BF16, kind="Internal").ap()
    KDT_d = dt("KDT_d", [4, 128, T], BF16, kind="Internal").ap()
    KDt_d = dt("KDt_d", [4, 128, NTK, 128], BF16, kind="Internal").ap()
    VD1t_d = dt("VD1t_d", [4, 128, NTK, 160], BF16, kind="Internal").ap()
    SODT_d = dt("SODT_d", [4, 128, T], BF16, kind="Internal").ap()
    G_d = dt("G_d", [4, 4, T], F32, kind="Internal").ap()

    with ExitStack() as gs:
        k = K(nc, gs)

        uid = [0]

        def sb(name, shape, dtype, st=gs):
            uid[0] += 1
            return st.enter_context(nc.sbuf_tensor("%s_%d" % (name, uid[0]), shape, dtype))

        def ps(name, shape, dtype, st=gs):
            uid[0] += 1
            return st.enter_context(nc.psum_tensor("%s_%d" % (name, uid[0]), shape, dtype))

        ones_bf = sb("ones_bf", [128, 128], BF16)
        r_ones = Res("ones_bf")
        k.op("pool", lambda e: e.memset(ones_bf[:], 1.0 / D), writes=[r_ones])
        eps_t = sb("eps_t", [128, 1], F32)
        k.op("pool", lambda e: e.memset(eps_t[:], EPS), writes=[r_ones])
        one_t = sb("one_t", [128, 1], F32)
        k.op("pool", lambda e: e.memset(one_t[:], 1.0), writes=[r_ones])
        cTs = sb("cTs", [128, NCH, 2], F32)
        scT = sb("scT", [128, NCH, 2], BF16)
        bada = sb("bada", [128, DEPTH, 72], F32)
        gnrm = sb("gnrm", [128, DEPTH, 3, NCH], F32)
        r_par = Res("par")
        k.dma("sp", [(cTs[:], cT_in), (bada[:], b_adaT), (gnrm[:], g_normT)], r_par, writes=[r_par])
        r_scT = Res("scT")
        k.op("act", lambda e: e.activation(out=scT[:], in_=cTs[:], func=AF.Silu), reads=[r_par], writes=[r_scT])
        MOD = sb("MOD", [128, DEPTH, 9, NCH, 2], F32)
        r_mod = Res("MOD")
        AM = sb("AM", [128, DEPTH, 3, NCH, 2], F32)
        GM = sb("GM", [128, DEPTH, 3, NCH, 2], F32)
        r_am = Res("AM")

        with ExitStack() as st:
            wa = [sb("wa%d" % i, [128, NCH, D], BF16, st) for i in range(2)]
            r_wa = [Res("wa%d" % i) for i in range(2)]
            pm = ps("pm", [128, 8, 2], F32, st)
            r_pm = Res("pm")
            it = 0
            for l in range(cfg.depth):
                for j in range(9):
                    b = it % 2
                    it += 1
                    k.dma("pool", [(wa[b][:, kc, :], w_ada[l, kc * 128:(kc + 1) * 128, j * D:(j + 1) * D])
                                   for kc in range(NCH)], r_wa[b], writes=[r_wa[b]])
                    for cc in range(NCH):
                        for kc in range(NCH):
                            k.op("pe", lambda e, cc=cc, kc=kc, b=b: e.matmul(
                                pm[:, cc, :], lhsT=wa[b][:, kc, cc * 128:(cc + 1) * 128], rhs=scT[:, kc, :],
                                start=(kc == 0), stop=(kc == NCH - 1)),
                                reads=[r_wa[b], r_scT], writes=[r_pm])
                    k.op("dve", lambda e, l=l, j=j: e.tensor_tensor(
                        out=MOD[:, l, j, :, :], in0=pm[:],
                        in1=bada[:, l, j * 8:(j + 1) * 8].unsqueeze(2).to_broadcast([128, 8, 2]),
                        op=ALU.add), reads=[r_pm, r_par], writes=[r_mod])
            for l in range(cfg.depth):
                for w in range(3):
                    k.op("dve", lambda e, l=l, w=w: e.scalar_tensor_tensor(
                        out=AM[:, l, w, :, :], in0=MOD[:, l, 3 * w + 1, :, :], scalar=1.0,
                        in1=gnrm[:, l, w, :].unsqueeze(2).to_broadcast([128, 8, 2]),
                        op0=ALU.add, op1=ALU.mult), reads=[r_mod, r_par], writes=[r_am])
                    k.op("dve", lambda e, l=l, w=w: e.tensor_scalar(
                        out=GM[:, l, w, :, :], in0=MOD[:, l, 3 * w + 2, :, :],
                        scalar1=(1.0 if w == 1 else 0.5), scalar2=None, op0=ALU.mult),
                        reads=[r_mod], writes=[r_am])
            k.barrier()
            k.release(r_wa)

        def norm_mod(st_x, r_x, xt, ht, r_h, l, w, cond, tmp):
            sq, r_sq, msb, r_ms, rstd, r_rstd, u, r_u = tmp
            for c in range(NCH):
                b = c % 2
                k.op("act", lambda e, c=c, b=b: e.activation(out=sq[b][:], in_=xt[:, c, :], func=AF.Square),
                     reads=[r_x], writes=[r_sq[b]])
                k.op("pe", lambda e, c=c, b=b: e.matmul(msb[:], lhsT=ones_bf[:], rhs=sq[b][:],
                                                       start=(c == 0), stop=(c == NCH - 1)),
                     reads=[r_sq[b], r_ones], writes=[r_ms])
            k.op("act", lambda e: e.activation(out=rstd[:], in_=msb[:], func=AF.Ln, bias=eps_t[:], scale=1.0),
                 reads=[r_ms, r_ones], writes=[r_rstd])
            k.op("act", lambda e: e.activation(out=rstd[:], in_=rstd[:], func=AF.Exp, scale=-0.5), reads=[r_rstd], writes=[r_rstd])
            for c in range(NCH):
                b = c % 2
                k.op("dve", lambda e, c=c, b=b: e.scalar_tensor_tensor(
                    out=u[b][:], in0=xt[:, c, :], scalar=AM[:, l, w, c, cond:cond + 1], in1=rstd[:],
                    op0=ALU.mult, op1=ALU.mult), reads=[r_x, r_rstd, r_am], writes=[r_u[b]])
                k.op("act", lambda e, c=c, b=b: e.activation(
                    out=ht[:, c, :], in_=u[b][:], func=AF.Identity,
                    bias=MOD[:, l, 3 * w, c, cond:cond + 1], scale=1.0),
                    reads=[r_u[b], r_mod], writes=[r_h])

        def alloc_norm_tmp(st):
            sq = [sb("sq%d" % i, [128, TT], BF16, st) for i in range(2)]
            r_sq = [Res("sq%d" % i) for i in range(2)]
            msb = ps("msb", [128, TT], F32, st)
            rstd = sb("rstd", [128, TT], F32, st)
            u = [sb("u%d" % i, [128, TT], F32, st) for i in range(2)]
            r_u = [Res("u%d" % i) for i in range(2)]
            return (sq, r_sq, msb, Res("msb"), rstd, Res("rstd"), u, r_u)

        def tile_cond(t):
            return 1 if t * TT < cfg.NS else 0

        def ffn_phase(l, j, src):
            w = 0 if j == 0 else 2
            with ExitStack() as st:
                wi = sb("wi", [128, NCH, 2 * DFF], BF16, st)
                wo = sb("wo", [128, NFC, D], BF16, st)
                r_wi = [Res("wi%d" % i) for i in range(NCH)]
                r_wo = [Res("wo%d" % i) for i in range(2)]
                for kc in range(NCH):
                    k.dma("pool", [(wi[:, kc, :], w_ffn_in[l, j, kc * 128:(kc + 1) * 128, :])], r_wi[kc],
                          writes=[r_wi[kc]])
                for hf in range(2):
                    k.dma("pool", [(wo[:, fc, :], w_ffn_out[l, j, fc * 128:(fc + 1) * 128, :])
                                   for fc in range(hf * 11, hf * 11 + 11)], r_wo[hf], writes=[r_wo[hf]])
                xt = sb("xt", [128, NCH, TT], F32, st)
                r_x = Res("xt")
                ht = sb("ht", [128, NCH, TT], BF16, st)
                r_h = Res("ht")
                tmp = alloc_norm_tmp(st)
                sg = [sb("sg%d" % i, [128, TT], F32, st) for i in range(2)]
                r_sg = [Res("sg%d" % i) for i in range(2)]
                act = sb("act", [128, NFC, TT], BF16, st)
                r_act = [Res("act%d" % i) for i in range(NFC)]
                pg = [ps("pg%d" % i, [128, TT], F32, st) for i in range(2)]
                pu = [ps("pu%d" % i, [128, TT], F32, st) for i in range(2)]
                py = [ps("py%d" % i, [128, TT], F32, st) for i in range(2)]
                r_pg = [Res("pg%d" % i) for i in range(2)]
                r_pu = [Res("pu%d" % i) for i in range(2)]
                r_py = [Res("py%d" % i) for i in range(2)]
                r_yT = Res("yT")
                for t in range(NT):
                    cond = tile_cond(t)
                    tsl = slice(t * TT, (t + 1) * TT)
                    k.dma("sp", [(xt[:, c, :], src[c, :, tsl]) for c in range(NCH)], r_x,
                          reads=[r_yT], writes=[r_x])
                    norm_mod(st, r_x, xt, ht, r_h, l, w, cond, tmp)
                    for fc in range(NFC):
                        b = fc % 2
                        for kc in range(NCH):
                            k.op("pe", lambda e, fc=fc, kc=kc, b=b: e.matmul(
                                pg[b][:], lhsT=wi[:, kc, fc * 128:(fc + 1) * 128], rhs=ht[:, kc, :],
                                start=(kc == 0), stop=(kc == NCH - 1)),
                                reads=[r_wi[kc], r_h], writes=[r_pg[b]])
                        for kc in range(NCH):
                            k.op("pe", lambda e, fc=fc, kc=kc, b=b: e.matmul(
                                pu[b][:], lhsT=wi[:, kc, DFF + fc * 128:DFF + (fc + 1) * 128], rhs=ht[:, kc, :],
                                start=(kc == 0), stop=(kc == NCH - 1)),
                                reads=[r_wi[kc], r_h], writes=[r_pu[b]])
                        k.op("act", lambda e, b=b: e.activation(out=sg[b][:], in_=pg[b][:], func=AF.Silu),
                             reads=[r_pg[b]], writes=[r_sg[b]])
                        k.op("dve", lambda e, b=b, fc=fc: e.tensor_tensor(
                            out=act[:, fc, :], in0=pu[b][:], in1=sg[b][:], op=ALU.mult),
                            reads=[r_pu[b], r_sg[b]], writes=[r_act[fc]])
                    for dc in range(NCH):
                        b = dc % 2
                        for fc in range(NFC):
                            k.op("pe", lambda e, fc=fc, dc=dc, b=b: e.matmul(
                                py[b][:], lhsT=wo[:, fc, dc * 128:(dc + 1) * 128], rhs=act[:, fc, :],
                                start=(fc == 0), stop=(fc == NFC - 1)),
                                reads=[r_wo[fc // 11], r_act[fc]], writes=[r_py[b]])
                        k.op("dve", lambda e, dc=dc, b=b: e.scalar_tensor_tensor(
                            out=xt[:, dc, :], in0=py[b][:], scalar=GM[:, l, w, dc, cond:cond + 1],
                            in1=xt[:, dc, :], op0=ALU.mult, op1=ALU.add),
                            reads=[r_py[b], r_am, r_x], writes=[r_x])
                    k.dma("sp", [(yT[c, :, tsl], xt[:, c, :]) for c in range(NCH)], r_x,
                          reads=[r_x], writes=[r_yT])
                k.barrier()
                k.release(r_wi + r_wo + [r_x])

        cstf = sb("cstf", [128, 5, 128], F32)
        cstb = sb("cstb", [128, 13, 128], BF16)
        c4f = sb("c4f", [4, 4, 128], F32)
        c4b = sb("c4b", [4, 5, 128], BF16)
        pare = sb("pare", [128, 2, 2], F32)
        paro = sb("paro", [128, 2, 4], F32)
        bgo = sb("bgo", [4, 2, 4], F32)
        m0t = sb("m0t", [4, 2, 2], F32)
        r_cst = Res("cst")
        k.dma("sp", [(cstf[:], cst_f), (cstb[:], cst_b), (c4f[:], cst4), (c4b[:], cst4b), (pare[:], par_e),
                     (paro[:], par_o), (bgo[:], bg_o), (m0t[:], m0_o)], r_cst, writes=[r_cst])
        RM, BD, F128, IDN, ONESF = 0, 1, 2, 3, 4
        CB_CS, CB_MP, CB_MN, CB_MF, CB_MB, CB_ONE = 0, 2, 6, 10, 11, 12

        def load_w_bf(st, name, src2d, ncols):
            ncp = (ncols + 31) // 32 * 32
            wt = sb(name, [128, NCH, ncp], BF16, st)
            r = Res(name)
            k.dma("pool", [(wt[:, kc, 0:ncols], src2d[kc * 128:(kc + 1) * 128, :]) for kc in range(NCH)], r, writes=[r])
            return wt, r

        def proj_fm(wt, r_w, col0, ht, r_h, pq, r_pq, m=128):
            for kc in range(NCH):
                k.op("pe", lambda e, kc=kc: e.matmul(pq, lhsT=wt[:, kc, col0:col0 + m], rhs=ht[:, kc, :],
                                                     start=(kc == 0), stop=(kc == NCH - 1)),
                     reads=[r_w, r_h], writes=[r_pq])

        def proj_tm(wt, r_w, col0, ncols, ht, r_h, sub, pv, r_pv):
            for kc in range(NCH):
                k.op("pe", lambda e, kc=kc: e.matmul(pv, lhsT=ht[:, kc, sub * 128:(sub + 1) * 128],
                                                     rhs=wt[:, kc, col0:col0 + ncols],
                                                     start=(kc == 0), stop=(kc == NCH - 1)),
                     reads=[r_w, r_h], writes=[r_pv])

        class QKN:
            def __init__(self, st):
                self.sets = []
                for z in range(2):
                    d_ = dict(
                        sqq=sb("sqq%d" % z, [128, TT], F32, st), rq=sb("rq%d" % z, [128, TT], F32, st),
                        qn=sb("qn%d" % z, [128, TT], F32, st), t1=sb("t1%d" % z, [128, TT], F32, st),
                        t2=sb("t2%d" % z, [128, TT], F32, st), pms=ps("pms", [128, TT], F32, st),
                        prot=ps("prot", [128, TT], F32, st))
                    d_["r"] = {n: Res(n) for n in ("sqq", "rq", "qn", "t1", "t2", "pms", "prot")}
                    self.sets.append(d_)
                self.n = 0
                self.last = None

            def all_qn_res(self):
                return [d_["r"]["qn"] for d_ in self.sets]

            def run(self, pq, r_pq, gain, rope, out_bf, r_out, r_rt=None):
                S_ = self.sets[self.n % 2]
                self.n += 1
                self.last = S_
                r = S_["r"]
                sqq, rq, qn, t1, t2, pms, prot = (S_[n_] for n_ in ("sqq", "rq", "qn", "t1", "t2", "pms", "prot"))
                k.op("act", lambda e: e.activation(out=sqq[:], in_=pq, func=AF.Square),
                     reads=[r_pq], writes=[r["sqq"]])
                k.op("pe", lambda e: e.matmul(pms[:], lhsT=cstf[:, BD, :], rhs=sqq[:], start=True, stop=True),
                     reads=[r["sqq"], r_cst], writes=[r["pms"]])
                k.op("act", lambda e: e.activation(out=rq[:], in_=pms[:], func=AF.Ln, bias=eps_t[:], scale=1.0),
                     reads=[r["pms"], r_ones], writes=[r["rq"]])
                k.op("act", lambda e: e.activation(out=rq[:], in_=rq[:], func=AF.Exp, scale=-0.5), reads=[r["rq"]], writes=[r["rq"]])
                k.op("dve", lambda e: e.scalar_tensor_tensor(out=qn[:], in0=pq, scalar=gain, in1=rq[:],
                                                             op0=ALU.mult, op1=ALU.mult),
                     reads=[r_pq, r["rq"], r_cst], writes=[r["qn"]])
                if rope is not None:
                    k.op("pe", lambda e: e.matmul(prot[:], lhsT=cstf[:, RM, :], rhs=qn[:], start=True, stop=True),
                         reads=[r["qn"], r_cst], writes=[r["prot"]])
                    k.op("pool", lambda e: e.tensor_tensor(out=t1[:], in0=qn[:], in1=rope[:, 0, :], op=ALU.mult),
                         reads=[r["qn"], r_rt], writes=[r["t1"]])
                    k.op("dve", lambda e: e.tensor_tensor(out=t2[:], in0=prot[:], in1=rope[:, 1, :], op=ALU.mult),
                         reads=[r["prot"], r_rt], writes=[r["t2"]])
                    k.op("pool", lambda e: e.tensor_tensor(out=out_bf, in0=t1[:], in1=t2[:], op=ALU.add),
                         reads=[r["t1"], r["t2"]], writes=[r_out])
                else:
                    k.op("act", lambda e: e.activation(out=out_bf, in_=qn[:], func=AF.Copy),
                         reads=[r["qn"]], writes=[r_out])

        def is_sample_tile(t):
            return t * TT < NS

        def even_proj(l, i):
            with ExitStack() as st:
                wie, r_wie = load_w_bf(st, "wie", w_in_e[i], 1280)
                xt = sb("xt", [128, NCH, TT], F32, st)
                r_x = Res("xt")
                ht = sb("ht", [128, NCH, TT], BF16, st)
                r_h = Res("ht")
                tmp = alloc_norm_tmp(st)
                qk = QKN(st)
                pq = [ps("pq%d" % b, [128, TT], F32, st) for b in range(2)]
                r_pq = [Res("pq%d" % b) for b in range(2)]
                pv = ps("pv", [128, 256], F32, st)
                r_pv = Res("pv")
                rt = sb("rt", [128, 2, TT], F32, st)
                r_rt = Res("rt")
                qo = sb("qo", [128, 5, TT], BF16, st)
                r_qo = Res("qo")
                va = sb("va", [128, 4, 2, 128], BF16, st)
                r_va = Res("va")
                v32 = sb("v32", [128, 4, 128], F32, st)
                r_v32 = Res("v32")
                ut = [sb("ut%d" % b, [128, TT], BF16, st) for b in range(2)]
                r_ut = [Res("ut%d" % b) for b in range(2)]
                pqt = sb("pqt", [128, 4, 4, 256], BF16, st)
                r_pqt = Res("pqt")
                k.op("pool", lambda e: e.memset(va[:], 0.0), writes=[r_va])
                k.op("pool", lambda e: e.memset(va[:, :, 0, 64:65], 1.0), writes=[r_va])
                k.op("pool", lambda e: e.memset(va[:, :, 1, 0:1], 1.0), writes=[r_va])
                r_scr = Res("scr_e")
                r_yT = Res("yT")
                for t in range(NT):
                    smp = is_sample_tile(t)
                    cond = 1 if smp else 0
                    tsl = slice(t * TT, (t + 1) * TT)
                    k.dma("sp", [(xt[:, c, :], yT[c, :, tsl]) for c in range(NCH)], r_x, reads=[r_yT], writes=[r_x])
                    if smp:
                        k.dma("sp", [(rt[:], ropeT[:, :, tsl])], r_rt, writes=[r_rt])
                    norm_mod(st, r_x, xt, ht, r_h, l, 1, cond, tmp)
                    for blk in range(5):
                        b = blk % 2
                        proj_fm(wie, r_wie, blk * 128, ht, r_h, pq[b][:], r_pq[b])
                        gain = pare[:, i, (0 if blk < 4 else 1):(1 if blk < 4 else 2)]
                        qk.run(pq[b][:], r_pq[b], gain, rt if smp else None, qo[:, blk, :], r_qo, r_rt)
                        if blk == 4 and not smp:
                            p0 = t * TT - NS
                            k.dma("sp", [(knew_a[i, :, p0:p0 + TT], qk.last["qn"][:])], qk.last["r"]["qn"], reads=[qk.last["r"]["qn"]])
                    k.dma("sp", [(QT_d[:, :, tsl], qo[:, 0:4, :]), (KT_d[:, 0, tsl], qo[:, 4, :])], r_qo,
                          reads=[r_qo], writes=[r_scr])
                    for sub in range(4):
                        proj_tm(wie, r_wie, 640, 128, ht, r_h, sub, pv[:, 0:128], r_pv)
                        k.op("act", lambda e, sub=sub: e.activation(out=va[:, sub, 0, 0:64], in_=pv[:, 0:64], func=AF.Copy),
                             reads=[r_pv], writes=[r_va])
                        k.op("dve", lambda e, sub=sub: e.tensor_copy(out=va[:, sub, 1, 64:128], in_=pv[:, 64:128]),
                             reads=[r_pv], writes=[r_va])
                        if not smp:
                            k.op("dve", lambda e, sub=sub: e.tensor_copy(out=v32[:, sub, :], in_=pv[:, 0:128]),
                                 reads=[r_pv], writes=[r_v32])
                    k.dma("sp", [(VA_d[:, t * 4:(t + 1) * 4, :, :], va[:])], r_va, reads=[r_va], writes=[r_scr])
                    if not smp:
                        p0 = (t * TT - NS) // 128
                        k.dma("sp", [(vnew_a[i, :, p0:p0 + 4, :], v32[:])], r_v32, reads=[r_v32])
                    for g in range(4):
                        b = g % 2
                        proj_fm(wie, r_wie, 768 + g * 128, ht, r_h, pq[b][:], r_pq[b])
                        k.op("act", lambda e, b=b: e.activation(out=ut[b][:], in_=pq[b][:], func=AF.Copy),
                             reads=[r_pq[b]], writes=[r_ut[b]])
                        for sub in range(4):
                            k.op("pe", lambda e, b=b, sub=sub: e.matmul(
                                pv[:], lhsT=ut[b][:, sub * 128:(sub + 1) * 128], rhs=cstb[:, CB_CS:CB_CS + 2, :],
                                start=True, stop=True), reads=[r_ut[b], r_cst], writes=[r_pv])
                            eng = "dve" if sub % 2 == 0 else "act"
                            if eng == "dve":
                                k.op("dve", lambda e, g=g, sub=sub: e.tensor_copy(out=pqt[:, sub, g, :], in_=pv[:]),
                                     reads=[r_pv], writes=[r_pqt])
                            else:
                                k.op("act", lambda e, g=g, sub=sub: e.activation(out=pqt[:, sub, g, :], in_=pv[:], func=AF.Copy),
                                     reads=[r_pv], writes=[r_pqt])
                    k.dma("sp", [(PQ_d[:, t * 4:(t + 1) * 4, :, :], pqt[:])], r_pqt, reads=[r_pqt], writes=[r_scr])
                k.barrier()
                k.release([r_wie, r_x, r_rt, r_qo, r_va, r_v32, r_pqt] + qk.all_qn_res())

        def even_fnet(i):
            with ExitStack() as st:
                pf = [ps("pf%d" % b, [128, 256], F32, st) for b in range(2)]
                r_pf = [Res("pf%d" % b) for b in range(2)]
                fo = [sb("fo%d" % b, [128, 4, 256], BF16, st) for b in range(2)]
                r_fo = [Res("fo%d" % b) for b in range(2)]
                rel = list(r_fo)
                r_scr = Res("mixd")
                for (off, S, cond, smp) in cfg.seqs:
                    nst, nkt = S // 128, S // 256
                    tabd = tabS if smp else tabP
                    with ExitStack() as s2:
                        pqs = sb("pqs", [128, nst, 4, 256], BF16, s2)
                        r_pqs = Res("pqs")
                        k.dma("sp", [(pqs[:, a:min(a + 8, nst)], PQ_d[:, off // 128 + a:off // 128 + min(a + 8, nst)])
                                     for a in range(0, nst, 8)], r_pqs, writes=[r_pqs])
                        tab = [sb("tab%d" % b, [128, nst, 2, 256], BF16, s2) for b in range(2)]
                        r_tab = [Res("tab%d" % b) for b in range(2)]
                        u = 0
                        for kt in range(nkt):
                            tb = kt % 2
                            k.dma("sp", [(tab[tb][:], tabd[kt])], r_tab[tb], writes=[r_tab[tb]])
                            fb = kt % 2
                            for g in range(4):
                                b = u % 2
                                u += 1
                                n = 0
                                for s_ in range(nst):
                                    for cs in range(2):
                                        k.op("pe", lambda e, b=b, s_=s_, cs=cs, g=g, n=n, tb=tb: e.matmul(
                                            pf[b][:], lhsT=pqs[:, s_, g, cs * 128:(cs + 1) * 128], rhs=tab[tb][:, s_, cs, :],
                                            start=(n == 0), stop=(n == 2 * nst - 1)),
                                            reads=[r_pqs, r_tab[tb]], writes=[r_pf[b]])
                                        n += 1
                                if g % 2 == 0:
                                    k.op("dve", lambda e, b=b, g=g, fb=fb: e.tensor_copy(out=fo[fb][:, g, :], in_=pf[b][:]),
                                         reads=[r_pf[b]], writes=[r_fo[fb]])
                                else:
                                    k.op("act", lambda e, b=b, g=g, fb=fb: e.activation(out=fo[fb][:, g, :], in_=pf[b][:], func=AF.Copy),
                                         reads=[r_pf[b]], writes=[r_fo[fb]])
                            k.dma("sp", [(MIX_d[:, 4:8, off + kt * 256:off + (kt + 1) * 256], fo[fb][:])], r_fo[fb],
                                  reads=[r_fo[fb]], writes=[r_scr])
                        k.barrier()
                        k.release([r_pqs] + r_tab)
                k.release(rel)

        def even_attn(i):
            with ExitStack() as st:
                esk = sb("esk", [128, 8], F32, st)
                r_esk = Res("esk")
                k.dma("sp", [(esk[0:1, :], sink_e[i:i + 1, :]), (esk[64:65, :], sink_e[i:i + 1, :])], r_esk, writes=[r_esk])
                k.op("act", lambda e: e.activation(out=esk[0:1, :], in_=esk[0:1, :], func=AF.Exp), reads=[r_esk], writes=[r_esk])
                k.op("act", lambda e: e.activation(out=esk[64:65, :], in_=esk[64:65, :], func=AF.Exp), reads=[r_esk], writes=[r_esk])
                pS = [ps("pS%d" % b, [128, 4, 128], F32, st) for b in range(3)]
                pO = [ps("pO%d" % b, [128, 4, 128], F32, st) for b in range(2)]
                pB = [ps("pB%d" % b, [128, 4, 128], F32, st) for b in range(2)]
                r_pS = [Res("pS%d" % b) for b in range(3)]
                r_pO = [Res("pO%d" % b) for b in range(2)]
                r_pB = [Res("pB%d" % b) for b in range(2)]
                pT = [sb("pT%d" % b, [128, 4, 128], BF16, st) for b in range(4)]
                r_pT = [Res("pT%d" % b) for b in range(4)]
                dd = [sb("dd%d" % b, [128, 4, 128], F32, st) for b in range(2)]
                r_dd = [Res("dd%d" % b) for b in range(2)]
                ob = [sb("ob%d" % b, [128, 4, 128], F32, st) for b in range(2)]
                r_ob = [Res("ob%d" % b) for b in range(2)]
                ao = [sb("ao%d" % b, [128, 4, 512], BF16, st) for b in range(2)]
                r_ao = [Res("ao%d" % b) for b in range(2)]
                qm = [[sb("qm%d%d" % (g, p_), [128, 4, 128], BF16, st) for p_ in range(2)] for g in range(2)]
                r_qm = [[Res("qm%d%d" % (g, p_)) for p_ in range(2)] for g in range(2)]
                for g in range(2):
                    for p_ in range(2):
                        k.op("pool", lambda e, g=g, p_=p_: e.memset(qm[g][p_][:], 0.0), writes=[r_qm[g][p_]])
                r_scr = Res("mixd")
                rel = [r_esk] + r_ao
                for (off, S, cond, smp) in cfg.seqs:
                    nb = S // 128
                    with ExitStack() as s2:
                        qs = sb("qs", [128, 4, S], BF16, s2)
                        ks = sb("ks", [128, S], BF16, s2)
                        vs = sb("vs", [128, nb, 2, 128], BF16, s2)
                        r_q = Res("qs")
                        k.dma("sp", [(qs[:], QT_d[:, :, off:off + S]), (ks[:], KT_d[:, 0, off:off + S]),
                                     (vs[:], VA_d[:, off // 128:off // 128 + nb])], r_q, writes=[r_q])
                        rr = [r_q]
                        if smp:
                            kcx = sb("kcx", [128, PAST], BF16, s2)
                            vcx = sb("vcx", [128, 2, 2, 128], BF16, s2)
                            r_cx = Res("cx")
                            k.dma("pool", [(kcx[:], kctx_e[i]), (vcx[:], vctx_e[i])], r_cx, writes=[r_cx])
                            rr.append(r_cx)
                        def kts_of(qb):
                            if smp:
                                kts = [("l", j, (CB_MP if j == qb - 1 else (CB_MN if j == qb + 1 else None)))
                                       for j in (qb - 1, qb, qb + 1) if 0 <= j < nb]
                                return kts + [("c", 0, None), ("c", 1, None)]
                            return [("l", j, None) for j in range(nb)]

                        steps = []
                        for qb in range(nb):
                            for g in range(2):
                                kts = kts_of(qb)
                                for n in range(len(kts)):
                                    steps.append((qb, g, n, kts[n], len(kts), len(steps) and 0))
                        unit_of = {}
                        for idx, (qb, g, n, kt, nk, _) in enumerate(steps):
                            unit_of[idx] = qb * 2 + g

                        def opnd(g, kt):
                            kind, j, msk = kt
                            rows = slice(g * 64, (g + 1) * 64)
                            if kind == "l":
                                return ks[rows, j * 128:(j + 1) * 128], vs[:, j, g, :], r_q
                            return kcx[rows, j * 128:(j + 1) * 128], vcx[:, j, g, :], r_cx

                        done_q = set()

                        def ensure_q(qb, g):
                            if (qb, g) in done_q or qb >= nb:
                                return
                            done_q.add((qb, g))
                            rows = slice(g * 64, (g + 1) * 64)
                            k.op("pool", lambda e: e.tensor_copy(out=qm[g][qb % 2][rows], in_=qs[rows, :, qb * 128:(qb + 1) * 128]),
                                 reads=[r_q], writes=[r_qm[g][qb % 2]])

                        def emit_S(idx):
                            qb, g, n, kt, nk, _ = steps[idx]
                            kind, j, msk = kt
                            if n == 0:
                                ensure_q(qb, g)
                                ensure_q(qb + (1 if g == 1 else 0), 1 - g)
                            if kind == "l":
                                kap_, rk = ks[:, j * 128:(j + 1) * 128], r_q
                            else:
                                kap_, rk = kcx[:, j * 128:(j + 1) * 128], r_cx
                            sbuf_i = idx % 3
                            k.op("pe", lambda e: e.matmul(pS[sbuf_i][:], lhsT=kap_, rhs=qm[g][qb % 2][:],
                                                          start=True, stop=True), reads=[rk, r_qm[g][qb % 2]], writes=[r_pS[sbuf_i]])

                        def emit_rest(idx):
                            qb, g, n, kt, nk, _ = steps[idx]
                            kap_, vap_, rk = opnd(g, kt)
                            msk = kt[2]
                            rows = slice(g * 64, (g + 1) * 64)
                            sbuf_i = idx % 3
                            tbuf = idx % 4
                            ub = unit_of[idx] % 2
                            ab = (qb // 4) % 2
                            k.op("act", lambda e: e.activation(out=pT[tbuf][:], in_=pS[sbuf_i][:], func=AF.Exp, scale=0.125),
                                 reads=[r_pS[sbuf_i]], writes=[r_pT[tbuf]])
                            if msk is not None:
                                k.op("pool", lambda e: e.tensor_tensor(out=pT[tbuf][:], in0=pT[tbuf][:], in1=cstb[:, msk:msk + 4, :], op=ALU.mult),
                                     reads=[r_pT[tbuf], r_cst], writes=[r_pT[tbuf]])
                            k.op("pe", lambda e: e.matmul(pO[ub][:], lhsT=vap_, rhs=pT[tbuf][:], start=(n == 0), stop=(n == nk - 1)),
                                 reads=[rk, r_pT[tbuf]], writes=[r_pO[ub]])
                            if n != nk - 1:
                                return
                            row = 64 if g == 0 else 0
                            k.op("dve", lambda e: e.tensor_tensor(
                                out=dd[ub][row:row + 1], in0=pO[ub][row:row + 1],
                                in1=esk[row:row + 1, g * 4:(g + 1) * 4].unsqueeze(2).to_broadcast([1, 4, 128]),
                                op=ALU.add), reads=[r_pO[ub], r_esk], writes=[r_dd[ub]])
                            k.op("dve", lambda e: e.reciprocal(out=dd[ub][row:row + 1], in_=dd[ub][row:row + 1]),
                                 reads=[r_dd[ub]], writes=[r_dd[ub]])
                            k.op("pe", lambda e: e.matmul(pB[ub][:], lhsT=cstf[row:row + 1, ONESF, :], rhs=dd[ub][row:row + 1], start=True, stop=True),
                                 reads=[r_dd[ub], r_cst], writes=[r_pB[ub]])
                            k.op("act", lambda e: e.activation(out=ob[ub][rows], in_=pO[ub][rows], func=AF.Copy),
                                 reads=[r_pO[ub]], writes=[r_ob[ub]])
                            k.op("dve", lambda e: e.tensor_tensor(
                                out=ao[ab][rows, :, (qb % 4) * 128:(qb % 4 + 1) * 128], in0=ob[ub][rows], in1=pB[ub][rows],
                                op=ALU.mult), reads=[r_ob[ub], r_pB[ub]], writes=[r_ao[ab]])
                            if g == 1 and (qb % 4 == 3 or qb == nb - 1):
                                q0 = (qb // 4) * 512
                                wdt = (qb % 4 + 1) * 128
                                k.dma("sp", [(MIX_d[:, 0:4, off + q0:off + q0 + wdt], ao[ab][:, :, 0:wdt])], r_ao[ab],
                                      reads=[r_ao[ab]], writes=[r_scr])

                        emit_S(0)
                        if len(steps) > 1:
                            emit_S(1)
                        for idx in range(len(steps)):
                            if idx + 2 < len(steps):
                                emit_S(idx + 2)
                            emit_rest(idx)
                        k.barrier()
                        k.release(rr)
                k.release(rel)

        def out_proj(l, wsrc):
            with ExitStack() as st:
                wom, r_wom = load_w_bf(st, "wom", wsrc, D)
                xt = sb("xt", [128, NCH, TT], F32, st)
                r_x = Res("xt")
                mx = sb("mx", [128, NCH, TT], BF16, st)
                r_mx = Res("mx")
                py = [ps("py%d" % b, [128, TT], F32, st) for b in range(2)]
                r_py = [Res("py%d" % b) for b in range(2)]
                r_yT = Res("yT")
                for t in range(NT):
                    cond = 1 if is_sample_tile(t) else 0
                    tsl = slice(t * TT, (t + 1) * TT)
                    k.dma("sp", [(xt[:, c, :], yT[c, :, tsl]) for c in range(NCH)], r_x, reads=[r_yT], writes=[r_x])
                    k.dma("sp", [(mx[:], MIX_d[:, :, tsl])], r_mx, writes=[r_mx])
                    for dc in range(NCH):
                        b = dc % 2
                        for kc in range(NCH):
                            k.op("pe", lambda e, kc=kc, dc=dc, b=b: e.matmul(
                                py[b][:], lhsT=wom[:, kc, dc * 128:(dc + 1) * 128], rhs=mx[:, kc, :],
                                start=(kc == 0), stop=(kc == NCH - 1)), reads=[r_wom, r_mx], writes=[r_py[b]])
                        k.op("dve", lambda e, dc=dc, b=b: e.scalar_tensor_tensor(
                            out=xt[:, dc, :], in0=py[b][:], scalar=GM[:, l, 1, dc, cond:cond + 1], in1=xt[:, dc, :],
                            op0=ALU.mult, op1=ALU.add), reads=[r_py[b], r_am, r_x], writes=[r_x])
                    k.dma("sp", [(yT[c, :, tsl], xt[:, c, :]) for c in range(NCH)], r_x, reads=[r_x], writes=[r_yT])
                k.barrier()
                k.release([r_wom, r_x, r_mx])

        DKS = 128 ** -0.5

        def odd_proj(l, i):
            with ExitStack() as st:
                wio, r_wio = load_w_bf(st, "wio", w_in_o[i], 3600)
                xt = sb("xt", [128, NCH, TT], F32, st)
                r_x = Res("xt")
                ht = sb("ht", [128, NCH, TT], BF16, st)
                r_h = Res("ht")
                tmp = alloc_norm_tmp(st)
                qk = QKN(st)
                pq = [ps("pq%d" % b, [128, TT], F32, st) for b in range(2)]
                r_pq = [Res("pq%d" % b) for b in range(2)]
                pvv = ps("pvv", [128, 512], F32, st)
                r_pvv = Res("pvv")
                pg4 = tmp[2][0:4, :]
                r_pg4 = tmp[3]
                rt = sb("rt", [128, 2, TT], F32, st)
                r_rt = Res("rt")
                qo = sb("qo", [128, 8, TT], BF16, st)
                r_qo = Res("qo")
                vt = sb("vt", [128, 4, 512], BF16, st)
                r_vt = Res("vt")
                v32 = sb("v32", [128, 4, 512], F32, st)
                r_v32 = Res("v32")
                fo = sb("fo", [128, 12, TT], BF16, st)
                r_fo = Res("fo")
                kt_ = sb("kt_", [128, 4, 4, 128], BF16, st)
                r_kt = Res("kt_")
                v1 = sb("v1", [128, 4, 4, 160], BF16, st)
                r_v1 = Res("v1")
                gt = sb("gt", [4, 4, TT], F32, st)
                r_gt = Res("gt")
                k.op("pool", lambda e: e.memset(v1[:], 1.0), writes=[r_v1])
                r_scr = Res("scr_o")
                r_yT = Res("yT")
                for t in range(NT):
                    smp = is_sample_tile(t)
                    cond = 1 if smp else 0
                    tsl = slice(t * TT, (t + 1) * TT)
                    k.dma("sp", [(xt[:, c, :], yT[c, :, tsl]) for c in range(NCH)], r_x, reads=[r_yT], writes=[r_x])
                    if smp:
                        k.dma("sp", [(rt[:], ropeT[:, :, tsl])], r_rt, writes=[r_rt])
                    norm_mod(st, r_x, xt, ht, r_h, l, 1, cond, tmp)
                    sub_ = getattr(cfg, "odd_sub", 31)
                    for blk in (range(8) if sub_ & 1 else []):
                        b = blk % 2
                        proj_fm(wio, r_wio, blk * 128, ht, r_h, pq[b][:], r_pq[b])
                        gain = paro[:, i, (0 if blk < 4 else 1):(1 if blk < 4 else 2)]
                        qk.run(pq[b][:], r_pq[b], gain, rt if smp else None, qo[:, blk, :], r_qo, r_rt)
                        if blk >= 4 and not smp:
                            p0 = t * TT - NS
                            k.dma("sp", [(knew_c[i, :, blk - 4, p0:p0 + TT], qk.last["qn"][:])], qk.last["r"]["qn"], reads=[qk.last["r"]["qn"]])
                    k.dma("sp", [(QT_d[:, :, tsl], qo[:, 0:4, :]), (KT_d[:, :, tsl], qo[:, 4:8, :])], r_qo,
                          reads=[r_qo], writes=[r_scr])
                    dbg_ = getattr(cfg, "odd_dbg", 31)
                    for sub in (range(4) if sub_ & 2 else []):
                        if dbg_ & 1:
                            proj_tm(wio, r_wio, 1024, 512, ht, r_h, sub, pvv[:], r_pvv)
                        if dbg_ & 2:
                            k.op("act", lambda e, sub=sub: e.activation(out=vt[:, sub, :], in_=pvv[:], func=AF.Copy),
                                 reads=[r_pvv], writes=[r_vt])
                        if not smp and (dbg_ & 4):
                            k.op("dve", lambda e, sub=sub: e.tensor_copy(out=v32[:, sub, :], in_=pvv[:]),
                                 reads=[r_pvv], writes=[r_v32])
                    if dbg_ & 8:
                        k.dma("sp", [(VCt_d[:, t * 4:(t + 1) * 4, :], vt[:])], r_vt, reads=[r_vt], writes=[r_scr])
                    if not smp and (dbg_ & 16):
                        p0 = (t * TT - NS) // 128
                        k.dma("sp", [(vnew_c[i, :, p0:p0 + 4, :], v32[:])], r_v32, reads=[r_v32])
                    for blk in (range(12) if sub_ & 4 else []):
                        b = blk % 2
                        col0 = (1536 + blk * 128) if blk < 8 else (3072 + (blk - 8) * 128)
                        proj_fm(wio, r_wio, col0, ht, r_h, pq[b][:], r_pq[b])
                        if blk < 4:
                            k.op("dve", lambda e, b=b, blk=blk: e.tensor_copy(out=fo[:, blk, :], in_=pq[b][:]),
                                 reads=[r_pq[b]], writes=[r_fo])
                        elif blk < 8:
                            k.op("act", lambda e, b=b, blk=blk: e.activation(out=fo[:, blk, :], in_=pq[b][:], func=AF.Identity, scale=DKS),
                                 reads=[r_pq[b]], writes=[r_fo])
                        else:
                            k.op("act", lambda e, b=b, blk=blk: e.activation(out=fo[:, blk, :], in_=pq[b][:], func=AF.Sigmoid),
                                 reads=[r_pq[b]], writes=[r_fo])
                    k.dma("sp", [(QDT_d[h, :, tsl], fo[:, h, :]) for h in range(4)]
                          + [(KDT_d[h, :, tsl], fo[:, 4 + h, :]) for h in range(4)]
                          + [(SODT_d[h, :, tsl], fo[:, 8 + h, :]) for h in range(4)], r_fo, reads=[r_fo], writes=[r_scr])
                    for sub in (range(4) if sub_ & 8 else []):
                        proj_tm(wio, r_wio, 2048, 512, ht, r_h, sub, pvv[:], r_pvv)
                        k.op("act", lambda e, sub=sub: e.activation(out=kt_[:, sub, :, :], in_=pvv[:].rearrange("p (h d) -> p h d", h=4),
                                                                    func=AF.Identity, scale=DKS), reads=[r_pvv], writes=[r_kt])
                        proj_tm(wio, r_wio, 2560, 512, ht, r_h, sub, pvv[:], r_pvv)
                        k.op("dve", lambda e, sub=sub: e.tensor_copy(out=v1[:, sub, :, 0:128], in_=pvv[:].rearrange("p (h d) -> p h d", h=4)),
                             reads=[r_pvv], writes=[r_v1])
                    k.dma("sp", [(KDt_d[h, :, t * 4:(t + 1) * 4, :], kt_[:, :, h, :]) for h in range(4)], r_kt,
                          reads=[r_kt], writes=[r_scr])
                    k.dma("sp", [(VD1t_d[h, :, t * 4:(t + 1) * 4, :], v1[:, :, h, :]) for h in range(4)], r_v1,
                          reads=[r_v1], writes=[r_scr])
                    for grp in (range(4) if sub_ & 16 else []):
                        proj_fm(wio, r_wio, 3584 + grp * 4, ht, r_h, pg4, r_pg4, m=4)
                        k.op("act", lambda e, grp=grp: e.activation(out=gt[:, grp, :], in_=pg4, func=AF.Identity,
                                                                    bias=bgo[:, i, grp:grp + 1], scale=1.0),
                             reads=[r_pg4, r_cst], writes=[r_gt])
                    k.dma("sp", [(G_d[grp, :, tsl], gt[:, grp, :]) for grp in range(4)], r_gt, reads=[r_gt], writes=[r_scr])
                k.barrier()
                k.release([r_wio, r_x, r_rt, r_qo, r_vt, r_v32, r_fo, r_kt, r_v1, r_gt] + qk.all_qn_res())

        def odd_attn(l, i):
            lam_init = 0.8 - 0.6 * math.exp(-0.3 * l)
            with ExitStack() as st:
                lv = sb("lv", [1, 2, 2, 64], F32, st)
                pr = sb("pr", [1, 2, 64], F32, st)
                s2_ = sb("s2_", [1, 16], F32, st)
                nlb = sb("nlb", [128, 1], F32, st)
                r_lv = Res("lv")
                k.dma("sp", [(lv[:], lam_o[i:i + 1, :].rearrange("o (a b d) -> o a b d", a=2, b=2))], r_lv, writes=[r_lv])
                k.op("dve", lambda e: e.tensor_tensor(out=pr[:], in0=lv[:, :, 0, :], in1=lv[:, :, 1, :], op=ALU.mult),
                     reads=[r_lv], writes=[r_lv])
                k.op("dve", lambda e: e.reduce_sum(out=s2_[:, 0:2], in_=pr[:], axis=AX.X), reads=[r_lv], writes=[r_lv])
                k.op("act", lambda e: e.activation(out=s2_[:, 0:2], in_=s2_[:, 0:2], func=AF.Exp), reads=[r_lv], writes=[r_lv])
                k.op("dve", lambda e: e.tensor_tensor(out=s2_[:, 2:3], in0=s2_[:, 1:2], in1=s2_[:, 0:1], op=ALU.subtract),
                     reads=[r_lv], writes=[r_lv])
                k.op("dve", lambda e: e.tensor_scalar(out=s2_[:, 8:9], in0=s2_[:, 2:3], scalar1=-lam_init, scalar2=None, op0=ALU.add),
                     reads=[r_lv], writes=[r_lv])
                pS = [ps("pS%d" % b, [128, 512], F32, st) for b in range(3)]
                pO = [ps("pO%d" % b, [128, 512], F32, st) for b in range(2)]
                pD = [ps("pD%d" % b, [128, 512], F32, st) for b in range(2)]
                pms = ps("pms", [128, 512], F32, st)
                r_pS = [Res("pS%d" % b) for b in range(3)]
                r_pO = [Res("pO%d" % b) for b in range(2)]
                r_pD = [Res("pD%d" % b) for b in range(2)]
                r_pms = Res("pms")
                k.op("pe", lambda e: e.matmul(pms[:, 0:1], lhsT=cstf[0:1, ONESF, :], rhs=s2_[:, 8:9], start=True, stop=True),
                     reads=[r_lv, r_cst], writes=[r_pms])
                r_nlb = Res("nlb")
                k.op("dve", lambda e: e.tensor_copy(out=nlb[:], in_=pms[:, 0:1]), reads=[r_pms], writes=[r_nlb])
                E = [sb("E%d" % b, [128, 512], BF16, st) for b in range(4)]
                r_E = [Res("E%d" % b) for b in range(4)]
                rd = [sb("rd%d" % b, [128, 512], F32, st) for b in range(2)]
                om = [sb("om%d" % b, [128, 512], F32, st) for b in range(2)]
                r_rd = [Res("rd%d" % b) for b in range(2)]
                r_om = [Res("om%d" % b) for b in range(2)]
                aa = sb("aa", [128, 512], F32, st)
                sq = sb("sqa", [128, 512], F32, st)
                rq = sb("rqa", [128, 512], F32, st)
                r_aa, r_sq, r_rq = Res("aa"), Res("sqa"), Res("rqa")
                ao = [sb("ao%d" % b, [128, 4, 512], BF16, st) for b in range(2)]
                r_ao = [Res("ao%d" % b) for b in range(2)]
                qm = [[sb("qm%d%d" % (m, p_), [128, 512], BF16, st) for p_ in range(2)] for m in range(2)]
                r_qm = [[Res("qm%d%d" % (m, p_)) for p_ in range(2)] for m in range(2)]
                for m in range(2):
                    for p_ in range(2):
                        k.op("pool", lambda e, m=m, p_=p_: e.memset(qm[m][p_][:], 0.0), writes=[r_qm[m][p_]])
                r_scr = Res("mixd")
                rel = [r_lv] + r_ao
                for (off, S, cond, smp) in cfg.seqs:
                    nb = S // 128
                    QW = min(512, S)
                    with ExitStack() as s2:
                        qs = sb("qs", [128, 4, S], BF16, s2)
                        ks = sb("ks", [128, 4, S], BF16, s2)
                        vs = sb("vs", [128, nb, 512], BF16, s2)
                        r_q = Res("qs")
                        k.dma("sp", [(qs[:], QT_d[:, :, off:off + S]), (ks[:], KT_d[:, :, off:off + S]),
                                     (vs[:], VCt_d[:, off // 128:off // 128 + nb, :])], r_q, writes=[r_q])
                        rr = [r_q]
                        if smp:
                            kcx = sb("kcx", [128, 4, PAST], BF16, s2)
                            vcx = sb("vcx", [128, 2, 512], BF16, s2)
                            r_cx = Res("cx")
                            k.dma("pool", [(kcx[:], kctx_o[i]), (vcx[:], vctx_o[i])], r_cx, writes=[r_cx])
                            rr.append(r_cx)
                        kts = [("l", j) for j in range(nb)] + ([("c", 0), ("c", 1)] if smp else [])
                        NK = len(kts)
                        steps = [(qt, h, m, n) for qt in range(S // QW) for h in range(4) for m in range(2) for n in range(NK)]

                        def opnd(h, m, n):
                            kind, j = kts[n]
                            rows = slice(m * 64, (m + 1) * 64)
                            if kind == "l":
                                return ks[rows, h, j * 128:(j + 1) * 128], vs[:, j, h * 128:(h + 1) * 128], r_q
                            return kcx[rows, h, j * 128:(j + 1) * 128], vcx[:, j, h * 128:(h + 1) * 128], r_cx

                        done_q = set()

                        def ensure_q(qt, h, m):
                            if (qt, h, m) in done_q or qt >= S // QW:
                                return
                            done_q.add((qt, h, m))
                            par = (qt * 4 + h) % 2
                            rows = slice(m * 64, (m + 1) * 64)
                            qsl = slice(qt * QW, (qt + 1) * QW)
                            k.op("pool", lambda e: e.tensor_copy(out=qm[m][par][rows, 0:QW], in_=qs[rows, h, qsl]),
                                 reads=[r_q], writes=[r_qm[m][par]])

                        def emit_S(idx):
                            qt, h, m, n = steps[idx]
                            kind, j = kts[n]
                            par = (qt * 4 + h) % 2
                            if n == 0:
                                ensure_q(qt, h, m)
                                nh = qt * 4 + h + (1 if m == 1 else 0)
                                ensure_q(nh // 4, nh % 4, 1 - m)
                            if kind == "l":
                                kap_, rk = ks[:, h, j * 128:(j + 1) * 128], r_q
                            else:
                                kap_, rk = kcx[:, h, j * 128:(j + 1) * 128], r_cx
                            sb_i = idx % 3
                            k.op("pe", lambda e: e.matmul(pS[sb_i][:, 0:QW], lhsT=kap_, rhs=qm[m][par][:, 0:QW], start=True, stop=True),
                                 reads=[rk, r_qm[m][par]], writes=[r_pS[sb_i]])

                        def emit_rest(idx):
                            qt, h, m, n = steps[idx]
                            kap_, vap_, rk = opnd(h, m, n)
                            sb_i = idx % 3
                            eb = idx % 4
                            ab = qt % 2
                            k.op("act", lambda e: e.activation(out=E[eb][:, 0:QW], in_=pS[sb_i][:, 0:QW], func=AF.Exp, scale=0.125),
                                 reads=[r_pS[sb_i]], writes=[r_E[eb]])
                            k.op("pe", lambda e: e.matmul(pO[m][:, 0:QW], lhsT=vap_, rhs=E[eb][:, 0:QW], start=(n == 0), stop=(n == NK - 1)),
                                 reads=[rk, r_E[eb]], writes=[r_pO[m]])
                            k.op("pe", lambda e: e.matmul(pD[m][:, 0:QW], lhsT=cstb[:, CB_ONE, :], rhs=E[eb][:, 0:QW], start=(n == 0), stop=(n == NK - 1)),
                                 reads=[r_cst, r_E[eb]], writes=[r_pD[m]])
                            if n != NK - 1:
                                return
                            k.op("dve", lambda e: e.reciprocal(out=rd[m][:, 0:QW], in_=pD[m][:, 0:QW]),
                                 reads=[r_pD[m]], writes=[r_rd[m]])
                            k.op("dve", lambda e: e.tensor_tensor(out=om[m][:, 0:QW], in0=pO[m][:, 0:QW], in1=rd[m][:, 0:QW], op=ALU.mult),
                                 reads=[r_pO[m], r_rd[m]], writes=[r_om[m]])
                            if m != 1:
                                return
                            k.op("dve", lambda e: e.scalar_tensor_tensor(out=aa[:, 0:QW], in0=om[1][:, 0:QW], scalar=nlb[:, 0:1],
                                                                         in1=om[0][:, 0:QW], op0=ALU.mult, op1=ALU.add),
                                 reads=[r_om[0], r_om[1], r_nlb], writes=[r_aa])
                            k.op("act", lambda e: e.activation(out=sq[:, 0:QW], in_=aa[:, 0:QW], func=AF.Square),
                                 reads=[r_aa], writes=[r_sq])
                            k.op("pe", lambda e: e.matmul(pms[:, 0:QW], lhsT=cstf[:, F128, :], rhs=sq[:, 0:QW], start=True, stop=True),
                                 reads=[r_sq, r_cst], writes=[r_pms])
                            k.op("act", lambda e: e.activation(out=rq[:, 0:QW], in_=pms[:, 0:QW], func=AF.Ln, bias=eps_t[:], scale=1.0),
                                 reads=[r_pms, r_ones], writes=[r_rq])
                            k.op("act", lambda e: e.activation(out=rq[:, 0:QW], in_=rq[:, 0:QW], func=AF.Exp, scale=-0.5), reads=[r_rq], writes=[r_rq])
                            k.op("dve", lambda e: e.scalar_tensor_tensor(out=aa[:, 0:QW], in0=aa[:, 0:QW], scalar=paro[:, i, 2:3],
                                                                         in1=rq[:, 0:QW], op0=ALU.mult, op1=ALU.mult),
                                 reads=[r_aa, r_rq, r_cst], writes=[r_aa])
                            k.op("act", lambda e: e.activation(out=ao[ab][:, h, 0:QW], in_=aa[:, 0:QW], func=AF.Identity,
                                                               scale=(1.0 - lam_init)),
                                 reads=[r_aa], writes=[r_ao[ab]])
                            if h == 3:
                                k.dma("sp", [(MIX_d[:, 0:4, off + qt * QW:off + (qt + 1) * QW], ao[ab][:, :, 0:QW])], r_ao[ab],
                                      reads=[r_ao[ab]], writes=[r_scr])

                        emit_S(0)
                        if len(steps) > 1:
                            emit_S(1)
                        for idx in range(len(steps)):
                            if idx + 2 < len(steps):
                                emit_S(idx + 2)
                            emit_rest(idx)
                        k.barrier()
                        k.release(rr)
                k.release(rel)

        def odd_mlstm(l, i):
            nch = T // 128
            with ExitStack() as st:
                Bt = [sb("Bt%d" % d_, [4, T], BF16, st) for d_ in range(2)]
                CLt = [sb("CLt%d" % d_, [4, T], BF16, st) for d_ in range(2)]
                WI = [sb("WI%d" % d_, [4, nch], F32, st) for d_ in range(2)]
                mo = sb("mo", [4, 2, 2], F32, st)
                r_bk = Res("bk")
                r_mo = Res("mo")
                with ExitStack() as s1:
                    A1 = sb("A1", [4, T], F32, s1)
                    A2 = sb("A2", [4, T], F32, s1)
                    A3 = sb("A3", [4, T], F32, s1)
                    seg = sb("seg", [4, T], F32, s1)
                    bendN = sb("bendN", [4, nch], F32, s1)
                    kap = sb("kap", [4, nch], F32, s1)
                    Mr = sb("Mr", [4, nch], F32, s1)
                    WIe = sb("WIe", [4, nch], F32, s1)
                    marr = sb("marr", [4, nch], F32, s1)
                    zer = sb("zer", [4, 1], F32, s1)
                    r_a = Res("A")
                    r_seg = Res("seg")
                    k.op("pool", lambda e: e.memset(seg[:], 1.0), writes=[r_seg])
                    k.op("pool", lambda e: e.memset(seg[:].rearrange("p (c t) -> p c t", t=128)[:, :, 0:1], 0.0), writes=[r_seg])
                    k.op("pool", lambda e: e.memset(zer[:], 0.0), writes=[r_seg])
                    A2v = A2[:].rearrange("p (c t) -> p c t", t=128)
                    A3v = A3[:].rearrange("p (c t) -> p c t", t=128)
                    for d_ in range(2):
                        k.dma("sp", [(A1[:], G_d[d_ * 2 + 1]), (A2[:], G_d[d_ * 2])], r_a, writes=[r_a])
                        k.op("act", lambda e: e.activation(out=A1[:], in_=A1[:], func=AF.Exp, scale=-1.0), reads=[r_a], writes=[r_a])
                        k.op("act", lambda e: e.activation(out=A1[:], in_=A1[:], func=AF.Ln, bias=one_t[0:4, :], scale=1.0), reads=[r_a], writes=[r_a])
                        k.op("dve", lambda e: e.tensor_tensor_scan(out=A3[:], data0=seg[:], data1=A1[:], initial=0.0,
                                                                  op0=ALU.mult, op1=ALU.add), reads=[r_a, r_seg], writes=[r_a])
                        k.op("dve", lambda e: e.tensor_copy(out=bendN[:], in_=A3v[:, :, 127]), reads=[r_a], writes=[r_a])
                        if d_ == 1:
                            k.op("dve", lambda e: e.tensor_tensor(out=A3v, in0=bendN[:].unsqueeze(2).to_broadcast([4, nch, 128]),
                                                                  in1=A3v, op=ALU.subtract), reads=[r_a], writes=[r_a])
                            k.op("dve", lambda e: e.tensor_tensor(out=A3[:], in0=A3[:], in1=A1[:], op=ALU.add), reads=[r_a], writes=[r_a])
                        k.op("dve", lambda e: e.tensor_tensor(out=A2[:], in0=A2[:], in1=A3[:], op=ALU.add), reads=[r_a], writes=[r_a])
                        k.op("dve", lambda e: e.tensor_reduce(out=kap[:], in_=A2v, axis=AX.X, op=ALU.max), reads=[r_a], writes=[r_a])
                        for si, (off, S, cond, smp) in enumerate(cfg.seqs):
                            c0, c1 = off // 128, (off + S) // 128
                            order = list(range(c0, c1)) if d_ == 0 else list(range(c1 - 1, c0 - 1, -1))
                            mcur = m0t[:, i, d_:d_ + 1] if smp else zer[:]
                            for j in order:
                                k.op("dve", lambda e, j=j, mcur=mcur: e.tensor_tensor(out=Mr[:, j:j + 1], in0=mcur, in1=kap[:, j:j + 1], op=ALU.max),
                                     reads=[r_a, r_cst, r_seg], writes=[r_a])
                                k.op("dve", lambda e, j=j, mcur=mcur: e.tensor_tensor(out=WIe[:, j:j + 1], in0=mcur, in1=Mr[:, j:j + 1], op=ALU.subtract),
                                     reads=[r_a, r_cst, r_seg], writes=[r_a])
                                k.op("dve", lambda e, j=j: e.tensor_tensor(out=marr[:, j:j + 1], in0=Mr[:, j:j + 1], in1=bendN[:, j:j + 1], op=ALU.subtract),
                                     reads=[r_a], writes=[r_a])
                                mcur = marr[:, j:j + 1]
                            if not smp:
                                k.op("dve", lambda e, si=si, mcur=mcur, d_=d_: e.tensor_copy(out=mo[:, si - 1, d_:d_ + 1], in_=mcur),
                                     reads=[r_a], writes=[r_mo])
                        Mb = Mr[:].unsqueeze(2).to_broadcast([4, nch, 128])
                        k.op("dve", lambda e: e.tensor_tensor(out=A2v, in0=A2v, in1=Mb, op=ALU.subtract), reads=[r_a], writes=[r_a])
                        k.op("act", lambda e, d_=d_: e.activation(out=Bt[d_][:], in_=A2[:], func=AF.Exp), reads=[r_a], writes=[r_bk])
                        k.op("dve", lambda e: e.tensor_tensor(out=A3v, in0=A3v, in1=Mb, op=ALU.subtract), reads=[r_a], writes=[r_a])
                        k.op("act", lambda e, d_=d_: e.activation(out=CLt[d_][:], in_=A3[:], func=AF.Exp), reads=[r_a], writes=[r_bk])
                        k.op("act", lambda e, d_=d_: e.activation(out=WI[d_][:], in_=WIe[:], func=AF.Exp), reads=[r_a], writes=[r_bk])
                    k.dma("sp", [(mnew[:, i, :, :], mo[:])], r_mo, reads=[r_mo])
                    k.barrier()
                    k.release([r_a])
                pb = [ps("pb%d" % b, [128, 512], F32, st) for b in range(2)]
                r_pb = [Res("pb%d" % b) for b in range(2)]
                pmisc = ps("pmisc", [128, 2, nch], F32, st)
                r_pmisc = Res("pmisc")
                PA = [ps("PA%d" % b, [128, 3, 128], F32, st) for b in range(2)]
                r_PA = [[Res("PA%d_%d" % (b, c_)) for c_ in range(3)] for b in range(2)]
                r_PAL = [Res("PAlock%d" % b, lock=True) for b in range(2)]
                PC = [ps("PC%d" % b, [128, 129], F32, st) for b in range(2)]
                r_PC = [Res("PC%d" % b) for b in range(2)]
                r_scr = Res("mixd")
                for h in range(4):
                    with ExitStack() as sh:
                        qd = sb("qd", [128, T], BF16, sh)
                        kdT = sb("kdT", [128, T], BF16, sh)
                        kdt = sb("kdt", [128, nch, 128], BF16, sh)
                        vd1 = sb("vd1", [128, nch, 160], BF16, sh)
                        r_ld = Res("ld")
                        k.dma("sp", [(qd[:], QDT_d[h]), (kdT[:], KDT_d[h]), (kdt[:], KDt_d[h]), (vd1[:], VD1t_d[h])], r_ld, writes=[r_ld])
                        Hs = [sb("Hs%d" % d_, [128, T], F32, sh) for d_ in range(2)]
                        r_Hs = [[Res("Hs%d_%d" % (d_, j)) for j in range(nch)] for d_ in range(2)]
                        KpT = [sb("KpT%d" % d_, [128, T], BF16, sh) for d_ in range(2)]
                        CLB = [sb("CLB%d" % d_, [128, T], BF16, sh) for d_ in range(2)]
                        Kpt = [sb("Kpt%d" % d_, [128, nch, 128], BF16, sh) for d_ in range(2)]
                        bcol = [sb("bcol%d" % d_, [128, nch], F32, sh) for d_ in range(2)]
                        wib = [sb("wib%d" % d_, [128, nch], F32, sh) for d_ in range(2)]
                        r_pre = [Res("pre%d" % d_) for d_ in range(2)]
                        scm = [sb("scm%d" % b, [128, 128], BF16, sh) for b in range(2)]
                        r_scm = [Res("scm%d" % b) for b in range(2)]
                        dcl = [sb("dcl%d" % b, [128, 128], F32, sh) for b in range(2)]
                        r_dcl = [Res("dcl%d" % b) for b in range(2)]
                        ucnt = [0]
                        rel_c = []
                        for d_ in range(2):
                            for t in range(NT):
                                tsl = slice(t * TT, (t + 1) * TT)
                                k.op("pe", lambda e, tsl=tsl, d_=d_: e.matmul(pb[0][:], lhsT=c4b[:, h, :], rhs=Bt[d_][:, tsl], start=True, stop=True),
                                     reads=[r_bk, r_cst], writes=[r_pb[0]])
                                k.op("dve", lambda e, tsl=tsl, d_=d_: e.tensor_tensor(out=KpT[d_][:, tsl], in0=kdT[:, tsl], in1=pb[0][:], op=ALU.mult),
                                     reads=[r_pb[0], r_ld], writes=[r_pre[d_]])
                                k.op("pe", lambda e, tsl=tsl, d_=d_: e.matmul(pb[1][:], lhsT=c4b[:, h, :], rhs=CLt[d_][:, tsl], start=True, stop=True),
                                     reads=[r_bk, r_cst], writes=[r_pb[1]])
                                k.op("act", lambda e, tsl=tsl, d_=d_: e.activation(out=CLB[d_][:, tsl], in_=pb[1][:], func=AF.Copy),
                                     reads=[r_pb[1]], writes=[r_pre[d_]])
                            for j in range(nch):
                                k.op("pe", lambda e, j=j, d_=d_: e.matmul(pmisc[:, 0, j:j + 1], lhsT=Bt[d_][:, j * 128:(j + 1) * 128],
                                                                          rhs=c4b[:, 4, h * 32:h * 32 + 1], start=True, stop=True),
                                     reads=[r_bk, r_cst], writes=[r_pmisc])
                            k.op("pe", lambda e, d_=d_: e.matmul(pmisc[:, 1, :], lhsT=c4f[:, h, :], rhs=WI[d_][:], start=True, stop=True),
                                 reads=[r_bk, r_cst], writes=[r_pmisc])
                            k.op("dve", lambda e, d_=d_: e.tensor_copy(out=bcol[d_][:], in_=pmisc[:, 0, :]), reads=[r_pmisc], writes=[r_pre[d_]])
                            k.op("dve", lambda e, d_=d_: e.tensor_copy(out=wib[d_][:], in_=pmisc[:, 1, :]), reads=[r_pmisc], writes=[r_pre[d_]])
                            k.op("pool", lambda e, d_=d_: e.tensor_tensor(out=Kpt[d_][:], in0=kdt[:], in1=bcol[d_][:].unsqueeze(2).to_broadcast([128, nch, 128]),
                                                                          op=ALU.mult), reads=[r_pre[d_], r_ld], writes=[r_pre[d_]])

                        def chain(d_, si, off, S, smp):
                            mT = CB_MF if d_ == 0 else CB_MB
                            c0, c1 = off // 128, (off + S) // 128
                            order = list(range(c0, c1)) if d_ == 0 else list(range(c1 - 1, c0 - 1, -1))
                            cst_ = [sb("cst%d" % b, [128, 129], F32, sh) for b in range(2)]
                            cs = [sb("cs%d" % b, [128, 128], BF16, sh) for b in range(2)]
                            nsb = [sb("nsb%d" % b, [128, 128], BF16, sh) for b in range(2)]
                            r_c = [Res("c%d" % b) for b in range(2)]
                            r_cs = [Res("cs%d" % b) for b in range(2)]
                            rel_c.extend(r_c)
                            rp = r_pre[d_]
                            cur = 0
                            if smp:
                                k.dma("sp", [(cst_[cur][:, 0:128], c0_o[:, i, d_, h, :]), (cst_[cur][:, 128:129], n0_o[i, d_, h])],
                                      r_c[cur], writes=[r_c[cur]])
                            else:
                                k.op("pool", lambda e, cur=cur: e.memset(cst_[cur][:], 0.0), writes=[r_c[cur]])

                            def scaled_state(cur, j):
                                k.op("act", lambda e: e.activation(out=cs[cur][:], in_=cst_[cur][:, 0:128], func=AF.Identity,
                                                                   scale=wib[d_][:, j:j + 1]), reads=[r_c[cur], rp], writes=[r_cs[cur]])
                                k.op("dve", lambda e: e.tensor_scalar(out=nsb[cur][:], in0=cst_[cur][:, 128:129].to_broadcast([128, 128]),
                                                                      scalar1=wib[d_][:, j:j + 1], scalar2=None, op0=ALU.mult),
                                     reads=[r_c[cur], rp], writes=[r_cs[cur]])
                            scaled_state(cur, order[0])
                            for oi, j in enumerate(order):
                                ub = ucnt[0] % 2
                                ucnt[0] += 1
                                P = PA[ub]
                                rP = r_PA[ub]
                                rL = r_PAL[ub]
                                csl = slice(j * 128, (j + 1) * 128)
                                k.op("pe", lambda e: e.matmul(P[:, 0, :], lhsT=KpT[d_][:, csl], rhs=qd[:, csl], start=True, stop=True),
                                     reads=[rp, r_ld], writes=[rP[0], rL])
                                k.op("dve", lambda e: e.tensor_tensor(out=scm[ub][:], in0=P[:, 0, :], in1=cstb[:, mT, :], op=ALU.mult),
                                     reads=[rP[0], r_cst], writes=[r_scm[ub], rL])
                                k.op("pe", lambda e: e.matmul(P[:, 1, :], lhsT=vd1[:, j, 0:128], rhs=scm[ub][:], start=True, stop=False),
                                     reads=[r_ld, r_scm[ub]], writes=[rP[1], rL])
                                k.op("pe", lambda e: e.matmul(P[:, 1, :], lhsT=cs[cur][:], rhs=qd[:, csl], start=False, stop=True),
                                     reads=[r_cs[cur], r_ld], writes=[rP[1], rL])
                                k.op("pe", lambda e: e.matmul(P[:, 2, :], lhsT=cstb[:, CB_ONE, :], rhs=scm[ub][:], start=True, stop=False),
                                     reads=[r_cst, r_scm[ub]], writes=[rP[2], rL])
                                k.op("pe", lambda e: e.matmul(P[:, 2, :], lhsT=nsb[cur][:], rhs=qd[:, csl], start=False, stop=True),
                                     reads=[r_cs[cur], r_ld], writes=[rP[2], rL])
                                k.op("act", lambda e: e.activation(out=dcl[ub][:], in_=P[:, 2, :], func=AF.Abs),
                                     reads=[rP[2]], writes=[r_dcl[ub], rL])
                                k.op("dve", lambda e: e.tensor_tensor(out=dcl[ub][:], in0=dcl[ub][:], in1=CLB[d_][:, csl], op=ALU.max),
                                     reads=[r_dcl[ub], rp], writes=[r_dcl[ub]])
                                k.op("act", lambda e: e.activation(out=dcl[ub][:], in_=dcl[ub][:], func=AF.Ln), reads=[r_dcl[ub]], writes=[r_dcl[ub]])
                                k.op("act", lambda e: e.activation(out=dcl[ub][:], in_=dcl[ub][:], func=AF.Exp, scale=-1.0), reads=[r_dcl[ub]], writes=[r_dcl[ub]])
                                k.op("dve", lambda e: e.tensor_tensor(out=Hs[d_][:, csl], in0=P[:, 1, :], in1=dcl[ub][:], op=ALU.mult),
                                     reads=[rP[1], r_dcl[ub]], writes=[r_Hs[d_][j], rL])
                                k.op("pe", lambda e: e.matmul(PC[ub][:], lhsT=Kpt[d_][:, j, :], rhs=vd1[:, j, 0:129], start=True, stop=True),
                                     reads=[rp, r_ld], writes=[r_PC[ub]])
                                nxt = 1 - cur
                                k.op("dve", lambda e, cur=cur, nxt=nxt: e.scalar_tensor_tensor(
                                    out=cst_[nxt][:], in0=cst_[cur][:], scalar=wib[d_][:, j:j + 1], in1=PC[ub][:], op0=ALU.mult, op1=ALU.add),
                                    reads=[r_c[cur], r_PC[ub], rp], writes=[r_c[nxt]])
                                cur = nxt
                                if oi + 1 < len(order):
                                    scaled_state(cur, order[oi + 1])
                                else:
                                    if not smp:
                                        k.dma("sp", [(Cnew[i, si - 1, d_, h], cst_[cur][:, 0:128]), (nnew[i, si - 1, d_, h], cst_[cur][:, 128:129])],
                                              r_c[cur], reads=[r_c[cur]])
                                yield

                        chains = [chain(d_, si, off, S, smp) for d_ in range(2) for si, (off, S, cond, smp) in enumerate(cfg.seqs)]
                        while chains:
                            for c_ in list(chains):
                                try:
                                    next(c_)
                                except StopIteration:
                                    chains.remove(c_)
                        sqh = sb("sqh", [128, TT], F32, sh)
                        hsum = sb("hsum", [128, TT], F32, sh)
                        r_hsum = Res("hsum")
                        rqh = sb("rqh", [128, TT], F32, sh)
                        hm = sb("hm", [128, TT], F32, sh)
                        sod = sb("sod", [128, TT], BF16, sh)
                        ho = [sb("ho%d" % b, [128, TT], BF16, sh) for b in range(2)]
                        r_sqh, r_rqh, r_hm, r_sod = Res("sqh"), Res("rqh"), Res("hm"), Res("sod")
                        r_ho = [Res("ho%d" % b) for b in range(2)]
                        for t in range(NT):
                            tsl = slice(t * TT, (t + 1) * TT)
                            rH = r_Hs[0][t * 4:(t + 1) * 4] + r_Hs[1][t * 4:(t + 1) * 4]
                            k.dma("sp", [(sod[:], SODT_d[h, :, tsl])], r_sod, writes=[r_sod])
                            k.op("pool", lambda e, tsl=tsl: e.tensor_tensor(out=hsum[:], in0=Hs[0][:, tsl], in1=Hs[1][:, tsl], op=ALU.add),
                                 reads=rH, writes=[r_hsum])
                            rH = [r_hsum]
                            k.op("act", lambda e, tsl=tsl: e.activation(out=sqh[:], in_=hsum[:], func=AF.Square), reads=rH, writes=[r_sqh])
                            k.op("pe", lambda e: e.matmul(pb[0][:], lhsT=cstf[:, F128, :], rhs=sqh[:], start=True, stop=True),
                                 reads=[r_sqh, r_cst], writes=[r_pb[0]])
                            k.op("act", lambda e: e.activation(out=rqh[:], in_=pb[0][:], func=AF.Ln, bias=eps_t[:], scale=1.0),
                                 reads=[r_pb[0], r_ones], writes=[r_rqh])
                            k.op("act", lambda e: e.activation(out=rqh[:], in_=rqh[:], func=AF.Exp, scale=-0.5), reads=[r_rqh], writes=[r_rqh])
                            k.op("dve", lambda e, tsl=tsl: e.scalar_tensor_tensor(out=hm[:], in0=hsum[:], scalar=paro[:, i, 3:4], in1=rqh[:],
                                                                                 op0=ALU.mult, op1=ALU.mult), reads=rH + [r_rqh, r_cst], writes=[r_hm])
                            b = t % 2
                            k.op("pool", lambda e, b=b: e.tensor_tensor(out=ho[b][:], in0=hm[:], in1=sod[:], op=ALU.mult),
                                 reads=[r_hm, r_sod], writes=[r_ho[b]])
                            k.dma("sp", [(MIX_d[:, 4 + h, tsl], ho[b][:])], r_ho[b], reads=[r_ho[b]], writes=[r_scr])
                        k.barrier()
                        k.release([r_ld, r_sod] + r_ho + rel_c)
                k.release([r_mo])

        first = True
        for l in range(cfg.depth):
            i = l // 2
            ffn_phase(l, 0, xT_in if first else yT)
            first = False
            if cfg.do_mix:
                if l % 2 == 0:
                    even_proj(l, i)
                    even_fnet(i)
                    even_attn(i)
                    out_proj(l, w_out_e[i])
                else:
                    stage = getattr(cfg, "odd_stage", 9)
                    if stage >= 1:
                        odd_proj(l, i)
                    if stage >= 2:
                        odd_attn(l, i)
                    if stage >= 3:
                        odd_mlstm(l, i)
                    if stage >= 4:
                        out_proj(l, w_out_o[i])
            ffn_phase(l, 1, yT)
        k.barrier()
        print("instructions:", k.ninst, "waits:", k.nwait, "dma sems:", k.n_dma_sems, "sems:", len(k.sems))
    return nc


def _fm(x2d):
    return np.ascontiguousarray(x2d.T.reshape(NCH, 128, x2d.shape[0]))


def _tm(xT):
    return np.ascontiguousarray(xT.reshape(D, xT.shape[2]).T)


def _bf(a):
    return np.ascontiguousarray(a.astype(np.float32)).astype(ml_dtypes.bfloat16)


_CONST_CACHE = {}


def make_consts(cfg):
    key = (cfg.NS, cfg.NP)
    if key in _CONST_CACHE:
        return _CONST_CACHE[key]
    f32 = np.float32
    p = np.arange(128)
    cst_f = np.zeros((128, 5, 128), f32)
    for m in range(128):
        if m % 32 < 16:
            cst_f[m + 16, 0, m] = -1.0
        else:
            cst_f[m - 16, 0, m] = 1.0
    cst_f[:, 1, :] = (p[:, None] // 64 == p[None, :] // 64) / 64.0
    cst_f[:, 2, :] = 1.0 / 128.0
    cst_f[:, 3, :] = np.eye(128)
    cst_f[:, 4, :] = 1.0
    cst_b = np.zeros((128, 13, 128), f32)
    ang = 2.0 * np.pi * ((p[:, None] * p[None, :]) % 128) / 128.0
    cst_b[:, 0, :] = np.cos(ang)
    cst_b[:, 1, :] = -np.sin(ang)
    mprev = (p[None, :] <= p[:, None]).astype(f32)
    mnext = (p[:, None] <= p[None, :]).astype(f32)
    for hh in range(4):
        cst_b[:, 2 + hh, :] = mprev
        cst_b[:, 6 + hh, :] = mnext
    cst_b[:, 10, :] = (p[:, None] <= p[None, :])
    cst_b[:, 11, :] = (p[:, None] >= p[None, :])
    cst_b[:, 12, :] = 1.0
    cst4 = np.zeros((4, 4, 128), f32)
    for h in range(4):
        cst4[h, h, :] = 1.0
    cst4b = np.zeros((4, 5, 128), f32)
    cst4b[:, 0:4, :] = cst4
    for h in range(4):
        cst4b[h, 4, h * 32] = 1.0
    NS = cfg.NS
    tok = np.arange(NS)
    row = (tok // 64).astype(np.float64)
    col = (tok % 64).astype(np.float64)
    inv = 10000.0 ** (-np.arange(16, dtype=np.float32) / 16).astype(np.float32)
    d = p % 64
    axis = d // 32
    fr = d % 16
    pos = np.where(axis[:, None] == 0, row[None, :], col[None, :]).astype(np.float32)
    angr = (pos * inv[fr][:, None]).astype(np.float32)
    ropeT = np.stack([np.cos(angr), np.sin(angr)], axis=1).astype(f32)

    def seq_tab(S):
        s = np.arange(S, dtype=np.int64)
        prod = (s[:, None] * s[None, :]) % S
        a = 2.0 * np.pi * prod / S
        sc = 1.0 / math.sqrt(S * 128.0)
        cs = np.stack([np.cos(a) * sc, np.sin(a) * sc], axis=0).astype(f32)
        t = cs.reshape(2, S // 128, 128, S // 256, 256).transpose(3, 2, 1, 0, 4)
        return _bf(t)

    out = dict(cst_f=cst_f, cst_b=_bf(cst_b), cst4=cst4, cst4b=_bf(cst4b), ropeT=np.ascontiguousarray(ropeT),
               tabS=seq_tab(cfg.NS), tabP=seq_tab(cfg.NP))
    _CONST_CACHE[key] = out
    return out


def make_in_maps(cfg, inp, n_cores=8):
    f32 = np.float32
    g = lambda n: np.asarray(inp[n], f32)
    cs = make_consts(cfg)
    b_adaT = np.ascontiguousarray(g("b_ada").reshape(DEPTH, 72, 128).transpose(2, 0, 1))
    g_normT = np.ascontiguousarray(g("g_norm").reshape(DEPTH, 3, NCH, 128).transpose(3, 0, 1, 2))
    qperm = np.concatenate([np.r_[c * 64:(c + 1) * 64, (c + 4) * 64:(c + 5) * 64] for c in range(4)])
    wie = g("w_in_even")
    w_in_e = np.ascontiguousarray(np.concatenate([wie[:, :, qperm], wie[:, :, 512:]], axis=2))
    woe = g("w_out_even")
    w_out_e = np.ascontiguousarray(np.concatenate([woe[:, qperm, :], woe[:, 512:, :]], axis=1))
    p64 = np.arange(128) % 64
    par_e = np.ascontiguousarray(np.stack([g("qn_a")[:, p64], g("kn_a")[:, p64]], axis=-1).transpose(1, 0, 2))
    par_o = np.ascontiguousarray(np.stack([g("qn_c")[:, p64], g("kn_c")[:, p64], g("subln_c"), g("outnorm_d")],
                                          axis=-1).transpose(1, 0, 2))
    bg_o = np.ascontiguousarray(g("b_gate_odd").reshape(2, 4, 4).transpose(2, 0, 1))
    lam_o = np.ascontiguousarray(g("lam_c").reshape(2, 256))
    shared = dict(cs)
    shared.update(w_ada=g("w_ada"), b_adaT=b_adaT, g_normT=g_normT, w_ffn_in=g("w_ffn_in"), w_ffn_out=g("w_ffn_out"),
                  w_in_e=w_in_e, w_out_e=w_out_e, par_e=par_e, sink_e=g("sink_a"), w_in_o=g("w_in_odd"),
                  w_out_o=g("w_out_odd"), par_o=par_o, bg_o=bg_o, lam_o=lam_o)
    maps = []
    for core in range(n_cores):
        b = core // 2
        toks = np.concatenate([g("x_sample")[b], g("x_prompt")[2 * core], g("x_prompt")[2 * core + 1]], axis=0)
        cT = np.stack([g("c_ctx").reshape(NCH, 128).T, g("c")[b].reshape(NCH, 128).T], axis=-1)
        cka = g("cache_k_a")[b]
        kctx_e = np.ascontiguousarray(cka.transpose(0, 2, 3, 1).reshape(2, 128, PAST))
        cva = g("cache_v_a")[b]
        vctx_e = np.zeros((2, 128, 2, 2, 128), f32)
        cv = cva.reshape(2, 2, 128, 2, 64)
        vctx_e[:, :, :, 0, 0:64] = cv[:, :, :, 0, :].transpose(0, 2, 1, 3)
        vctx_e[:, :, :, 0, 64] = 1.0
        vctx_e[:, :, :, 1, 64:128] = cv[:, :, :, 1, :].transpose(0, 2, 1, 3)
        vctx_e[:, :, :, 1, 0] = 1.0
        ckc = g("cache_k_c")[b]
        kctx_o = np.ascontiguousarray(ckc.transpose(0, 3, 4, 2, 1).reshape(2, 128, 4, PAST))
        cvc = g("cache_v_c")[b]
        vctx_o = np.ascontiguousarray(cvc.reshape(2, 2, 128, 512).transpose(0, 2, 1, 3))
        sC = g("state_C_d")[b]
        c0_o = np.ascontiguousarray(sC.transpose(3, 0, 1, 2, 4))
        sn = g("state_n_d")[b]
        n0_o = np.ascontiguousarray(sn.reshape(2, 2, 4, 128, 1))
        sm = g("state_m_d")[b]
        m0_o = np.ascontiguousarray(sm.transpose(2, 0, 1))
        m = dict(shared)
        m.update(xT=_fm(toks), cT=np.ascontiguousarray(cT), kctx_e=kctx_e, vctx_e=vctx_e, kctx_o=kctx_o,
                 vctx_o=vctx_o, c0_o=c0_o, n0_o=n0_o, m0_o=m0_o)
        maps.append(m)
    return maps


def assemble(cfg, outs, B, n_cores=8):
    NS, NP = cfg.NS, cfg.NP
    f32 = np.float32
    yp = np.zeros((B, NP, D), f32)
    ys = np.zeros((max(1, n_cores // 2), NS, D), f32)
    ka = np.zeros((B, 2, NP, 2, 64), f32)
    va = np.zeros((B, 2, NP, 2, 64), f32)
    kc = np.zeros((B, 2, NP, 4, 2, 64), f32)
    vc = np.zeros((B, 2, NP, 4, 128), f32)
    Cd = np.zeros((B, 2, 2, 4, 128, 128), f32)
    nd = np.zeros((B, 2, 2, 4, 128), f32)
    md = np.zeros((B, 2, 2, 4), f32)
    for core in range(n_cores):
        o = outs[core]
        y = _tm(np.asarray(o["yT"], f32))
        if core % 2 == 0:
            ys[core // 2] = y[:NS]
        for s in range(2):
            bi = 2 * core + s
            yp[bi] = y[NS + s * NP:NS + (s + 1) * NP]
            tsl = slice(s * NP, (s + 1) * NP)
            kk = np.asarray(o["knew_a"], f32)[:, :, tsl]
            ka[bi] = kk.reshape(2, 2, 64, NP).transpose(0, 3, 1, 2)
            vv = np.asarray(o["vnew_a"], f32)
            vv = vv.transpose(0, 2, 1, 3).reshape(2, 2 * NP, 2, 64)[:, tsl]
            va[bi] = vv
            kk = np.asarray(o["knew_c"], f32)[:, :, :, tsl]
            kc[bi] = kk.reshape(2, 2, 64, 4, NP).transpose(0, 4, 3, 1, 2)
            vv = np.asarray(o["vnew_c"], f32).transpose(0, 2, 1, 3).reshape(2, 2 * NP, 4, 128)[:, tsl]
            vc[bi] = vv
            Cd[bi] = np.asarray(o["Cnew"], f32)[:, s]
            nd[bi] = np.asarray(o["nnew"], f32)[:, s, :, :, :, 0]
            md[bi] = np.asarray(o["mnew"], f32)[:, :, s, :].transpose(1, 2, 0)
    return (yp, ys, ka, va, kc, vc, Cd, nd, md)


_NC_CACHE = {}


def kernel(**inputs):
    inp = {k_: np.asarray(v) for k_, v in inputs.items()}
    cfg = Cfg()
    if "nc" not in _NC_CACHE:
        _NC_CACHE["nc"] = build(cfg)
    nc = _NC_CACHE["nc"]
    maps = make_in_maps(cfg, inp)
    res = run_bass_kernel_spmd(nc, maps, core_ids=list(range(8)))
    return assemble(cfg, res.results, inp["x_prompt"].shape[0])
```
